# Optimizing a Trainium2 kernel written in Bass

```python
import math
import jax, jax.numpy as jnp
from jax import lax
import numpy as np

D_MODEL = 2048
BATCH = 2
SEQ = 16384
DEPTH = 1

ATTN_WIDTH = D_MODEL // 2
ATTN_HEAD_DIM = 128
ATTN_HEADS = ATTN_WIDTH // ATTN_HEAD_DIM
DILATED_BRANCHES = ((128, 1), (512, 4), (2048, 16))
ROPE_THETA = 500000.0
ROPE_DIM = ATTN_HEAD_DIM // 4
HG_WIDTH = D_MODEL - ATTN_WIDTH
HG_EXPAND = 128
HG_HEADS = HG_WIDTH // HG_EXPAND
HG_CHUNK = 64
MIX_WIDTH = ATTN_WIDTH + HG_WIDTH
IN_COLS = 3 * ATTN_WIDTH + 4 * HG_WIDTH
D_FF = 4 * D_MODEL
N_MOD = 6
EPS = 1e-6

kernel_name = 'hybrid_dilated_attn_hgrn2_block'


def rms_norm(x, w):
    xf = x.astype(jnp.float32)
    y = xf * lax.rsqrt(jnp.mean(xf * xf, axis=-1, keepdims=True) + EPS)
    return y * w.astype(jnp.float32)


def rope_partial(x, pos):
    half = ROPE_DIM // 2
    inv_freq = ROPE_THETA ** (-(jnp.arange(half, dtype=jnp.float32) * 2.0) / ROPE_DIM)
    ang = pos.astype(jnp.float32)[..., None] * inv_freq
    cos = jnp.cos(ang)[:, :, None, :]
    sin = jnp.sin(ang)[:, :, None, :]
    x1 = x[..., :half]
    x2 = x[..., half:ROPE_DIM]
    return jnp.concatenate([x1 * cos - x2 * sin, x2 * cos + x1 * sin, x[..., ROPE_DIM:]], axis=-1)


def dilated_branch(q, k, v, window, dilation):
    blk = window // dilation
    B, S, H, Dh = q.shape
    span = blk * dilation
    L = -(-S // span) * span
    M = L // dilation
    nb = M // blk

    def to_blocks(t):
        t = jnp.pad(t, ((0, 0), (0, L - S), (0, 0), (0, 0)))
        t = t.reshape(B, M, dilation, H, Dh).transpose(0, 2, 3, 1, 4)
        return t.reshape(B, dilation, H, nb, blk, Dh)

    def with_prev(t):
        prev = jnp.pad(t[:, :, :, :-1], ((0, 0), (0, 0), (0, 0), (1, 0), (0, 0), (0, 0)))
        return jnp.concatenate([prev, t], axis=4)

    qb = to_blocks(q)
    kk = with_prev(to_blocks(k))
    vv = with_prev(to_blocks(v))
    s = jnp.einsum('brhnqd,brhnkd->brhnqk', qb, kk) / math.sqrt(Dh)
    i = jnp.arange(blk)[:, None]
    j = jnp.arange(2 * blk)[None, :]
    dist = blk + i - j
    band = (dist >= 0) & (dist <= blk)
    not_before_start = (jnp.arange(nb)[:, None, None] > 0) | (j[None] >= blk)
    mask = band[None] & not_before_start
    s = jnp.where(mask, s, -jnp.inf)
    m = jnp.max(s, axis=-1, keepdims=True)
    p = jnp.exp(s - m)
    l = jnp.sum(p, axis=-1, keepdims=True)
    o = jnp.einsum('brhnqk,brhnkd->brhnqd', p, vv) / l
    lse = (m + jnp.log(l))[..., 0]
    o = o.reshape(B, dilation, H, M, Dh).transpose(0, 3, 1, 2, 4).reshape(B, L, H, Dh)[:, :S]
    lse = lse.reshape(B, dilation, H, M).transpose(0, 3, 1, 2).reshape(B, L, H)[:, :S]
    return o, lse


def dilated_attention(q, k, v):
    outs, lses = [], []
    for window, dilation in DILATED_BRANCHES:
        o, lse = dilated_branch(q, k, v, window, dilation)
        outs.append(o)
        lses.append(lse)
    w = jax.nn.softmax(jnp.stack(lses, axis=0), axis=0)
    return jnp.sum(w[..., None] * jnp.stack(outs, axis=0), axis=0)


def hgrn2(q, f_logit, inp, lb):
    B, S, _ = q.shape
    C, Hh, dk = HG_CHUNK, HG_HEADS, HG_EXPAND
    f = lb + (1.0 - lb) * jax.nn.sigmoid(f_logit)
    logf = jnp.log(f)
    key = 1.0 - f
    qf = jax.nn.silu(q) * (dk ** -0.5)

    def chunks(t):
        return t.reshape(B, S // C, C, Hh, dk).transpose(1, 0, 3, 2, 4)

    causal = jnp.tril(jnp.ones((C, C), dtype=bool))[..., None]

    def step(state, xs):
        qc, kc, vc, lfc = xs
        b = jnp.cumsum(lfc, axis=2)
        o_inter = jnp.einsum('bhck,bhkv->bhcv', qc * jnp.exp(b), state)
        diff = b[:, :, :, None, :] - b[:, :, None, :, :]
        decay = jnp.exp(jnp.where(causal, diff, -jnp.inf))
        a = jnp.einsum('bhtk,bhsk,bhtsk->bhts', qc, kc, decay)
        o = o_inter + jnp.einsum('bhts,bhsv->bhtv', a, vc)
        b_last = b[:, :, -1:, :]
        new_state = jnp.exp(b_last[:, :, 0, :])[..., None] * state + jnp.einsum(
            'bhsk,bhsv->bhkv', kc * jnp.exp(b_last - b), vc)
        return new_state, o

    state0 = jnp.zeros((B, Hh, dk, dk), dtype=jnp.float32)
    _, o = lax.scan(step, state0, (chunks(qf), chunks(key), chunks(inp), chunks(logf)))
    return o.transpose(1, 0, 3, 2, 4).reshape(B, S, Hh, dk)


def setup_inputs(seed: int = 0) -> dict:
    key = jax.random.key(seed)
    ks = jax.random.split(key, 20)
    f32 = jnp.float32
    nrm = lambda k, shape, scale: jax.random.normal(k, shape, f32) * scale
    gain = lambda k, shape: 1.0 + 0.05 * jax.random.normal(k, shape, f32)
    x = jax.random.normal(ks[0], (BATCH, SEQ, D_MODEL), f32)
    c = jax.random.normal(ks[1], (BATCH, D_MODEL), f32)
    positions = (jnp.arange(SEQ, dtype=jnp.int32)[None, :]
                 + jax.random.randint(ks[2], (BATCH, 1), 0, 4096, dtype=jnp.int32))
    return {
        'x': x,
        'c': c,
        'positions': positions,
        'norm1_w': gain(ks[3], (DEPTH, D_MODEL)),
        'w_ada': nrm(ks[4], (DEPTH, D_MODEL, N_MOD * D_MODEL), 0.5 * D_MODEL ** -0.5),
        'b_ada': nrm(ks[5], (DEPTH, N_MOD * D_MODEL), 0.02),
        'w_in': nrm(ks[6], (DEPTH, D_MODEL, IN_COLS), D_MODEL ** -0.5),
        'q_norm_w': gain(ks[7], (DEPTH, ATTN_HEAD_DIM)),
        'k_norm_w': gain(ks[8], (DEPTH, ATTN_HEAD_DIM)),
        'attn_out_norm_w': gain(ks[9], (DEPTH, ATTN_WIDTH)),
        'hg_lb_logits': nrm(ks[10], (DEPTH + 1, HG_WIDTH), 0.5),
        'hg_norm_w': gain(ks[11], (DEPTH, HG_EXPAND)),
        'w_out': nrm(ks[12], (DEPTH, MIX_WIDTH, D_MODEL), MIX_WIDTH ** -0.5),
        'norm2_w': gain(ks[13], (DEPTH, D_MODEL)),
        'w_ff1': nrm(ks[14], (DEPTH, D_MODEL, D_FF), D_MODEL ** -0.5),
        'w_ff2': nrm(ks[15], (DEPTH, D_FF, D_MODEL), D_FF ** -0.5),
    }


def reference(x, c, positions, norm1_w, w_ada, b_ada, w_in, q_norm_w, k_norm_w, attn_out_norm_w,
              hg_lb_logits, hg_norm_w, w_out, norm2_w, w_ff1, w_ff2):
    dt = x.dtype
    B, S, _ = x.shape
    lb_all = jnp.cumsum(jax.nn.softmax(hg_lb_logits.astype(jnp.float32), axis=0), axis=0)
    split_at = [ATTN_WIDTH, 2 * ATTN_WIDTH, 3 * ATTN_WIDTH, 3 * ATTN_WIDTH + HG_WIDTH,
                3 * ATTN_WIDTH + 2 * HG_WIDTH, 3 * ATTN_WIDTH + 3 * HG_WIDTH]
    for l in range(DEPTH):
        mod = (jax.nn.silu(c) @ w_ada[l] + b_ada[l]).astype(jnp.float32)
        shift1, scale1, gate1, shift2, scale2, gate2 = [m[:, None, :] for m in jnp.split(mod, N_MOD, axis=-1)]

        h = rms_norm(x, norm1_w[l]) * (1.0 + scale1) + shift1
        proj = (h.astype(dt) @ w_in[l]).astype(jnp.float32)
        aq, ak, av, gq, gf, gi, gg = jnp.split(proj, split_at, axis=-1)

        hs = (B, S, ATTN_HEADS, ATTN_HEAD_DIM)
        aq = rope_partial(rms_norm(aq.reshape(hs), q_norm_w[l]), positions)
        ak = rope_partial(rms_norm(ak.reshape(hs), k_norm_w[l]), positions)
        attn = dilated_attention(aq, ak, av.reshape(hs)).reshape(B, S, ATTN_WIDTH)
        attn = rms_norm(attn, attn_out_norm_w[l])

        ho = hgrn2(gq, gf, gi, lb_all[l])
        hg = rms_norm(ho, hg_norm_w[l]) * jax.nn.silu(gg.reshape(B, S, HG_HEADS, HG_EXPAND))
        hg = hg.reshape(B, S, HG_WIDTH)

        mix = jnp.concatenate([attn, hg], axis=-1).astype(dt) @ w_out[l]
        x = x + (gate1 * mix.astype(jnp.float32)).astype(dt)

        h2 = rms_norm(x, norm2_w[l]) * (1.0 + scale2) + shift2
        ff = jnp.square(jax.nn.relu(h2.astype(dt) @ w_ff1[l])) @ w_ff2[l]
        x = x + (gate2 * ff.astype(jnp.float32)).astype(dt)
    return x
```

```python
import math
from contextlib import ExitStack
import numpy as np
import ml_dtypes
import concourse.bass as bass
import concourse.mybir as mybir
from concourse.bass_utils import run_bass_kernel_spmd

F32 = mybir.dt.float32
BF16 = mybir.dt.bfloat16
I32 = mybir.dt.int32
AF = mybir.ActivationFunctionType
ALU = mybir.AluOpType

EPS = 1e-6
MSHIFT = 8.0
ROPE_THETA = 500000.0
TWO_PI = 2.0 * math.pi
CW1 = 6.28125
CW2 = TWO_PI - CW1


class Op:
    __slots__ = ("eng", "fn", "deps", "sig", "sem", "val", "idx")

    def __init__(self, eng, fn):
        self.eng = eng
        self.fn = fn
        self.deps = []
        self.sig = False
        self.sem = None
        self.val = 0
        self.idx = 0


class Sched:
    G = None
    ENGS = ("pe", "act", "dve", "pool", "sp")
    NS_DMA = 24
    ROT = 30000

    def __init__(self, nc):
        self.nc = nc
        self.ops = {e: [] for e in self.ENGS}
        self.last_writer = {}
        self.readers = {}

    def add(self, eng, fn, reads=(), writes=()):
        op = Op(eng, fn)
        ds = {}
        for r in reads:
            w = self.last_writer.get(r)
            if w is not None:
                ds[id(w)] = w
        for w in writes:
            lw = self.last_writer.get(w)
            if lw is not None:
                ds[id(lw)] = lw
            for rd in self.readers.get(w, ()):
                ds[id(rd)] = rd
            self.readers[w] = []
            self.last_writer[w] = op
        for r in reads:
            self.readers.setdefault(r, []).append(op)
        for d in ds.values():
            if d is op:
                continue
            if d.eng == "pe" and eng == "pe":
                continue
            op.deps.append(d)
            d.sig = True
        if eng == "sp":
            op.sig = True
        self.ops[eng].append(op)
        return op

    def emit(self):
        nc = self.nc
        G = Sched.G
        with ExitStack() as st:
            for e in ("pe", "act", "dve", "pool"):
                sl = G["sems"][e]
                c = G["cnt"][e]
                for o in self.ops[e]:
                    if o.sig:
                        o.sem = sl[c // self.ROT]
                        o.val = c % self.ROT + 1
                        c += 1
                G["cnt"][e] = c
            ring = G["ring"]
            NS = self.NS_DMA
            base = G["dma"]
            for i, o in enumerate(self.ops["sp"]):
                gi = base + i
                o.sem = ring[gi % NS]
                o.val = 16 * (gi // NS + 1)
                o.idx = gi
            G["dma"] = base + len(self.ops["sp"])
            ops = self.ops

            def stream(ename, e):
                waited = {}

                def wait(sem, val):
                    k = id(sem)
                    if waited.get(k, 0) >= val:
                        return
                    waited[k] = val
                    e.wait_ge(sem, val)

                for o in ops[ename]:
                    for d in o.deps:
                        wait(d.sem, d.val)
                    if ename == "sp" and o.idx >= NS:
                        wait(o.sem, o.val - 16)
                    ins = o.fn(e)
                    if o.sig:
                        ins.then_inc(o.sem, 16 if ename == "sp" else 1)
                if ename == "sp":
                    n = len(ops["sp"])
                    for i in range(max(0, n - NS), n):
                        o = ops["sp"][i]
                        wait(o.sem, o.val)

            blk = st.enter_context(nc.Block())

            @blk.tensor
            def _(e):
                stream("pe", e)

            @blk.scalar
            def _(e):
                stream("act", e)

            @blk.vector
            def _(e):
                stream("dve", e)

            @blk.gpsimd
            def _(e):
                stream("pool", e)

            @blk.sync
            def _(e):
                stream("sp", e)


class Ctx:
    def __init__(self, nc, st):
        self.nc = nc
        self.st = st
        self.S = Sched(nc)
        self.n = 0

    CNT = [0]

    def sb(self, shape, dt, name=None):
        Ctx.CNT[0] += 1
        return self.st.enter_context(self.nc.sbuf_tensor(f"t{Ctx.CNT[0]}", list(shape), dt))

    def ps(self, shape=(128, 512), dt=F32, name=None):
        Ctx.CNT[0] += 1
        return self.st.enter_context(self.nc.psum_tensor(f"p{Ctx.CNT[0]}", list(shape), dt))

    def dma(self, out, in_, r=(), w=()):
        return self.S.add("sp", lambda e: e.dma_start(out=out, in_=in_), r, w)

    def mm(self, out, lhsT, rhs, start=True, stop=True, r=(), w=()):
        return self.S.add("pe", lambda e: e.matmul(out, lhsT=lhsT, rhs=rhs, start=start, stop=stop,
                                                    skip_group_check=True), r, w)

    def tr(self, out, in_, ident, r=(), w=()):
        return self.S.add("pe", lambda e: e.transpose(out, in_, ident), r, w)

    def act(self, out, in_, func, scale=1.0, bias=None, accum=None, r=(), w=()):
        def f(e):
            kw = {}
            if bias is not None:
                kw["bias"] = bias
            if accum is not None:
                kw["accum_out"] = accum
            return e.activation(out=out, in_=in_, func=func, scale=scale, **kw)
        return self.S.add("act", f, r, w)

    def tt(self, eng, out, in0, in1, op, r=(), w=()):
        return self.S.add(eng, lambda e: e.tensor_tensor(out=out, in0=in0, in1=in1, op=op), r, w)

    def ts(self, eng, out, in0, s1, s2, op0, op1=None, r=(), w=()):
        if op1 is None:
            return self.S.add(eng, lambda e: e.tensor_scalar(out=out, in0=in0, scalar1=s1, scalar2=None, op0=op0), r, w)
        return self.S.add(eng, lambda e: e.tensor_scalar(out=out, in0=in0, scalar1=s1, scalar2=s2, op0=op0, op1=op1), r, w)

    def stt(self, out, in0, scalar, in1, op0, op1, r=(), w=()):
        return self.S.add("dve", lambda e: e.scalar_tensor_tensor(out=out, in0=in0, scalar=scalar, in1=in1,
                                                                   op0=op0, op1=op1), r, w)

    def cp(self, eng, out, in_, r=(), w=()):
        if eng == "act":
            return self.act(out, in_, AF.Copy, r=r, w=w)
        return self.S.add(eng, lambda e: e.tensor_copy(out=out, in_=in_), r, w)

    def recip(self, out, in_, r=(), w=()):
        return self.S.add("dve", lambda e: e.reciprocal(out=out, in_=in_), r, w)


def build_program(D, NH, DFF, TM, TH, only=None):
    NKC = D // 128
    W = 128 * NH
    assert 2 * W == D
    TT = TM + TH
    NFF = DFF // 128
    NT = TT // 512
    NTH = TH // 512
    NTM = TM // 512
    NSPAN = TM // 2048
    assert TH == 2048 and TM % 2048 == 0
    ISQ = 1.0 / math.sqrt(128.0)

    nc = bass.Bass("TRN2", target_bir_lowering=False)

    def din(name, shape, dt=F32):
        return nc.dram_tensor(name, list(shape), dt, kind="ExternalInput").ap()

    x = din("x", [TT, D])
    pos = din("pos", [1, TT], I32)
    cT = din("cT", [128, NKC])
    valid_d = din("valid", [128, 1])
    n1w_d = din("n1w", [128, NKC])
    n2w_d = din("n2w", [128, NKC])
    bada_d = din("bada", [128, 6 * NKC])
    w_ada = din("w_ada", [D, 6 * D])
    w_in = din("w_in", [D, 7 * W])
    qnw_d = din("qnw", [128, 1])
    knw_d = din("knw", [128, 1])
    aonw_d = din("aonw", [128, NH])
    lbl_d = din("lbl", [128, 2 * NH])
    hgnw_d = din("hgnw", [128, 1])
    w_out = din("w_out", [D, D])
    w_ff1 = din("w_ff1", [D, DFF])
    w_ff2 = din("w_ff2", [DFF, D])
    identb_d = din("identb", [128, 128], BF16)
    identf_d = din("identf", [128, 128])
    onesb_d = din("onesb", [128, 128], BF16)
    rotT_d = din("rotT", [128, 128], BF16)
    maskA_d = din("maskA", [128, 256], BF16)
    maskH_d = din("maskH", [128, 64], BF16)
    resetm_d = din("resetm", [128, 512])
    invf_d = din("invf", [128, 1])
    cst_d = din("cst", [128, 4])
    out = nc.dram_tensor("out", [TM, D], F32, kind="ExternalOutput").ap()

    hT_s = nc.dram_tensor("hT_s", [128, NKC, TT], BF16).ap()
    mixT_s = nc.dram_tensor("mixT_s", [128, NKC, TM], BF16).ap()
    cs_s = nc.dram_tensor("cs_s", [2, 128, TT], F32).ap()
    wo_s = nc.dram_tensor("wo_s", [D, D], BF16).ap()
    wf1_s = nc.dram_tensor("wf1_s", [D, DFF], BF16).ap()
    wf2_s = nc.dram_tensor("wf2_s", [DFF, D], BF16).ap()

    with ExitStack() as gst:
        Sched.G = {"sems": {e: [gst.enter_context(nc.semaphore(f"s_{e}{i}")) for i in range(5)]
                            for e in ("pe", "act", "dve", "pool")},
                   "cnt": {e: 0 for e in ("pe", "act", "dve", "pool")},
                   "ring": [gst.enter_context(nc.semaphore(f"s_dma{i}")) for i in range(Sched.NS_DMA)],
                   "dma": 0}

        def gsb(shape, dt, name):
            return gst.enter_context(nc.sbuf_tensor(name + "_sb", list(shape), dt))

        modv = gsb([128, 6 * NKC], F32, "modv")
        g1 = gsb([128, NKC], F32, "g1")
        g2 = gsb([128, NKC], F32, "g2")
        lb = gsb([128, NH], F32, "lb")
        omlb = gsb([128, NH], F32, "omlb")
        omlbv = gsb([128, NH], F32, "omlbv")
        qnw = gsb([128, 1], F32, "qnw")
        knw = gsb([128, 1], F32, "knw")
        aonw = gsb([128, NH], F32, "aonw")
        hgnw = gsb([128, 1], F32, "hgnw")
        valid = gsb([128, 1], F32, "validt")
        invf = gsb([128, 1], F32, "invft")
        cst = gsb([128, 4], F32, "cstt")
        identb = gsb([128, 128], BF16, "identbt")
        identf = gsb([128, 128], F32, "identft")
        onesb = gsb([128, 128], BF16, "onesbt")
        rotT = gsb([128, 128], BF16, "rotTt")
        maskA = gsb([128, 256], BF16, "maskAt")
        maskAh = gsb([128, 256], BF16, "maskAht")
        maskH = gsb([128, 64], BF16, "maskHt")
        resetm = gsb([128, 512], F32, "resetmt")
        sh1 = modv[:, 0 * NKC:1 * NKC]
        sc1 = modv[:, 1 * NKC:2 * NKC]
        gate1 = modv[:, 2 * NKC:3 * NKC]
        sh2 = modv[:, 3 * NKC:4 * NKC]
        sc2 = modv[:, 4 * NKC:5 * NKC]
        gate2 = modv[:, 5 * NKC:6 * NKC]
        c_eps = cst[:, 0:1]
        c_eps128 = cst[:, 1:2]
        c_mshift = cst[:, 2:3]

        pcount = [0]

        def phase(fn):
            pcount[0] += 1
            if only is not None and pcount[0] not in only:
                return
            with ExitStack() as st:
                c = Ctx(nc, st)
                fn(c)
                c.S.emit()
            pass

        def ph0(c):
            for t, d in ((valid, valid_d), (invf, invf_d), (cst, cst_d), (identb, identb_d), (identf, identf_d),
                         (onesb, onesb_d), (rotT, rotT_d), (maskA, maskA_d), (maskH, maskH_d), (resetm, resetm_d),
                         (qnw, qnw_d), (knw, knw_d), (aonw, aonw_d), (hgnw, hgnw_d)):
                c.dma(t[:], d, w=[("g", id(t))])
            n1w = c.sb([128, NKC], F32)
            n2w = c.sb([128, NKC], F32)
            bada = c.sb([128, 6 * NKC], F32)
            lbl = c.sb([128, 2 * NH], F32)
            cTt = c.sb([128, NKC], F32)
            c.dma(n1w[:], n1w_d, w=["n1w"])
            c.dma(n2w[:], n2w_d, w=["n2w"])
            c.dma(bada[:], bada_d, w=["bada"])
            c.dma(lbl[:], lbl_d, w=["lbl"])
            c.dma(cTt[:], cT, w=["cT"])
            c.cp("dve", maskAh[:, 128:256], maskA[:, 128:256], r=[("g", id(maskA))], w=["mAh1"])
            c.ts("dve", maskAh[:, 0:128], maskA[:, 0:128], valid[:, 0:1], None, ALU.mult,
                 r=[("g", id(maskA)), ("g", id(valid))], w=["mAh0"])
            dl = c.sb([128, NH], F32)
            c.tt("dve", dl[:], lbl[:, 0:NH], lbl[:, NH:2 * NH], ALU.subtract, r=["lbl"], w=["dl"])
            c.act(lb[:], dl[:], AF.Sigmoid, r=["dl"], w=["lb"])
            c.ts("dve", omlb[:], lb[:], -1.0, 1.0, ALU.mult, ALU.add, r=["lb"], w=["omlb"])
            c.ts("dve", omlbv[:], omlb[:], valid[:, 0:1], None, ALU.mult, r=["omlb", ("g", id(valid))], w=["omlbv"])
            scs = c.sb([128, NKC], F32)
            c.act(scs[:], cTt[:], AF.Silu, r=["cT"], w=["scs"])
            zer = c.sb([128, 128], F32)
            c.S.add("pool", lambda e: e.memset(zer[:], 0.0), (), ["zer"])
            pm = c.ps([128, 512], F32)
            NJ = 6 * NKC
            c.mm(pm[:, 0:NJ], zer[:, 0:128], zer[:, 0:NJ], True, False, r=["zer"], w=["pm"])
            CH = min(6 * D, 6144)
            NCH = (6 * D) // CH
            wa = [c.sb([128, CH], F32) for _ in range(2)]
            it = 0
            for kc in range(NKC):
                for ch in range(NCH):
                    b = it % 2
                    it += 1
                    c.dma(wa[b][:], w_ada[kc * 128:(kc + 1) * 128, ch * CH:(ch + 1) * CH], w=[("wa", b)])
                    for jj in range(CH // 128):
                        j = ch * (CH // 128) + jj
                        last = (kc == NKC - 1)
                        c.mm(pm[:, j:j + 1], wa[b][:, jj * 128:(jj + 1) * 128], scs[:, kc:kc + 1], False, last,
                             r=[("wa", b), "scs"], w=["pm"])
            c.tt("dve", modv[:], pm[:, 0:NJ], bada[:], ALU.add, r=["pm", "bada"], w=["modv"])
            c.stt(g1[:], sc1, 1.0, n1w[:], ALU.add, ALU.mult, r=["modv", "n1w"], w=["g1"])
            c.stt(g2[:], sc2, 1.0, n2w[:], ALU.add, ALU.mult, r=["modv", "n2w"], w=["g2"])

        phase(ph0)

        def php(c):
            CW = 4096
            f = [c.sb([128, CW], F32) for _ in range(3)]
            g = [c.sb([128, CW], BF16) for _ in range(3)]
            engs = ("act", "dve", "pool")
            it = 0
            for src, dst, R, Cn in ((w_out, wo_s, D, D), (w_ff1, wf1_s, D, DFF), (w_ff2, wf2_s, DFF, D)):
                for r0 in range(0, R, 128):
                    for c0 in range(0, Cn, CW):
                        cw = min(CW, Cn - c0)
                        b = it % 3
                        c.dma(f[b][:, 0:cw], src[r0:r0 + 128, c0:c0 + cw], w=[("f", b)])
                        c.cp(engs[b], g[b][:, 0:cw], f[b][:, 0:cw], r=[("f", b)], w=[("g", b)])
                        c.dma(dst[r0:r0 + 128, c0:c0 + cw], g[b][:, 0:cw], r=[("g", b)])
                        it += 1

        phase(php)

        def pha(c):
            xa = [c.sb([128, D], F32) for _ in range(2)]
            junk = c.sb([128, D], F32)
            xs = [c.sb([128, D], BF16) for _ in range(2)]
            sm = [c.sb([128, 4], F32) for _ in range(2)]
            hst = [c.sb([128, NKC, 512], BF16) for _ in range(2)]
            tp = [c.ps([128, 1024], BF16) for _ in range(2 * ((NKC + 7) // 8))]
            NB = (NKC + 7) // 8
            for i in range(TT // 128):
                b = i % 2
                gI = i // 4
                slot = i % 4
                hb = gI % 2
                c.dma(xa[b][:], x[i * 128:(i + 1) * 128, :], w=[("xa", b)])
                c.act(junk[:], xa[b][:], AF.Square, r=[("xa", b)], w=["junk"])
                c.S.add("dve", lambda e, o_=sm[b][:, 0:1], i_=junk[:]: e.reduce_sum(out=o_, in_=i_, axis=mybir.AxisListType.X),
                        ["junk"], [("sm0", b)])
                c.act(sm[b][:, 1:2], sm[b][:, 0:1], AF.Sqrt, scale=1.0 / D, bias=c_eps, r=[("sm0", b)], w=[("sm1", b)])
                c.recip(sm[b][:, 2:3], sm[b][:, 1:2], r=[("sm1", b)], w=[("sm2", b)])
                c.ts("dve", xs[b][:], xa[b][:], sm[b][:, 2:3], None, ALU.mult, r=[("xa", b), ("sm2", b)], w=[("xs", b)])
                for kc in range(NKC):
                    bank = tp[b * NB + kc // 8]
                    tv = bank
                    k8 = kc % 8
                    c.tr(tv[:, k8 * 128:(k8 + 1) * 128], xs[b][:, kc * 128:(kc + 1) * 128], identb[:],
                         r=[("xs", b)], w=[("tp", b, kc // 8)])
                for kc in range(NKC):
                    bank = tp[b * NB + kc // 8]
                    tv = bank
                    k8 = kc % 8
                    dst = hst[hb][:, kc, slot * 128:(slot + 1) * 128]
                    if True:
                        c.ts("dve", dst, tv[:, k8 * 128:(k8 + 1) * 128], g1[:, kc:kc + 1], sh1[:, kc:kc + 1],
                             ALU.mult, ALU.add, r=[("tp", b, kc // 8)], w=[("hst", hb, slot, kc)])
                if slot == 3:
                    c.dma(hT_s[:, :, gI * 512:(gI + 1) * 512], hst[hb][:],
                          r=[("hst", hb, s_, k_) for s_ in range(4) for k_ in range(NKC)])

        phase(pha)

        def phr(c):
            PI_S = 3.1415925
            pis = [c.sb([128, 512], I32) for _ in range(2)]
            tmps = [[c.sb([128, 512], F32) for _ in range(5)] for _ in range(2)]
            kis = [c.sb([128, 512], I32) for _ in range(2)]
            for j in range(NT):
                b = j % 2
                pi_ = pis[b]
                ang, u, kf, rr, m = tmps[b]
                ki = kis[b]
                c.dma(pi_[:], pos[:, j * 512:(j + 1) * 512].partition_broadcast(128), w=[("pi", b)])
                c.cp("dve", ang[:], pi_[:], r=[("pi", b)], w=[("ang", b)])
                c.ts("dve", ang[:], ang[:], invf[:, 0:1], None, ALU.mult, r=[("ang", b)], w=[("ang", b)])
                for which, shift in ((0, math.pi / 2), (1, 0.0)):
                    c.ts("dve", u[:], ang[:], 1.0 / TWO_PI, shift / TWO_PI + 0.5, ALU.mult, ALU.add,
                         r=[("ang", b)], w=[("u", b)])
                    c.cp("dve", ki[:], u[:], r=[("u", b)], w=[("ki", b)])
                    c.cp("dve", kf[:], ki[:], r=[("ki", b)], w=[("kf", b)])
                    c.ts("dve", rr[:], ang[:], shift, None, ALU.add, r=[("ang", b)], w=[("rr", b)])
                    c.stt(rr[:], kf[:], -CW1, rr[:], ALU.mult, ALU.add, r=[("kf", b), ("rr", b)], w=[("rr", b)])
                    c.stt(rr[:], kf[:], -CW2, rr[:], ALU.mult, ALU.add, r=[("kf", b), ("rr", b)], w=[("rr", b)])
                    c.ts("dve", m[:], rr[:], math.pi, -TWO_PI, ALU.is_gt, ALU.mult, r=[("rr", b)], w=[("m", b)])
                    c.tt("dve", rr[:], rr[:], m[:], ALU.add, r=[("rr", b), ("m", b)], w=[("rr", b)])
                    c.ts("dve", m[:], rr[:], -math.pi, TWO_PI, ALU.is_lt, ALU.mult, r=[("rr", b)], w=[("m", b)])
                    c.tt("dve", rr[:], rr[:], m[:], ALU.add, r=[("rr", b), ("m", b)], w=[("rr", b)])
                    c.ts("dve", rr[:], rr[:], PI_S, -PI_S, ALU.min, ALU.max, r=[("rr", b)], w=[("rr", b)])
                    c.act(u[:], rr[:], AF.Sin, r=[("rr", b)], w=[("u", b)])
                    c.dma(cs_s[which, :, j * 512:(j + 1) * 512], u[:], r=[("u", b)])
        phase(phr)

        w_in_v = w_in.rearrange("(kc p) n -> p kc n", p=128)

        def load_head_w(c, Wb, W32, h, groups, tag):
            engs = ("act", "dve", "pool")
            for gi_, g in enumerate(groups):
                b = gi_ % 2
                rk = ("W32", id(W32[b]))
                c.dma(W32[b][:], w_in_v[:, :, g * W + h * 128:g * W + (h + 1) * 128], w=[rk])
                c.cp(engs[gi_ % 3], Wb[:, :, gi_, :], W32[b][:], r=[rk], w=[("Wb", gi_)])

        def proj(c, pg, Wb, gi_, hTt, hb, cols=slice(0, 512), w=()):
            for kc in range(NKC):
                c.mm(pg, Wb[:, kc, gi_, :], hTt[hb][:, kc, cols], kc == 0, kc == NKC - 1,
                     r=[("Wb", gi_), ("hTt", hb)], w=list(w))

        def phb1(c):
            Wb = c.sb([128, NKC, 3, 128], BF16)
            W32_ = c.sb([128, NKC, 128], F32)
            W32 = [W32_, W32_]
            hTt = [c.sb([128, NKC, 512], BF16) for _ in range(2)]
            Ct = [c.sb([128, 512], F32) for _ in range(2)]
            St = [c.sb([128, 512], F32) for _ in range(2)]
            KT1 = c.sb([128, TT], BF16)
            KT2 = c.sb([128, TT], BF16)
            KT3 = c.sb([128, TT], BF16)
            QT = c.sb([128, TM], BF16)
            Vb1 = c.sb([128, TT // 128, 128], BF16)
            Vb2 = c.sb([128, TT // 128, 128], BF16)
            Vb3 = c.sb([128, TT // 128, 128], BF16)
            VTs = [c.sb([128, 512], BF16) for _ in range(2)]
            VT2 = [c.sb([128, 512], BF16) for _ in range(2)]
            VT3 = c.sb([128, 2048], BF16)
            acc = c.sb([128, 2, 2048], F32)
            rl = c.sb([128, 2048], F32)
            mst = c.sb([128, 2048], BF16)
            raw = c.sb([128, 512], F32)
            sq = c.sb([128, 512], BF16)
            sd = c.sb([128, 512], F32)
            rs = c.sb([128, 512], F32)
            qn = c.sb([128, 512], F32)
            qnb = c.sb([128, 512], BF16)
            t1 = c.sb([128, 512], F32)
            t2 = c.sb([128, 512], F32)
            Pt = [c.sb([128, 256], BF16) for _ in range(2)]
            Pm = [c.sb([128, 256], BF16) for _ in range(2)]
            pg = [c.ps() for _ in range(2)]
            px = [c.ps() for _ in range(2)]
            pT = c.ps([128, 1024], BF16)
            psS1 = c.ps()
            psS = [psS1, psS1]
            psO = [c.ps() for _ in range(2)]
            cnt = {"pg": 0, "px": 0, "u": 0}

            def qk_path(pgt, pgk, wcol, dst, cb):
                c.cp("act", raw[:], pgt[:], r=[pgk], w=["raw"])
                c.act(sq[:], pgt[:], AF.Square, r=[pgk], w=["sq"])
                pxi = cnt["px"] % 2
                cnt["px"] += 1
                c.mm(px[pxi][:], onesb[:], sq[:], r=["sq"], w=[("px", id(px[pxi]))])
                c.act(sd[:], px[pxi][:], AF.Sqrt, scale=1.0 / 128.0, bias=c_eps, r=[("px", id(px[pxi]))], w=["sd"])
                c.recip(rs[:], sd[:], r=["sd"], w=["rs"])
                c.stt(qn[:], raw[:], wcol, rs[:], ALU.mult, ALU.mult, r=["raw", "rs"], w=["qn"])
                c.cp("act", qnb[:], qn[:], r=["qn"], w=["qnb"])
                pxi = cnt["px"] % 2
                cnt["px"] += 1
                c.mm(px[pxi][:], rotT[:], qnb[:], r=["qnb"], w=[("px", id(px[pxi]))])
                c.tt("dve", t1[:], qn[:], Ct[cb][:], ALU.mult, r=["qn", ("Ct", cb)], w=["t1"])
                c.tt("dve", t2[:], px[pxi][:], St[cb][:], ALU.mult, r=[("px", id(px[pxi])), ("St", cb)], w=["t2"])
                c.tt("dve", dst, t1[:], t2[:], ALU.add, r=["t1", "t2"], w=["KQ"])

            for h in range(NH):
                load_head_w(c, Wb, W32, h, (0, 1, 2), "a")
                for j in range(NT):
                    hb = j % 2
                    main = j >= NTH
                    c.dma(hTt[hb][:], hT_s[:, :, j * 512:(j + 1) * 512], w=[("hTt", hb)])
                    c.dma(Ct[hb][:], cs_s[0, :, j * 512:(j + 1) * 512], w=[("Ct", hb)])
                    c.dma(St[hb][:], cs_s[1, :, j * 512:(j + 1) * 512], w=[("St", hb)])
                    tk = slice(j * 512, (j + 1) * 512)
                    pi_ = cnt["pg"] % 2
                    cnt["pg"] += 1
                    proj(c, pg[pi_][:], Wb, 1, hTt, hb, w=[("pg", pi_)])
                    qk_path(pg[pi_], ("pg", pi_), knw[:, 0:1], KT1[:, tk], hb)
                    c.cp("pool", KT2[:, tk].rearrange("p (r m) -> p r m", r=4),
                         KT1[:, tk].rearrange("p (m r) -> p r m", r=4), r=["KQ"], w=["K2"])
                    sp3 = (j // 4) * 2048
                    jj = j % 4
                    c.cp("pool", KT3[:, sp3:sp3 + 2048].rearrange("p (r m) -> p r m", r=16)[:, :, jj * 32:(jj + 1) * 32],
                         KT1[:, tk].rearrange("p (m r) -> p r m", r=16), r=["KQ"], w=["K3"])
                    pi_ = cnt["pg"] % 2
                    cnt["pg"] += 1
                    proj(c, pg[pi_][:], Wb, 2, hTt, hb, w=[("pg", pi_)])
                    c.cp("act", VTs[hb][:], pg[pi_][:], r=[("pg", pi_)], w=[("VTs", hb)])
                    c.cp("pool", VT2[hb][:].rearrange("p (r m) -> p r m", r=4),
                         VTs[hb][:].rearrange("p (m r) -> p r m", r=4), r=[("VTs", hb)], w=[("VT2", hb)])
                    c.cp("pool", VT3[:].rearrange("p (r m) -> p r m", r=16)[:, :, jj * 32:(jj + 1) * 32],
                         VTs[hb][:].rearrange("p (m r) -> p r m", r=16), r=[("VTs", hb)], w=["VT3"])
                    for src, srck, dstV in ((VTs[hb], ("VTs", hb), Vb1), (VT2[hb], ("VT2", hb), Vb2)):
                        pv = pT
                        for q4 in range(4):
                            c.tr(pv[:, q4 * 128:(q4 + 1) * 128], src[:, q4 * 128:(q4 + 1) * 128], identb[:],
                                 r=[srck], w=["pT"])
                        c.cp("dve", dstV[:, j * 4:(j + 1) * 4, :], pv[:, 0:512].rearrange("p (a b) -> p a b", a=4),
                             r=["pT"], w=["Vb"])
                    if jj == 3:
                        for q16 in range(4):
                            pv = pT
                            for q4 in range(4):
                                r_ = q16 * 4 + q4
                                c.tr(pv[:, q4 * 128:(q4 + 1) * 128], VT3[:, r_ * 128:(r_ + 1) * 128], identb[:],
                                     r=["VT3"], w=["pT"])
                            t0 = (j // 4) * 16 + q16 * 4
                            c.cp("dve", Vb3[:, t0:t0 + 4, :], pv[:, 0:512].rearrange("p (a b) -> p a b", a=4),
                                 r=["pT"], w=["Vb"])
                    if main:
                        pi_ = cnt["pg"] % 2
                        cnt["pg"] += 1
                        proj(c, pg[pi_][:], Wb, 0, hTt, hb, w=[("pg", pi_)])
                        jm = j - NTH
                        qk_path(pg[pi_], ("pg", pi_), qnw[:, 0:1], QT[:, jm * 512:(jm + 1) * 512], hb)
                for n in range(NSPAN):
                    ng = n + 1
                    q0 = n * 2048
                    for br in (1, 2, 3):
                        for u in range(16):
                            if br == 1:
                                tcur = ng * 16 + u
                                kcur = KT1[:, tcur * 128:(tcur + 1) * 128]
                                kprev = KT1[:, (tcur - 1) * 128:tcur * 128]
                                vcur, vprev = Vb1[:, tcur, :], Vb1[:, tcur - 1, :]
                                qv = QT[:, q0 + u * 128:q0 + (u + 1) * 128]
                                av = acc[:, :, u * 128:(u + 1) * 128]
                                halo_prev = (n == 0 and u == 0)
                            elif br == 2:
                                s_, r_ = u // 4, u % 4
                                sg = ng * 4 + s_
                                tcur, tprev = sg * 4 + r_, (sg - 1) * 4 + r_
                                kcur = KT2[:, tcur * 128:(tcur + 1) * 128]
                                kprev = KT2[:, tprev * 128:(tprev + 1) * 128]
                                vcur, vprev = Vb2[:, tcur, :], Vb2[:, tprev, :]
                                b0 = q0 + s_ * 512 + r_
                                qv = QT[:, b0:b0 + 509:4]
                                a0 = s_ * 512 + r_
                                av = acc[:, :, a0:a0 + 509:4]
                                halo_prev = (n == 0 and s_ == 0)
                            else:
                                r_ = u
                                tcur, tprev = ng * 16 + r_, (ng - 1) * 16 + r_
                                kcur = KT3[:, tcur * 128:(tcur + 1) * 128]
                                kprev = KT3[:, tprev * 128:(tprev + 1) * 128]
                                vcur, vprev = Vb3[:, tcur, :], Vb3[:, tprev, :]
                                qv = QT[:, q0 + r_:q0 + r_ + 2033:16]
                                av = acc[:, :, r_:r_ + 2033:16]
                                halo_prev = (n == 0)
                            ub = cnt["u"] % 2
                            cnt["u"] += 1
                            c.mm(psS[ub][:, 0:128], kprev, qv, r=["K2", "K3", "KQ"], w=[("psS", id(psS[ub]))])
                            c.mm(psS[ub][:, 128:256], kcur, qv, r=["K2", "K3", "KQ"], w=[("psS", id(psS[ub]))])
                            c.act(Pt[ub][:], psS[ub][:, 0:256], AF.Exp, scale=ISQ, bias=c_mshift,
                                  r=[("psS", id(psS[ub]))], w=[("Pt", ub)])
                            mk = maskAh if halo_prev else maskA
                            c.tt("pool", Pm[ub][:], Pt[ub][:], mk[:], ALU.mult, r=[("Pt", ub)], w=[("Pm", ub)])
                            c.mm(psO[ub][:, 0:128], vprev, Pm[ub][:, 0:128], True, False, r=["Vb", ("Pm", ub)], w=[("psO", ub)])
                            c.mm(psO[ub][:, 0:128], vcur, Pm[ub][:, 128:256], False, True, r=["Vb", ("Pm", ub)], w=[("psO", ub)])
                            c.mm(psO[ub][:, 128:256], onesb[:], Pm[ub][:, 0:128], True, False, r=[("Pm", ub)], w=[("psO", ub)])
                            c.mm(psO[ub][:, 128:256], onesb[:], Pm[ub][:, 128:256], False, True, r=[("Pm", ub)], w=[("psO", ub)])
                            pso = psO[ub][:, 0:256].rearrange("p (a b) -> p a b", a=2)
                            if br == 1:
                                c.cp("dve", av, pso, r=[("psO", ub)], w=["acc"])
                            else:
                                c.tt("dve", av, pso, av, ALU.add, r=[("psO", ub), "acc"], w=["acc"])
                    c.recip(rl[:], acc[:, 1, :], r=["acc"], w=["rl"])
                    c.tt("dve", mst[:], acc[:, 0, :], rl[:], ALU.mult, r=["acc", "rl"], w=["mst"])
                    c.dma(mixT_s[:, h, n * 2048:(n + 1) * 2048], mst[:], r=["mst"])

        phase(phb1)

        def phb2(c):
            NCH = TT // 64
            NCHH = TH // 64
            Wb = c.sb([128, NKC, 4, 128], BF16)
            W32 = [c.sb([128, NKC, 128], F32) for _ in range(2)]
            hTt = [c.sb([128, NKC, 512], BF16) for _ in range(2)]
            KeT = c.sb([128, TT], BF16)
            QeT = c.sb([128, TM], BF16)
            gT = c.sb([128, TM], BF16)
            oT = c.sb([128, TM], F32)
            Kem = c.sb([128, TT // 128, 128], BF16)
            Vm = c.sb([128, TT // 128, 128], BF16)
            eref = c.sb([128, NCH + 1], F32)
            elast = c.sb([128, NCH], F32)
            e2 = c.sb([128, NCH], F32)
            dl2 = c.sb([128, 8], F32)
            sg = c.sb([128, 512], F32)
            sgn = c.sb([128, 512], F32)
            ff = c.sb([128, 512], F32)
            lf = c.sb([128, 512], F32)
            bb = c.sb([128, 512], F32)
            dd = c.sb([128, 512], F32)
            eq = c.sb([128, 512], F32)
            ek = c.sb([128, 512], F32)
            qs = c.sb([128, 512], F32)
            Sst = c.sb([128, 128], F32)
            Sb = c.sb([128, 128], BF16)
            aT = [c.sb([128, 64], BF16) for _ in range(2)]
            sq = c.sb([128, 512], BF16)
            sd = c.sb([128, 512], F32)
            rs = c.sb([128, 512], F32)
            tt_ = c.sb([128, 512], F32)
            hst = [c.sb([128, 512], BF16) for _ in range(2)]
            pg = [c.ps() for _ in range(2)]
            px0 = c.ps()
            px = [px0, px0]
            pT = c.ps([128, 1024], BF16)
            psA = [c.ps() for _ in range(2)]
            psU = [c.ps() for _ in range(2)]
            cnt = {"pg": 0, "px": 0}

            def nextpg():
                i = cnt["pg"] % 2
                cnt["pg"] += 1
                return i

            for h in range(NH):
                load_head_w(c, Wb, W32, h, (3, 4, 5, 6), "g")
                c.S.add("pool", lambda e: e.memset(eref[:, NCH:NCH + 1], 1.0), (), ["eref"])
                for j in range(NT):
                    hb = j % 2
                    main = j >= NTH
                    jm = j - NTH
                    c.dma(hTt[hb][:], hT_s[:, :, j * 512:(j + 1) * 512], w=[("hTt", hb)])
                    tk = slice(j * 512, (j + 1) * 512)
                    pi_ = nextpg()
                    proj(c, pg[pi_][:], Wb, 1, hTt, hb, w=[("pg", pi_)])
                    c.act(sg[:], pg[pi_][:], AF.Sigmoid, r=[("pg", pi_)], w=["sg"])
                    c.act(sgn[:], pg[pi_][:], AF.Sigmoid, scale=-1.0, r=[("pg", pi_)], w=["sgn"])
                    c.ts("dve", ff[:], sg[:], omlb[:, h:h + 1], lb[:, h:h + 1], ALU.mult, ALU.add, r=["sg"], w=["ff"])
                    c.act(lf[:], ff[:], AF.Ln, r=["ff"], w=["lf"])
                    c.S.add("dve", lambda e: e.tensor_tensor_scan(out=bb[:], data0=resetm[:], data1=lf[:], initial=0.0,
                                                                  op0=ALU.mult, op1=ALU.add), ["lf"], ["bb"])
                    bv = bb[:].rearrange("p (c j) -> p c j", j=64)
                    c.tt("dve", dd[:].rearrange("p (c j) -> p c j", j=64), bv, bv[:, :, 31:32].to_broadcast([128, 8, 64]),
                         ALU.subtract, r=["bb"], w=["dd"])
                    c.act(ek[:], dd[:], AF.Exp, scale=-1.0, r=["dd"], w=["ek"])
                    om = omlb if main else omlbv
                    c.stt(KeT[:, tk], sgn[:], om[:, h:h + 1], ek[:], ALU.mult, ALU.mult, r=["sgn", "ek"], w=["KeT"])
                    c.act(eref[:, j * 8:(j + 1) * 8], bv[:, :, 31], AF.Exp, r=["bb"], w=["eref"])
                    c.act(elast[:, j * 8:(j + 1) * 8], bv[:, :, 63], AF.Exp, r=["bb"], w=["elast"])
                    c.tt("dve", dl2[:], bv[:, :, 63], bv[:, :, 31], ALU.subtract, r=["bb"], w=["dl2"])
                    c.act(e2[:, j * 8:(j + 1) * 8], dl2[:], AF.Exp, r=["dl2"], w=["e2"])
                    pv = pT
                    for q4 in range(4):
                        c.tr(pv[:, q4 * 128:(q4 + 1) * 128], KeT[:, j * 512 + q4 * 128:j * 512 + (q4 + 1) * 128], identb[:],
                             r=["KeT"], w=["pT"])
                    c.cp("act", Kem[:, j * 4:(j + 1) * 4, :], pv[:, 0:512].rearrange("p (a b) -> p a b", a=4),
                         r=["pT"], w=["Kem"])
                    pi_ = nextpg()
                    for q4 in range(4):
                        for kc in range(NKC):
                            c.mm(pg[pi_][:, q4 * 128:(q4 + 1) * 128], hTt[hb][:, kc, q4 * 128:(q4 + 1) * 128],
                                 Wb[:, kc, 2, :], kc == 0, kc == NKC - 1, r=[("Wb", 2), ("hTt", hb)], w=[("pg", pi_)])
                    c.cp("act", Vm[:, j * 4:(j + 1) * 4, :], pg[pi_][:].rearrange("p (a b) -> p a b", a=4),
                         r=[("pg", pi_)], w=["Vm"])
                    if main:
                        c.act(eq[:], dd[:], AF.Exp, r=["dd"], w=["eq"])
                        pi_ = nextpg()
                        proj(c, pg[pi_][:], Wb, 0, hTt, hb, w=[("pg", pi_)])
                        c.act(qs[:], pg[pi_][:], AF.Silu, r=[("pg", pi_)], w=["qs"])
                        c.tt("dve", QeT[:, jm * 512:(jm + 1) * 512], qs[:], eq[:], ALU.mult, r=["qs", "eq"], w=["QeT"])
                        pi_ = nextpg()
                        proj(c, pg[pi_][:], Wb, 3, hTt, hb, w=[("pg", pi_)])
                        c.act(gT[:, jm * 512:(jm + 1) * 512], pg[pi_][:], AF.Silu, r=[("pg", pi_)], w=["gT"])
                c.S.add("pool", lambda e: e.memset(Sst[:], 0.0), (), ["S"])
                c.S.add("pool", lambda e: e.memset(Sb[:], 0.0), (), ["Sb"])
                for ch in range(NCH):
                    t128, half = ch // 2, ch % 2
                    po = 64 * half
                    main = ch >= NCHH
                    ub = ch % 2
                    if main:
                        cm = ch - NCHH
                        qcols = QeT[:, cm * 64:(cm + 1) * 64]
                        c.mm(psA[ub][po:po + 64, 0:64], KeT[:, ch * 64:(ch + 1) * 64], qcols,
                             r=["KeT", "QeT"], w=[("psA", ub)])
                        c.tt("dve", aT[ub][po:po + 64, :], psA[ub][po:po + 64, 0:64], maskH[po:po + 64, :], ALU.mult,
                             r=[("psA", ub)], w=[("aT", ub)])
                        c.mm(psA[ub][:, 128:192], Vm[po:po + 64, t128, :], aT[ub][po:po + 64, :], True, False,
                             r=["Vm", ("aT", ub)], w=[("psA", ub)])
                        c.mm(psA[ub][:, 128:192], Sb[:], qcols, False, True, r=["Sb", "QeT"], w=[("psA", ub)])
                        c.cp("act", oT[:, cm * 64:(cm + 1) * 64], psA[ub][:, 128:192], r=[("psA", ub)], w=["oT"])
                    c.mm(psU[ub][:, 0:128], Kem[po:po + 64, t128, :], Vm[po:po + 64, t128, :],
                         r=["Kem", "Vm"], w=[("psU", ub)])
                    c.ts("dve", Sst[:], Sst[:], elast[:, ch:ch + 1], None, ALU.mult, r=["S", "elast"], w=["S"])
                    c.stt(Sst[:], psU[ub][:, 0:128], e2[:, ch:ch + 1], Sst[:], ALU.mult, ALU.add,
                          r=[("psU", ub), "S", "e2"], w=["S"])
                    if ch + 1 >= NCHH and ch + 1 < NCH:
                        c.ts("dve", Sb[:], Sst[:], eref[:, ch + 1:ch + 2], None, ALU.mult, r=["S", "eref"], w=["Sb"])
                for jm in range(NTM):
                    sb_ = jm % 2
                    cs = slice(jm * 512, (jm + 1) * 512)
                    c.act(sq[:], oT[:, cs], AF.Square, r=["oT"], w=["sq"])
                    pxi = cnt["px"] % 2
                    cnt["px"] += 1
                    c.mm(px[pxi][:], onesb[:], sq[:], r=["sq"], w=[("px", id(px[pxi]))])
                    c.act(sd[:], px[pxi][:], AF.Sqrt, scale=1.0 / 128.0, bias=c_eps128, r=[("px", id(px[pxi]))], w=["sd"])
                    c.recip(rs[:], sd[:], r=["sd"], w=["rs"])
                    c.stt(tt_[:], oT[:, cs], hgnw[:, 0:1], rs[:], ALU.mult, ALU.mult, r=["oT", "rs"], w=["tt"])
                    c.tt("dve", hst[sb_][:], tt_[:], gT[:, cs], ALU.mult, r=["tt", "gT"], w=[("hst", sb_)])
                    c.dma(mixT_s[:, NH + h, cs], hst[sb_][:], r=[("hst", sb_)])

        phase(phb2)

        def phc(c):
            NOG = NKC // 4
            HALF = NFF // 2
            PF = min(16, HALF)
            mixT = c.sb([128, NKC, 512], BF16)
            xT = c.sb([128, NKC, 512], F32)
            xtok = [c.sb([128, D], F32) for _ in range(2)]
            h2T = c.sb([128, NKC, 512], BF16)
            uT = c.sb([128, HALF, 512], BF16)
            pan = [c.sb([128, 16, 512], BF16) for _ in range(3)]
            sqa = c.sb([128, NKC, 512], BF16)
            sd = c.sb([128, 512], F32)
            rr = c.sb([128, 512], F32)
            tmp = [c.sb([128, 512], F32) for _ in range(2)]
            rl = [c.sb([128, 512], BF16) for _ in range(2)]
            pa = [c.ps() for _ in range(4)]
            pb = [c.ps() for _ in range(4)]
            cnt = {"pan": 0, "pa": 0, "t": 0}
            wo_v = wo_s.rearrange("(kc p) n -> p kc n", p=128)
            wf1_v = wf1_s.rearrange("(kc p) n -> p kc n", p=128)
            wf2_v = wf2_s.rearrange("(kc p) n -> p kc n", p=128)

            def nextpan():
                i = cnt["pan"] % 3
                cnt["pan"] += 1
                return i

            def nextpa():
                i = cnt["pa"] % 4
                cnt["pa"] += 1
                return i

            for i in range(NTM):
                tok0 = TH + i * 512
                c.dma(mixT[:], mixT_s[:, :, i * 512:(i + 1) * 512], w=["mixT"])
                for blk in range(4):
                    xb = blk % 2
                    c.dma(xtok[xb][:], x[tok0 + blk * 128:tok0 + (blk + 1) * 128, :], w=[("xtok", xb)])
                    for k4 in range(NOG):
                        pi_ = nextpa()
                        for q4 in range(4):
                            kc = k4 * 4 + q4
                            c.tr(pa[pi_][:, q4 * 128:(q4 + 1) * 128], xtok[xb][:, kc * 128:(kc + 1) * 128], identf[:],
                                 r=[("xtok", xb)], w=[("pa", pi_)])
                        c.cp("act" if k4 % 2 == 0 else "dve", xT[:, k4 * 4:(k4 + 1) * 4, blk * 128:(blk + 1) * 128],
                             pa[pi_][:].rearrange("p (a b) -> p a b", a=4), r=[("pa", pi_)], w=[("xT", k4 * 4 + q_) for q_ in range(4)])
                c.act(sqa[:, 0:NH, :], mixT[:, 0:NH, :], AF.Square, r=["mixT"], w=["sqa"])
                pi_ = nextpa()
                for hh in range(NH):
                    c.mm(pa[pi_][:], onesb[:], sqa[:, hh, :], hh == 0, hh == NH - 1, r=["sqa"], w=[("pa", pi_)])
                c.act(sd[:], pa[pi_][:], AF.Sqrt, scale=1.0 / W, bias=c_eps, r=[("pa", pi_)], w=["sd"])
                c.recip(rr[:], sd[:], r=["sd"], w=["rr"])
                for hh in range(NH):
                    c.stt(mixT[:, hh, :], mixT[:, hh, :], aonw[:, hh:hh + 1], rr[:], ALU.mult, ALU.mult,
                          r=["mixT", "rr"], w=["mixT"])
                for og in range(NOG):
                    pn = nextpan()
                    c.dma(pan[pn][:, 0:NKC, :], wo_v[:, :, og * 512:(og + 1) * 512], w=[("pan", pn)])
                    for oc4 in range(4):
                        oc = og * 4 + oc4
                        pi_ = nextpa()
                        for kc in range(NKC):
                            c.mm(pa[pi_][:], pan[pn][:, kc, oc4 * 128:(oc4 + 1) * 128], mixT[:, kc, :], kc == 0, kc == NKC - 1,
                                 r=[("pan", pn), "mixT"], w=[("pa", pi_)])
                        c.stt(xT[:, oc, :], pa[pi_][:], gate1[:, oc:oc + 1], xT[:, oc, :], ALU.mult, ALU.add,
                              r=[("pa", pi_), ("xT", oc)], w=[("xT", oc)])
                c.act(sqa[:], xT[:], AF.Square, r=[("xT", k_) for k_ in range(NKC)], w=["sqa"])
                pi_ = nextpa()
                for kc in range(NKC):
                    c.mm(pa[pi_][:], onesb[:], sqa[:, kc, :], kc == 0, kc == NKC - 1, r=["sqa"], w=[("pa", pi_)])
                c.act(sd[:], pa[pi_][:], AF.Sqrt, scale=1.0 / D, bias=c_eps, r=[("pa", pi_)], w=["sd"])
                c.recip(rr[:], sd[:], r=["sd"], w=["rr"])
                for kc in range(NKC):
                    tb = cnt["t"] % 2
                    cnt["t"] += 1
                    c.stt(tmp[tb][:], xT[:, kc, :], g2[:, kc:kc + 1], rr[:], ALU.mult, ALU.mult,
                          r=[("xT", kc), "rr"], w=[("tmp", tb)])
                    c.ts("dve", h2T[:, kc, :], tmp[tb][:], sh2[:, kc:kc + 1], None, ALU.add, r=[("tmp", tb)], w=["h2T"])
                for hf in range(2):
                    for fg in range(HALF // 4):
                        pn = nextpan()
                        col0 = (hf * HALF + fg * 4) * 128
                        c.dma(pan[pn][:, 0:NKC, :], wf1_v[:, :, col0:col0 + 512], w=[("pan", pn)])
                        for f4 in range(4):
                            fl = fg * 4 + f4
                            pi_ = nextpa()
                            for kc in range(NKC):
                                c.mm(pa[pi_][:], pan[pn][:, kc, f4 * 128:(f4 + 1) * 128], h2T[:, kc, :], kc == 0, kc == NKC - 1,
                                     r=[("pan", pn), "h2T"], w=[("pa", pi_)])
                            tb = cnt["t"] % 2
                            cnt["t"] += 1
                            c.act(rl[tb][:], pa[pi_][:], AF.Relu, r=[("pa", pi_)], w=[("rl", tb)])
                            c.tt("pool", uT[:, fl, :], rl[tb][:], rl[tb][:], ALU.mult, r=[("rl", tb)], w=[("uT", fl)])
                    for og in range(NOG):
                        for q in range(HALF // PF):
                            pn = nextpan()
                            k0 = hf * HALF + q * PF
                            c.dma(pan[pn][:, 0:PF, :], wf2_v[:, k0:k0 + PF, og * 512:(og + 1) * 512], w=[("pan", pn)])
                            for oc4 in range(4):
                                for f in range(PF):
                                    fl = q * PF + f
                                    c.mm(pb[oc4][:], pan[pn][:, f, oc4 * 128:(oc4 + 1) * 128], uT[:, fl, :],
                                         q == 0 and f == 0, q == HALF // PF - 1 and f == PF - 1,
                                         r=[("pan", pn), ("uT", fl)], w=[("pb", oc4)])
                        for oc4 in range(4):
                            oc = og * 4 + oc4
                            c.stt(xT[:, oc, :], pb[oc4][:], gate2[:, oc:oc + 1], xT[:, oc, :], ALU.mult, ALU.add,
                                  r=[("pb", oc4), ("xT", oc)], w=[("xT", oc)])
                for blk in range(4):
                    xb = blk % 2
                    for k4 in range(NOG):
                        pi_ = nextpa()
                        for q4 in range(4):
                            kc = k4 * 4 + q4
                            c.tr(pa[pi_][:, q4 * 128:(q4 + 1) * 128], xT[:, kc, blk * 128:(blk + 1) * 128], identf[:],
                                 r=[("xT", kc)], w=[("pa", pi_)])
                        c.cp("act" if k4 % 2 == 0 else "dve", xtok[xb][:, k4 * 512:(k4 + 1) * 512], pa[pi_][:],
                             r=[("pa", pi_)], w=[("xtok", xb)])
                    c.dma(out[i * 512 + blk * 128:i * 512 + (blk + 1) * 128, :], xtok[xb][:], r=[("xtok", xb)])

        phase(phc)
    return nc


def make_consts():
    bf = ml_dtypes.bfloat16
    p = np.arange(128)
    i = np.arange(128)
    mprev = (p[:, None] >= i[None, :]).astype(np.float32)
    mcur = (p[:, None] <= i[None, :]).astype(np.float32)
    maskA = np.concatenate([mprev, mcur], axis=1).astype(bf)
    t = np.arange(64)
    maskH = ((p[:, None] % 64) <= t[None, :]).astype(np.float32).astype(bf)
    rotT = np.zeros((128, 128), np.float32)
    for m in range(16):
        rotT[m + 16, m] = -1.0
    for m in range(16, 32):
        rotT[m - 16, m] = 1.0
    resetm = np.ones((128, 512), np.float32)
    resetm[:, ::64] = 0.0
    invf = np.zeros((128, 1), np.float32)
    half = 16
    fr = (np.float32(ROPE_THETA) ** (-(np.arange(half, dtype=np.float32) * np.float32(2.0)) / np.float32(32))).astype(np.float32)
    invf[0:16, 0] = fr
    invf[16:32, 0] = fr
    cst = np.zeros((128, 4), np.float32)
    cst[:, 0] = EPS
    cst[:, 1] = EPS * 128.0
    cst[:, 2] = -MSHIFT
    return {
        "identb": np.eye(128, dtype=np.float32).astype(bf), "identf": np.eye(128, dtype=np.float32),
        "onesb": np.ones((128, 128), np.float32).astype(bf), "rotT": rotT.astype(bf), "maskA": maskA, "maskH": maskH,
        "resetm": resetm, "invf": invf, "cst": cst,
    }


def pk(v, nkc):
    return np.ascontiguousarray(np.asarray(v, np.float32).reshape(nkc, 128).T)


def make_in_maps(inputs, D, NH, DFF, TM, TH, n_seg):
    NKC = D // 128
    x = np.asarray(inputs["x"], np.float32)
    B, S, _ = x.shape
    posi = np.asarray(inputs["positions"], np.int32)
    consts = make_consts()
    shared = {
        "n1w": pk(inputs["norm1_w"][0], NKC), "n2w": pk(inputs["norm2_w"][0], NKC),
        "bada": np.ascontiguousarray(np.asarray(inputs["b_ada"][0], np.float32).reshape(6 * NKC, 128).T),
        "w_ada": np.ascontiguousarray(np.asarray(inputs["w_ada"][0], np.float32)),
        "w_in": np.ascontiguousarray(np.asarray(inputs["w_in"][0], np.float32)),
        "qnw": pk(inputs["q_norm_w"][0], 1), "knw": pk(inputs["k_norm_w"][0], 1),
        "aonw": pk(inputs["attn_out_norm_w"][0], NH),
        "lbl": np.ascontiguousarray(np.asarray(inputs["hg_lb_logits"], np.float32).reshape(2 * NH, 128).T),
        "hgnw": pk(inputs["hg_norm_w"][0], 1),
        "w_out": np.ascontiguousarray(np.asarray(inputs["w_out"][0], np.float32)),
        "w_ff1": np.ascontiguousarray(np.asarray(inputs["w_ff1"][0], np.float32)),
        "w_ff2": np.ascontiguousarray(np.asarray(inputs["w_ff2"][0], np.float32)),
    }
    shared.update(consts)
    maps = []
    for b in range(B):
        for j in range(n_seg):
            s0 = j * TM
            xin = np.zeros((TH + TM, D), np.float32)
            pin = np.zeros((1, TH + TM), np.int32)
            if j > 0:
                xin[:] = x[b, s0 - TH:s0 + TM]
                pin[0, :] = posi[b, s0 - TH:s0 + TM]
            else:
                xin[TH:] = x[b, 0:TM]
                pin[0, TH:] = posi[b, 0:TM]
            m = dict(shared)
            m["x"] = xin
            m["pos"] = pin
            m["cT"] = pk(inputs["c"][b], NKC)
            m["valid"] = np.full((128, 1), 1.0 if j > 0 else 0.0, np.float32)
            maps.append(m)
    return maps


_NC_CACHE = {}


def kernel(**inputs):
    D, NH, DFF, TM, TH = 2048, 8, 8192, 4096, 2048
    x = np.asarray(inputs["x"])
    B, S, _ = x.shape
    n_seg = S // TM
    key = (D, NH, DFF, TM, TH)
    if key not in _NC_CACHE:
        _NC_CACHE[key] = build_program(*key)
    nc = _NC_CACHE[key]
    in_maps = make_in_maps(inputs, D, NH, DFF, TM, TH, n_seg)
    res = run_bass_kernel_spmd(nc, in_maps, core_ids=list(range(len(in_maps))))
    outp = np.empty((B, S, D), np.float32)
    k = 0
    for b in range(B):
        for j in range(n_seg):
            outp[b, j * TM:(j + 1) * TM] = np.asarray(res.results[k]["out"], np.float32)
            k += 1
    return outp
```

```python
import math
from contextlib import ExitStack
import numpy as np
import ml_dtypes
import concourse.bass as bass
import concourse.mybir as mybir
from concourse.bass_utils import run_bass_kernel_spmd

F32 = mybir.dt.float32
BF16 = mybir.dt.bfloat16
I32 = mybir.dt.int32
AF = mybir.ActivationFunctionType
ALU = mybir.AluOpType

EPS = 1e-6
MSHIFT = 8.0
ROPE_THETA = 500000.0
TWO_PI = 2.0 * math.pi
CW1 = 6.28125
CW2 = TWO_PI - CW1


class Op:
    __slots__ = ("eng", "fn", "deps", "sig", "sem", "val", "idx")

    def __init__(self, eng, fn):
        self.eng = eng
        self.fn = fn
        self.deps = []
        self.sig = False
        self.sem = None
        self.val = 0
        self.idx = 0


class Sched:
    G = None
    ENGS = ("pe", "act", "dve", "pool", "sp")
    NS_DMA = 24
    ROT = 30000

    def __init__(self, nc):
        self.nc = nc
        self.ops = {e: [] for e in self.ENGS}
        self.last_writer = {}
        self.readers = {}

    def add(self, eng, fn, reads=(), writes=()):
        op = Op(eng, fn)
        ds = {}
        for r in reads:
            w = self.last_writer.get(r)
            if w is not None:
                ds[id(w)] = w
        for w in writes:
            lw = self.last_writer.get(w)
            if lw is not None:
                ds[id(lw)] = lw
            for rd in self.readers.get(w, ()):
                ds[id(rd)] = rd
            self.readers[w] = []
            self.last_writer[w] = op
        for r in reads:
            self.readers.setdefault(r, []).append(op)
        for d in ds.values():
            if d is op:
                continue
            if d.eng == "pe" and eng == "pe":
                continue
            op.deps.append(d)
            d.sig = True
        if eng == "sp":
            op.sig = True
        self.ops[eng].append(op)
        return op

    def emit(self):
        nc = self.nc
        G = Sched.G
        with ExitStack() as st:
            for e in ("pe", "act", "dve", "pool"):
                sl = G["sems"][e]
                c = G["cnt"][e]
                for o in self.ops[e]:
                    if o.sig:
                        o.sem = sl[c // self.ROT]
                        o.val = c % self.ROT + 1
                        c += 1
                G["cnt"][e] = c
            ring = G["ring"]
            NS = self.NS_DMA
            base = G["dma"]
            for i, o in enumerate(self.ops["sp"]):
                gi = base + i
                o.sem = ring[gi % NS]
                o.val = 16 * (gi // NS + 1)
                o.idx = gi
            G["dma"] = base + len(self.ops["sp"])
            ops = self.ops

            def stream(ename, e):
                waited = {}

                def wait(sem, val):
                    k = id(sem)
                    if waited.get(k, 0) >= val:
                        return
                    waited[k] = val
                    e.wait_ge(sem, val)

                for o in ops[ename]:
                    for d in o.deps:
                        wait(d.sem, d.val)
                    if ename == "sp" and o.idx >= NS:
                        wait(o.sem, o.val - 16)
                    ins = o.fn(e)
                    if o.sig:
                        ins.then_inc(o.sem, 16 if ename == "sp" else 1)
                if ename == "sp":
                    n = len(ops["sp"])
                    for i in range(max(0, n - NS), n):
                        o = ops["sp"][i]
                        wait(o.sem, o.val)

            blk = st.enter_context(nc.Block())

            @blk.tensor
            def _(e):
                stream("pe", e)

            @blk.scalar
            def _(e):
                stream("act", e)

            @blk.vector
            def _(e):
                stream("dve", e)

            @blk.gpsimd
            def _(e):
                stream("pool", e)

            @blk.sync
            def _(e):
                stream("sp", e)


class Ctx:
    def __init__(self, nc, st):
        self.nc = nc
        self.st = st
        self.S = Sched(nc)
        self.n = 0

    CNT = [0]

    def sb(self, shape, dt, name=None):
        Ctx.CNT[0] += 1
        return self.st.enter_context(self.nc.sbuf_tensor(f"t{Ctx.CNT[0]}", list(shape), dt))

    def ps(self, shape=(128, 512), dt=F32, name=None):
        Ctx.CNT[0] += 1
        return self.st.enter_context(self.nc.psum_tensor(f"p{Ctx.CNT[0]}", list(shape), dt))

    def dma(self, out, in_, r=(), w=()):
        return self.S.add("sp", lambda e: e.dma_start(out=out, in_=in_), r, w)

    def mm(self, out, lhsT, rhs, start=True, stop=True, r=(), w=()):
        return self.S.add("pe", lambda e: e.matmul(out, lhsT=lhsT, rhs=rhs, start=start, stop=stop,
                                                    skip_group_check=True), r, w)

    def tr(self, out, in_, ident, r=(), w=()):
        return self.S.add("pe", lambda e: e.transpose(out, in_, ident), r, w)

    def act(self, out, in_, func, scale=1.0, bias=None, accum=None, r=(), w=()):
        def f(e):
            kw = {}
            if bias is not None:
                kw["bias"] = bias
            if accum is not None:
                kw["accum_out"] = accum
            return e.activation(out=out, in_=in_, func=func, scale=scale, **kw)
        return self.S.add("act", f, r, w)

    def tt(self, eng, out, in0, in1, op, r=(), w=()):
        return self.S.add(eng, lambda e: e.tensor_tensor(out=out, in0=in0, in1=in1, op=op), r, w)

    def ts(self, eng, out, in0, s1, s2, op0, op1=None, r=(), w=()):
        if op1 is None:
            return self.S.add(eng, lambda e: e.tensor_scalar(out=out, in0=in0, scalar1=s1, scalar2=None, op0=op0), r, w)
        return self.S.add(eng, lambda e: e.tensor_scalar(out=out, in0=in0, scalar1=s1, scalar2=s2, op0=op0, op1=op1), r, w)

    def stt(self, out, in0, scalar, in1, op0, op1, r=(), w=()):
        return self.S.add("dve", lambda e: e.scalar_tensor_tensor(out=out, in0=in0, scalar=scalar, in1=in1,
                                                                   op0=op0, op1=op1), r, w)

    def cp(self, eng, out, in_, r=(), w=()):
        if eng == "act":
            return self.act(out, in_, AF.Copy, r=r, w=w)
        return self.S.add(eng, lambda e: e.tensor_copy(out=out, in_=in_), r, w)

    def recip(self, out, in_, r=(), w=()):
        return self.S.add("dve", lambda e: e.reciprocal(out=out, in_=in_), r, w)


def build_program(D, NH, DFF, TM, TH, only=None):
    NKC = D // 128
    W = 128 * NH
    assert 2 * W == D
    TT = TM + TH
    NFF = DFF // 128
    NT = TT // 512
    NTH = TH // 512
    NTM = TM // 512
    NSPAN = TM // 2048
    assert TH == 2048 and TM % 2048 == 0
    ISQ = 1.0 / math.sqrt(128.0)

    nc = bass.Bass("TRN2", target_bir_lowering=False)

    def din(name, shape, dt=F32):
        return nc.dram_tensor(name, list(shape), dt, kind="ExternalInput").ap()

    x = din("x", [TT, D])
    pos = din("pos", [1, TT], I32)
    cT = din("cT", [128, NKC])
    valid_d = din("valid", [128, 1])
    n1w_d = din("n1w", [128, NKC])
    n2w_d = din("n2w", [128, NKC])
    bada_d = din("bada", [128, 6 * NKC])
    w_ada = din("w_ada", [D, 6 * D])
    w_in = din("w_in", [D, 7 * W])
    qnw_d = din("qnw", [128, 1])
    knw_d = din("knw", [128, 1])
    aonw_d = din("aonw", [128, NH])
    lbl_d = din("lbl", [128, 2 * NH])
    hgnw_d = din("hgnw", [128, 1])
    w_out = din("w_out", [D, D])
    w_ff1 = din("w_ff1", [D, DFF])
    w_ff2 = din("w_ff2", [DFF, D])
    identb_d = din("identb", [128, 128], BF16)
    identf_d = din("identf", [128, 128])
    onesb_d = din("onesb", [128, 128], BF16)
    rotT_d = din("rotT", [128, 128], BF16)
    maskA_d = din("maskA", [128, 256], BF16)
    maskH_d = din("maskH", [128, 64], BF16)
    resetm_d = din("resetm", [128, 512])
    invf_d = din("invf", [128, 1])
    cst_d = din("cst", [128, 4])
    out = nc.dram_tensor("out", [TM, D], F32, kind="ExternalOutput").ap()

    hT_s = nc.dram_tensor("hT_s", [128, NKC, TT], BF16).ap()
    mixT_s = nc.dram_tensor("mixT_s", [128, NKC, TM], BF16).ap()
    cs_s = nc.dram_tensor("cs_s", [2, 128, TT], F32).ap()
    wo_s = nc.dram_tensor("wo_s", [D, D], BF16).ap()
    wf1_s = nc.dram_tensor("wf1_s", [D, DFF], BF16).ap()
    wf2_s = nc.dram_tensor("wf2_s", [DFF, D], BF16).ap()

    with ExitStack() as gst:
        Sched.G = {"sems": {e: [gst.enter_context(nc.semaphore(f"s_{e}{i}")) for i in range(5)]
                            for e in ("pe", "act", "dve", "pool")},
                   "cnt": {e: 0 for e in ("pe", "act", "dve", "pool")},
                   "ring": [gst.enter_context(nc.semaphore(f"s_dma{i}")) for i in range(Sched.NS_DMA)],
                   "dma": 0}

        def gsb(shape, dt, name):
            return gst.enter_context(nc.sbuf_tensor(name + "_sb", list(shape), dt))

        modv = gsb([128, 6 * NKC], F32, "modv")
        g1 = gsb([128, NKC], F32, "g1")
        g2 = gsb([128, NKC], F32, "g2")
        lb = gsb([128, NH], F32, "lb")
        omlb = gsb([128, NH], F32, "omlb")
        omlbv = gsb([128, NH], F32, "omlbv")
        qnw = gsb([128, 1], F32, "qnw")
        knw = gsb([128, 1], F32, "knw")
        aonw = gsb([128, NH], F32, "aonw")
        hgnw = gsb([128, 1], F32, "hgnw")
        valid = gsb([128, 1], F32, "validt")
        invf = gsb([128, 1], F32, "invft")
        cst = gsb([128, 4], F32, "cstt")
        identb = gsb([128, 128], BF16, "identbt")
        identf = gsb([128, 128], F32, "identft")
        onesb = gsb([128, 128], BF16, "onesbt")
        rotT = gsb([128, 128], BF16, "rotTt")
        maskA = gsb([128, 256], BF16, "maskAt")
        maskAh = gsb([128, 256], BF16, "maskAht")
        maskH = gsb([128, 64], BF16, "maskHt")
        resetm = gsb([128, 512], F32, "resetmt")
        sh1 = modv[:, 0 * NKC:1 * NKC]
        sc1 = modv[:, 1 * NKC:2 * NKC]
        gate1 = modv[:, 2 * NKC:3 * NKC]
        sh2 = modv[:, 3 * NKC:4 * NKC]
        sc2 = modv[:, 4 * NKC:5 * NKC]
        gate2 = modv[:, 5 * NKC:6 * NKC]
        c_eps = cst[:, 0:1]
        c_eps128 = cst[:, 1:2]
        c_mshift = cst[:, 2:3]

        pcount = [0]

        def phase(fn):
            pcount[0] += 1
            if only is not None and pcount[0] not in only:
                return
            with ExitStack() as st:
                c = Ctx(nc, st)
                fn(c)
                c.S.emit()
            pass

        def ph0(c):
            for t, d in ((valid, valid_d), (invf, invf_d), (cst, cst_d), (identb, identb_d), (identf, identf_d),
                         (onesb, onesb_d), (rotT, rotT_d), (maskA, maskA_d), (maskH, maskH_d), (resetm, resetm_d),
                         (qnw, qnw_d), (knw, knw_d), (aonw, aonw_d), (hgnw, hgnw_d)):
                c.dma(t[:], d, w=[("g", id(t))])
            n1w = c.sb([128, NKC], F32)
            n2w = c.sb([128, NKC], F32)
            bada = c.sb([128, 6 * NKC], F32)
            lbl = c.sb([128, 2 * NH], F32)
            cTt = c.sb([128, NKC], F32)
            c.dma(n1w[:], n1w_d, w=["n1w"])
            c.dma(n2w[:], n2w_d, w=["n2w"])
            c.dma(bada[:], bada_d, w=["bada"])
            c.dma(lbl[:], lbl_d, w=["lbl"])
            c.dma(cTt[:], cT, w=["cT"])
            c.cp("dve", maskAh[:, 128:256], maskA[:, 128:256], r=[("g", id(maskA))], w=["mAh1"])
            c.ts("dve", maskAh[:, 0:128], maskA[:, 0:128], valid[:, 0:1], None, ALU.mult,
                 r=[("g", id(maskA)), ("g", id(valid))], w=["mAh0"])
            dl = c.sb([128, NH], F32)
            c.tt("dve", dl[:], lbl[:, 0:NH], lbl[:, NH:2 * NH], ALU.subtract, r=["lbl"], w=["dl"])
            c.act(lb[:], dl[:], AF.Sigmoid, r=["dl"], w=["lb"])
            c.ts("dve", omlb[:], lb[:], -1.0, 1.0, ALU.mult, ALU.add, r=["lb"], w=["omlb"])
            c.ts("dve", omlbv[:], omlb[:], valid[:, 0:1], None, ALU.mult, r=["omlb", ("g", id(valid))], w=["omlbv"])
            scs = c.sb([128, NKC], F32)
            c.act(scs[:], cTt[:], AF.Silu, r=["cT"], w=["scs"])
            zer = c.sb([128, 128], F32)
            c.S.add("pool", lambda e: e.memset(zer[:], 0.0), (), ["zer"])
            pm = c.ps([128, 512], F32)
            NJ = 6 * NKC
            c.mm(pm[:, 0:NJ], zer[:, 0:128], zer[:, 0:NJ], True, False, r=["zer"], w=["pm"])
            CH = min(6 * D, 6144)
            NCH = (6 * D) // CH
            wa = [c.sb([128, CH], F32) for _ in range(2)]
            it = 0
            for kc in range(NKC):
                for ch in range(NCH):
                    b = it % 2
                    it += 1
                    c.dma(wa[b][:], w_ada[kc * 128:(kc + 1) * 128, ch * CH:(ch + 1) * CH], w=[("wa", b)])
                    for jj in range(CH // 128):
                        j = ch * (CH // 128) + jj
                        last = (kc == NKC - 1)
                        c.mm(pm[:, j:j + 1], wa[b][:, jj * 128:(jj + 1) * 128], scs[:, kc:kc + 1], False, last,
                             r=[("wa", b), "scs"], w=["pm"])
            c.tt("dve", modv[:], pm[:, 0:NJ], bada[:], ALU.add, r=["pm", "bada"], w=["modv"])
            c.stt(g1[:], sc1, 1.0, n1w[:], ALU.add, ALU.mult, r=["modv", "n1w"], w=["g1"])
            c.stt(g2[:], sc2, 1.0, n2w[:], ALU.add, ALU.mult, r=["modv", "n2w"], w=["g2"])

        phase(ph0)

        def php(c):
            CW = 4096
            f = [c.sb([128, CW], F32) for _ in range(3)]
            g = [c.sb([128, CW], BF16) for _ in range(3)]
            engs = ("act", "dve", "pool")
            it = 0
            for src, dst, R, Cn in ((w_out, wo_s, D, D), (w_ff1, wf1_s, D, DFF), (w_ff2, wf2_s, DFF, D)):
                for r0 in range(0, R, 128):
                    for c0 in range(0, Cn, CW):
                        cw = min(CW, Cn - c0)
                        b = it % 3
                        c.dma(f[b][:, 0:cw], src[r0:r0 + 128, c0:c0 + cw], w=[("f", b)])
                        c.cp(engs[b], g[b][:, 0:cw], f[b][:, 0:cw], r=[("f", b)], w=[("g", b)])
                        c.dma(dst[r0:r0 + 128, c0:c0 + cw], g[b][:, 0:cw], r=[("g", b)])
                        it += 1


        def pha(c):
            xa = [c.sb([128, D], F32) for _ in range(2)]
            junk = c.sb([128, D], F32)
            xs = [c.sb([128, D], BF16) for _ in range(2)]
            sm = [c.sb([128, 4], F32) for _ in range(2)]
            hst = [c.sb([128, NKC, 512], BF16) for _ in range(2)]
            tp = [c.ps([128, 1024], BF16) for _ in range(2 * ((NKC + 7) // 8))]
            NB = (NKC + 7) // 8
            for i in range(TT // 128):
                b = i % 2
                gI = i // 4
                slot = i % 4
                hb = gI % 2
                c.dma(xa[b][:], x[i * 128:(i + 1) * 128, :], w=[("xa", b)])
                c.act(junk[:], xa[b][:], AF.Square, r=[("xa", b)], w=["junk"])
                c.S.add("dve", lambda e, o_=sm[b][:, 0:1], i_=junk[:]: e.reduce_sum(out=o_, in_=i_, axis=mybir.AxisListType.X),
                        ["junk"], [("sm0", b)])
                c.act(sm[b][:, 1:2], sm[b][:, 0:1], AF.Sqrt, scale=1.0 / D, bias=c_eps, r=[("sm0", b)], w=[("sm1", b)])
                c.recip(sm[b][:, 2:3], sm[b][:, 1:2], r=[("sm1", b)], w=[("sm2", b)])
                c.ts("dve", xs[b][:], xa[b][:], sm[b][:, 2:3], None, ALU.mult, r=[("xa", b), ("sm2", b)], w=[("xs", b)])
                for kc in range(NKC):
                    bank = tp[b * NB + kc // 8]
                    tv = bank
                    k8 = kc % 8
                    c.tr(tv[:, k8 * 128:(k8 + 1) * 128], xs[b][:, kc * 128:(kc + 1) * 128], identb[:],
                         r=[("xs", b)], w=[("tp", b, kc // 8)])
                for kc in range(NKC):
                    bank = tp[b * NB + kc // 8]
                    tv = bank
                    k8 = kc % 8
                    dst = hst[hb][:, kc, slot * 128:(slot + 1) * 128]
                    if True:
                        c.ts("dve", dst, tv[:, k8 * 128:(k8 + 1) * 128], g1[:, kc:kc + 1], sh1[:, kc:kc + 1],
                             ALU.mult, ALU.add, r=[("tp", b, kc // 8)], w=[("hst", hb, slot, kc)])
                if slot == 3:
                    c.dma(hT_s[:, :, gI * 512:(gI + 1) * 512], hst[hb][:],
                          r=[("hst", hb, s_, k_) for s_ in range(4) for k_ in range(NKC)])

        phase(pha)

        def phr(c):
            PI_S = 3.1415925
            pis = [c.sb([128, 512], I32) for _ in range(2)]
            tmps = [[c.sb([128, 512], F32) for _ in range(5)] for _ in range(2)]
            kis = [c.sb([128, 512], I32) for _ in range(2)]
            for j in range(NT):
                b = j % 2
                pi_ = pis[b]
                ang, u, kf, rr, m = tmps[b]
                ki = kis[b]
                c.dma(pi_[:], pos[:, j * 512:(j + 1) * 512].partition_broadcast(128), w=[("pi", b)])
                c.cp("dve", ang[:], pi_[:], r=[("pi", b)], w=[("ang", b)])
                c.ts("dve", ang[:], ang[:], invf[:, 0:1], None, ALU.mult, r=[("ang", b)], w=[("ang", b)])
                for which, shift in ((0, math.pi / 2), (1, 0.0)):
                    c.ts("dve", u[:], ang[:], 1.0 / TWO_PI, shift / TWO_PI + 0.5, ALU.mult, ALU.add,
                         r=[("ang", b)], w=[("u", b)])
                    c.cp("dve", ki[:], u[:], r=[("u", b)], w=[("ki", b)])
                    c.cp("dve", kf[:], ki[:], r=[("ki", b)], w=[("kf", b)])
                    c.ts("dve", rr[:], ang[:], shift, None, ALU.add, r=[("ang", b)], w=[("rr", b)])
                    c.stt(rr[:], kf[:], -CW1, rr[:], ALU.mult, ALU.add, r=[("kf", b), ("rr", b)], w=[("rr", b)])
                    c.stt(rr[:], kf[:], -CW2, rr[:], ALU.mult, ALU.add, r=[("kf", b), ("rr", b)], w=[("rr", b)])
                    c.ts("dve", m[:], rr[:], math.pi, -TWO_PI, ALU.is_gt, ALU.mult, r=[("rr", b)], w=[("m", b)])
                    c.tt("dve", rr[:], rr[:], m[:], ALU.add, r=[("rr", b), ("m", b)], w=[("rr", b)])
                    c.ts("dve", m[:], rr[:], -math.pi, TWO_PI, ALU.is_lt, ALU.mult, r=[("rr", b)], w=[("m", b)])
                    c.tt("dve", rr[:], rr[:], m[:], ALU.add, r=[("rr", b), ("m", b)], w=[("rr", b)])
                    c.ts("dve", rr[:], rr[:], PI_S, -PI_S, ALU.min, ALU.max, r=[("rr", b)], w=[("rr", b)])
                    c.act(u[:], rr[:], AF.Sin, r=[("rr", b)], w=[("u", b)])
                    c.dma(cs_s[which, :, j * 512:(j + 1) * 512], u[:], r=[("u", b)])
        phase(phr)

        w_in_v = w_in.rearrange("(kc p) n -> p kc n", p=128)

        def load_head_w(c, Wb, W32, h, groups, tag):
            engs = ("act", "dve", "pool")
            for gi_, g in enumerate(groups):
                b = gi_ % 2
                rk = ("W32", id(W32[b]))
                c.dma(W32[b][:], w_in_v[:, :, g * W + h * 128:g * W + (h + 1) * 128], w=[rk])
                c.cp(engs[gi_ % 3], Wb[:, :, gi_, :], W32[b][:], r=[rk], w=[("Wb", gi_)])

        def proj(c, pg, Wb, gi_, hTt, hb, cols=slice(0, 512), w=()):
            for kc in range(NKC):
                c.mm(pg, Wb[:, kc, gi_, :], hTt[hb][:, kc, cols], kc == 0, kc == NKC - 1,
                     r=[("Wb", gi_), ("hTt", hb)], w=list(w))

        def phb1(c):
            Wb = c.sb([128, NKC, 3, 128], BF16)
            W32_ = c.sb([128, NKC, 128], F32)
            W32 = [W32_, W32_]
            hTt = [c.sb([128, NKC, 512], BF16) for _ in range(2)]
            Ct = [c.sb([128, 512], F32) for _ in range(2)]
            St = [c.sb([128, 512], F32) for _ in range(2)]
            KT1 = c.sb([128, TT], BF16)
            KT2 = c.sb([128, TT], BF16)
            KT3 = c.sb([128, TT], BF16)
            QT = c.sb([128, TM], BF16)
            Vb1 = c.sb([128, TT // 128, 128], BF16)
            Vb2 = c.sb([128, TT // 128, 128], BF16)
            Vb3 = c.sb([128, TT // 128, 128], BF16)
            VTs = [c.sb([128, 512], BF16) for _ in range(2)]
            VT2 = [c.sb([128, 512], BF16) for _ in range(2)]
            VT3 = c.sb([128, 2048], BF16)
            acc = c.sb([128, 2, 2048], F32)
            rl = c.sb([128, 2048], F32)
            mst = c.sb([128, 2048], BF16)
            raw = c.sb([128, 512], F32)
            sq = c.sb([128, 512], BF16)
            sd = c.sb([128, 512], F32)
            rs = c.sb([128, 512], F32)
            qn = c.sb([128, 512], F32)
            qnb = c.sb([128, 512], BF16)
            t1 = c.sb([128, 512], F32)
            t2 = c.sb([128, 512], F32)
            Pt = [c.sb([128, 256], BF16) for _ in range(2)]
            Pm = [c.sb([128, 256], BF16) for _ in range(2)]
            pg = [c.ps() for _ in range(2)]
            px = [c.ps() for _ in range(2)]
            pT = c.ps([128, 1024], BF16)
            psS1 = c.ps()
            psS = [psS1, px[1]]
            psO = [c.ps() for _ in range(2)]
            cnt = {"pg": 0, "px": 0, "u": 0}

            def qk_path(pgt, pgk, wcol, dst, cb):
                c.cp("act", raw[:], pgt[:], r=[pgk], w=["raw"])
                c.act(sq[:], pgt[:], AF.Square, r=[pgk], w=["sq"])
                pxi = cnt["px"] % 2
                cnt["px"] += 1
                c.mm(px[pxi][:], onesb[:], sq[:], r=["sq"], w=[("bank", id(px[pxi]))])
                c.act(sd[:], px[pxi][:], AF.Sqrt, scale=1.0 / 128.0, bias=c_eps, r=[("bank", id(px[pxi]))], w=["sd"])
                c.recip(rs[:], sd[:], r=["sd"], w=["rs"])
                c.stt(qn[:], raw[:], wcol, rs[:], ALU.mult, ALU.mult, r=["raw", "rs"], w=["qn"])
                c.cp("act", qnb[:], qn[:], r=["qn"], w=["qnb"])
                pxi = cnt["px"] % 2
                cnt["px"] += 1
                c.mm(px[pxi][:], rotT[:], qnb[:], r=["qnb"], w=[("bank", id(px[pxi]))])
                c.tt("dve", t1[:], qn[:], Ct[cb][:], ALU.mult, r=["qn", ("Ct", cb)], w=["t1"])
                c.tt("dve", t2[:], px[pxi][:], St[cb][:], ALU.mult, r=[("bank", id(px[pxi])), ("St", cb)], w=["t2"])
                c.tt("dve", dst, t1[:], t2[:], ALU.add, r=["t1", "t2"], w=["KQ"])

            for h in range(NH):
                load_head_w(c, Wb, W32, h, (0, 1, 2), "a")
                for j in range(NT):
                    hb = j % 2
                    main = j >= NTH
                    c.dma(hTt[hb][:], hT_s[:, :, j * 512:(j + 1) * 512], w=[("hTt", hb)])
                    c.dma(Ct[hb][:], cs_s[0, :, j * 512:(j + 1) * 512], w=[("Ct", hb)])
                    c.dma(St[hb][:], cs_s[1, :, j * 512:(j + 1) * 512], w=[("St", hb)])
                    tk = slice(j * 512, (j + 1) * 512)
                    pi_ = cnt["pg"] % 2
                    cnt["pg"] += 1
                    proj(c, pg[pi_][:], Wb, 1, hTt, hb, w=[("pg", pi_)])
                    qk_path(pg[pi_], ("pg", pi_), knw[:, 0:1], KT1[:, tk], hb)
                    c.cp("pool", KT2[:, tk].rearrange("p (r m) -> p r m", r=4),
                         KT1[:, tk].rearrange("p (m r) -> p r m", r=4), r=["KQ"], w=["K2"])
                    sp3 = (j // 4) * 2048
                    jj = j % 4
                    c.cp("pool", KT3[:, sp3:sp3 + 2048].rearrange("p (r m) -> p r m", r=16)[:, :, jj * 32:(jj + 1) * 32],
                         KT1[:, tk].rearrange("p (m r) -> p r m", r=16), r=["KQ"], w=["K3"])
                    pi_ = cnt["pg"] % 2
                    cnt["pg"] += 1
                    proj(c, pg[pi_][:], Wb, 2, hTt, hb, w=[("pg", pi_)])
                    c.cp("act", VTs[hb][:], pg[pi_][:], r=[("pg", pi_)], w=[("VTs", hb)])
                    c.cp("pool", VT2[hb][:].rearrange("p (r m) -> p r m", r=4),
                         VTs[hb][:].rearrange("p (m r) -> p r m", r=4), r=[("VTs", hb)], w=[("VT2", hb)])
                    c.cp("pool", VT3[:].rearrange("p (r m) -> p r m", r=16)[:, :, jj * 32:(jj + 1) * 32],
                         VTs[hb][:].rearrange("p (m r) -> p r m", r=16), r=[("VTs", hb)], w=["VT3"])
                    for src, srck, dstV in ((VTs[hb], ("VTs", hb), Vb1), (VT2[hb], ("VT2", hb), Vb2)):
                        pv = pT
                        for q4 in range(4):
                            c.tr(pv[:, q4 * 128:(q4 + 1) * 128], src[:, q4 * 128:(q4 + 1) * 128], identb[:],
                                 r=[srck], w=["pT"])
                        c.cp("dve", dstV[:, j * 4:(j + 1) * 4, :], pv[:, 0:512].rearrange("p (a b) -> p a b", a=4),
                             r=["pT"], w=["Vb"])
                    if jj == 3:
                        for q16 in range(4):
                            pv = pT
                            for q4 in range(4):
                                r_ = q16 * 4 + q4
                                c.tr(pv[:, q4 * 128:(q4 + 1) * 128], VT3[:, r_ * 128:(r_ + 1) * 128], identb[:],
                                     r=["VT3"], w=["pT"])
                            t0 = (j // 4) * 16 + q16 * 4
                            c.cp("dve", Vb3[:, t0:t0 + 4, :], pv[:, 0:512].rearrange("p (a b) -> p a b", a=4),
                                 r=["pT"], w=["Vb"])
                    if main:
                        pi_ = cnt["pg"] % 2
                        cnt["pg"] += 1
                        proj(c, pg[pi_][:], Wb, 0, hTt, hb, w=[("pg", pi_)])
                        jm = j - NTH
                        qk_path(pg[pi_], ("pg", pi_), qnw[:, 0:1], QT[:, jm * 512:(jm + 1) * 512], hb)
                for n in range(NSPAN):
                    ng = n + 1
                    q0 = n * 2048
                    units = []
                    for br in (1, 2, 3):
                        for u in range(16):
                            if br == 1:
                                tcur = ng * 16 + u
                                kcur = KT1[:, tcur * 128:(tcur + 1) * 128]
                                kprev = KT1[:, (tcur - 1) * 128:tcur * 128]
                                vcur, vprev = Vb1[:, tcur, :], Vb1[:, tcur - 1, :]
                                qv = QT[:, q0 + u * 128:q0 + (u + 1) * 128]
                                av = acc[:, :, u * 128:(u + 1) * 128]
                                halo_prev = (n == 0 and u == 0)
                            elif br == 2:
                                s_, r_ = u // 4, u % 4
                                sg = ng * 4 + s_
                                tcur, tprev = sg * 4 + r_, (sg - 1) * 4 + r_
                                kcur = KT2[:, tcur * 128:(tcur + 1) * 128]
                                kprev = KT2[:, tprev * 128:(tprev + 1) * 128]
                                vcur, vprev = Vb2[:, tcur, :], Vb2[:, tprev, :]
                                b0 = q0 + s_ * 512 + r_
                                qv = QT[:, b0:b0 + 509:4]
                                a0 = s_ * 512 + r_
                                av = acc[:, :, a0:a0 + 509:4]
                                halo_prev = (n == 0 and s_ == 0)
                            else:
                                r_ = u
                                tcur, tprev = ng * 16 + r_, (ng - 1) * 16 + r_
                                kcur = KT3[:, tcur * 128:(tcur + 1) * 128]
                                kprev = KT3[:, tprev * 128:(tprev + 1) * 128]
                                vcur, vprev = Vb3[:, tcur, :], Vb3[:, tprev, :]
                                qv = QT[:, q0 + r_:q0 + r_ + 2033:16]
                                av = acc[:, :, r_:r_ + 2033:16]
                                halo_prev = (n == 0)
                            units.append((br, kprev, kcur, vprev, vcur, qv, av, halo_prev))

                    def stage1(un, ub):
                        br, kprev, kcur, vprev, vcur, qv, av, halo_prev = un
                        bk = ("bank", id(psS[ub]))
                        c.mm(psS[ub][:, 0:128], kprev, qv, r=["K2", "K3", "KQ"], w=[bk])
                        c.mm(psS[ub][:, 128:256], kcur, qv, r=["K2", "K3", "KQ"], w=[bk])
                        c.act(Pt[ub][:], psS[ub][:, 0:256], AF.Exp, scale=ISQ, bias=c_mshift, r=[bk], w=[("Pt", ub)])
                        mk = maskAh if halo_prev else maskA
                        c.tt("pool", Pm[ub][:], Pt[ub][:], mk[:], ALU.mult, r=[("Pt", ub)], w=[("Pm", ub)])

                    def stage2(un, ub):
                        br, kprev, kcur, vprev, vcur, qv, av, halo_prev = un
                        c.mm(psO[ub][:, 0:128], vprev, Pm[ub][:, 0:128], True, False, r=["Vb", ("Pm", ub)], w=[("psO", ub)])
                        c.mm(psO[ub][:, 0:128], vcur, Pm[ub][:, 128:256], False, True, r=["Vb", ("Pm", ub)], w=[("psO", ub)])
                        c.mm(psO[ub][:, 128:256], onesb[:], Pm[ub][:, 0:128], True, False, r=[("Pm", ub)], w=[("psO", ub)])
                        c.mm(psO[ub][:, 128:256], onesb[:], Pm[ub][:, 128:256], False, True, r=[("Pm", ub)], w=[("psO", ub)])
                        pso = psO[ub][:, 0:256].rearrange("p (a b) -> p a b", a=2)
                        if br == 1:
                            c.cp("dve", av, pso, r=[("psO", ub)], w=["acc"])
                        else:
                            c.tt("dve", av, pso, av, ALU.add, r=[("psO", ub), "acc"], w=["acc"])

                    stage1(units[0], 0)
                    for ui in range(len(units)):
                        if ui + 1 < len(units):
                            stage1(units[ui + 1], (ui + 1) % 2)
                        stage2(units[ui], ui % 2)
                    c.recip(rl[:], acc[:, 1, :], r=["acc"], w=["rl"])
                    c.tt("dve", mst[:], acc[:, 0, :], rl[:], ALU.mult, r=["acc", "rl"], w=["mst"])
                    c.dma(mixT_s[:, h, n * 2048:(n + 1) * 2048], mst[:], r=["mst"])

        phase(phb1)

        def phb2(c):
            NCH = TT // 64
            NCHH = TH // 64
            Wb = c.sb([128, NKC, 4, 128], BF16)
            W32 = [c.sb([128, NKC, 128], F32) for _ in range(2)]
            hTt = [c.sb([128, NKC, 512], BF16) for _ in range(2)]
            KeT = c.sb([128, TT], BF16)
            QeT = c.sb([128, TM], BF16)
            gT = c.sb([128, TM], BF16)
            oT = c.sb([128, TM], F32)
            Kem = c.sb([128, TT // 128, 128], BF16)
            Vm = c.sb([128, TT // 128, 128], BF16)
            eref = c.sb([128, NCH + 1], F32)
            elast = c.sb([128, NCH], F32)
            e2 = c.sb([128, NCH], F32)
            dl2 = c.sb([128, 8], F32)
            sg = c.sb([128, 512], F32)
            sgn = c.sb([128, 512], F32)
            ff = c.sb([128, 512], F32)
            lf = c.sb([128, 512], F32)
            bb = c.sb([128, 512], F32)
            dd = c.sb([128, 512], F32)
            eq = c.sb([128, 512], F32)
            ek = c.sb([128, 512], F32)
            qs = c.sb([128, 512], F32)
            Sst = c.sb([128, 128], F32)
            Sb = c.sb([128, 128], BF16)
            aT = [c.sb([128, 64], BF16) for _ in range(2)]
            sq = c.sb([128, 512], BF16)
            sd = c.sb([128, 512], F32)
            rs = c.sb([128, 512], F32)
            tt_ = c.sb([128, 512], F32)
            hst = [c.sb([128, 512], BF16) for _ in range(2)]
            pg = [c.ps() for _ in range(2)]
            px0 = c.ps()
            px = [px0, px0]
            pT = c.ps([128, 1024], BF16)
            psA = [c.ps() for _ in range(2)]
            psU = [c.ps() for _ in range(2)]
            cnt = {"pg": 0, "px": 0}
            CWc = 2048
            cf = [c.sb([128, CWc], F32) for _ in range(2)]
            cg = [c.sb([128, CWc], BF16) for _ in range(2)]
            jobs = []
            for src, dst, R, Cn in ((w_out, wo_s, D, D), (w_ff1, wf1_s, D, DFF), (w_ff2, wf2_s, DFF, D)):
                for r0 in range(0, R, 128):
                    for c0 in range(0, Cn, CWc):
                        jobs.append((src, dst, r0, c0, min(CWc, Cn - c0)))
            jstate = {"i": 0}
            per_tile = -(-len(jobs) // (NH * NT))

            def emit_casts(k):
                for _ in range(k):
                    i = jstate["i"]
                    if i >= len(jobs):
                        return
                    jstate["i"] += 1
                    src, dst, r0, c0, cw = jobs[i]
                    b = i % 2
                    c.dma(cf[b][:, 0:cw], src[r0:r0 + 128, c0:c0 + cw], w=[("cf", b)])
                    c.cp("pool", cg[b][:, 0:cw], cf[b][:, 0:cw], r=[("cf", b)], w=[("cg", b)])
                    c.dma(dst[r0:r0 + 128, c0:c0 + cw], cg[b][:, 0:cw], r=[("cg", b)])

            def nextpg():
                i = cnt["pg"] % 2
                cnt["pg"] += 1
                return i

            for h in range(NH):
                load_head_w(c, Wb, W32, h, (3, 4, 5, 6), "g")
                c.S.add("pool", lambda e: e.memset(eref[:, NCH:NCH + 1], 1.0), (), ["eref"])
                for j in range(NT):
                    hb = j % 2
                    main = j >= NTH
                    jm = j - NTH
                    c.dma(hTt[hb][:], hT_s[:, :, j * 512:(j + 1) * 512], w=[("hTt", hb)])
                    tk = slice(j * 512, (j + 1) * 512)
                    pi_ = nextpg()
                    proj(c, pg[pi_][:], Wb, 1, hTt, hb, w=[("pg", pi_)])
                    c.act(sg[:], pg[pi_][:], AF.Sigmoid, r=[("pg", pi_)], w=["sg"])
                    c.act(sgn[:], pg[pi_][:], AF.Sigmoid, scale=-1.0, r=[("pg", pi_)], w=["sgn"])
                    c.ts("dve", ff[:], sg[:], omlb[:, h:h + 1], lb[:, h:h + 1], ALU.mult, ALU.add, r=["sg"], w=["ff"])
                    c.act(lf[:], ff[:], AF.Ln, r=["ff"], w=["lf"])
                    c.S.add("dve", lambda e: e.tensor_tensor_scan(out=bb[:], data0=resetm[:], data1=lf[:], initial=0.0,
                                                                  op0=ALU.mult, op1=ALU.add), ["lf"], ["bb"])
                    bv = bb[:].rearrange("p (c j) -> p c j", j=64)
                    c.tt("dve", dd[:].rearrange("p (c j) -> p c j", j=64), bv, bv[:, :, 31:32].to_broadcast([128, 8, 64]),
                         ALU.subtract, r=["bb"], w=["dd"])
                    c.act(ek[:], dd[:], AF.Exp, scale=-1.0, r=["dd"], w=["ek"])
                    om = omlb if main else omlbv
                    c.stt(KeT[:, tk], sgn[:], om[:, h:h + 1], ek[:], ALU.mult, ALU.mult, r=["sgn", "ek"], w=["KeT"])
                    c.act(eref[:, j * 8:(j + 1) * 8], bv[:, :, 31], AF.Exp, r=["bb"], w=["eref"])
                    c.act(elast[:, j * 8:(j + 1) * 8], bv[:, :, 63], AF.Exp, r=["bb"], w=["elast"])
                    c.tt("dve", dl2[:], bv[:, :, 63], bv[:, :, 31], ALU.subtract, r=["bb"], w=["dl2"])
                    c.act(e2[:, j * 8:(j + 1) * 8], dl2[:], AF.Exp, r=["dl2"], w=["e2"])
                    pv = pT
                    for q4 in range(4):
                        c.tr(pv[:, q4 * 128:(q4 + 1) * 128], KeT[:, j * 512 + q4 * 128:j * 512 + (q4 + 1) * 128], identb[:],
                             r=["KeT"], w=["pT"])
                    c.cp("act", Kem[:, j * 4:(j + 1) * 4, :], pv[:, 0:512].rearrange("p (a b) -> p a b", a=4),
                         r=["pT"], w=["Kem"])
                    pi_ = nextpg()
                    for q4 in range(4):
                        for kc in range(NKC):
                            c.mm(pg[pi_][:, q4 * 128:(q4 + 1) * 128], hTt[hb][:, kc, q4 * 128:(q4 + 1) * 128],
                                 Wb[:, kc, 2, :], kc == 0, kc == NKC - 1, r=[("Wb", 2), ("hTt", hb)], w=[("pg", pi_)])
                    c.cp("act", Vm[:, j * 4:(j + 1) * 4, :], pg[pi_][:].rearrange("p (a b) -> p a b", a=4),
                         r=[("pg", pi_)], w=["Vm"])
                    if main:
                        c.act(eq[:], dd[:], AF.Exp, r=["dd"], w=["eq"])
                        pi_ = nextpg()
                        proj(c, pg[pi_][:], Wb, 0, hTt, hb, w=[("pg", pi_)])
                        c.act(qs[:], pg[pi_][:], AF.Silu, r=[("pg", pi_)], w=["qs"])
                        c.tt("dve", QeT[:, jm * 512:(jm + 1) * 512], qs[:], eq[:], ALU.mult, r=["qs", "eq"], w=["QeT"])
                        pi_ = nextpg()
                        proj(c, pg[pi_][:], Wb, 3, hTt, hb, w=[("pg", pi_)])
                        c.act(gT[:, jm * 512:(jm + 1) * 512], pg[pi_][:], AF.Silu, r=[("pg", pi_)], w=["gT"])
                    emit_casts(per_tile)
                c.S.add("pool", lambda e: e.memset(Sst[:], 0.0), (), ["S"])
                c.S.add("pool", lambda e: e.memset(Sb[:], 0.0), (), ["Sb"])
                def emit_U(ch):
                    t128, po, ub = ch // 2, 64 * (ch % 2), ch % 2
                    c.mm(psU[ub][:, 0:128], Kem[po:po + 64, t128, :], Vm[po:po + 64, t128, :],
                         r=["Kem", "Vm"], w=[("psU", ub)])

                def emit_a(ch):
                    t128, po, ub = ch // 2, 64 * (ch % 2), ch % 2
                    cm = ch - NCHH
                    c.mm(psA[ub][po:po + 64, 0:64], KeT[:, ch * 64:(ch + 1) * 64], QeT[:, cm * 64:(cm + 1) * 64],
                         r=["KeT", "QeT"], w=[("psA", ub)])
                    c.tt("dve", aT[ub][po:po + 64, :], psA[ub][po:po + 64, 0:64], maskH[po:po + 64, :], ALU.mult,
                         r=[("psA", ub)], w=[("aT", ub)])

                def emit_o(ch):
                    t128, po, ub = ch // 2, 64 * (ch % 2), ch % 2
                    cm = ch - NCHH
                    qcols = QeT[:, cm * 64:(cm + 1) * 64]
                    c.mm(psA[ub][:, 128:192], Vm[po:po + 64, t128, :], aT[ub][po:po + 64, :], True, False,
                         r=["Vm", ("aT", ub)], w=[("psA", ub)])
                    c.mm(psA[ub][:, 128:192], Sb[:], qcols, False, True, r=["Sb", "QeT"], w=[("psA", ub)])
                    c.cp("act", oT[:, cm * 64:(cm + 1) * 64], psA[ub][:, 128:192], r=[("psA", ub)], w=["oT"])

                emit_U(0)
                for ch in range(NCH):
                    ub = ch % 2
                    if ch + 1 < NCH:
                        emit_U(ch + 1)
                        if ch + 1 >= NCHH:
                            emit_a(ch + 1)
                    if ch >= NCHH:
                        emit_o(ch)
                    c.ts("dve", Sst[:], Sst[:], elast[:, ch:ch + 1], None, ALU.mult, r=["S", "elast"], w=["S"])
                    c.stt(Sst[:], psU[ub][:, 0:128], e2[:, ch:ch + 1], Sst[:], ALU.mult, ALU.add,
                          r=[("psU", ub), "S", "e2"], w=["S"])
                    if ch + 1 >= NCHH and ch + 1 < NCH:
                        c.ts("dve", Sb[:], Sst[:], eref[:, ch + 1:ch + 2], None, ALU.mult, r=["S", "eref"], w=["Sb"])
                for jm in range(NTM):
                    sb_ = jm % 2
                    cs = slice(jm * 512, (jm + 1) * 512)
                    c.act(sq[:], oT[:, cs], AF.Square, r=["oT"], w=["sq"])
                    pxi = cnt["px"] % 2
                    cnt["px"] += 1
                    c.mm(px[pxi][:], onesb[:], sq[:], r=["sq"], w=[("bank", id(px[pxi]))])
                    c.act(sd[:], px[pxi][:], AF.Sqrt, scale=1.0 / 128.0, bias=c_eps128, r=[("bank", id(px[pxi]))], w=["sd"])
                    c.recip(rs[:], sd[:], r=["sd"], w=["rs"])
                    c.stt(tt_[:], oT[:, cs], hgnw[:, 0:1], rs[:], ALU.mult, ALU.mult, r=["oT", "rs"], w=["tt"])
                    c.tt("dve", hst[sb_][:], tt_[:], gT[:, cs], ALU.mult, r=["tt", "gT"], w=[("hst", sb_)])
                    c.dma(mixT_s[:, NH + h, cs], hst[sb_][:], r=[("hst", sb_)])
            emit_casts(len(jobs))

        phase(phb2)

        def phc(c):
            NOG = NKC // 4
            HALF = NFF // 2
            PF = min(16, HALF)
            mixT = c.sb([128, NKC, 512], BF16)
            xT = c.sb([128, NKC, 512], F32)
            xtok = [c.sb([128, D], F32) for _ in range(2)]
            h2T = c.sb([128, NKC, 512], BF16)
            uT = c.sb([128, HALF, 512], BF16)
            pan = [c.sb([128, 16, 512], BF16) for _ in range(3)]
            sqa = c.sb([128, NKC, 512], BF16)
            sd = c.sb([128, 512], F32)
            rr = c.sb([128, 512], F32)
            tmp = [c.sb([128, 512], F32) for _ in range(2)]
            rl = [c.sb([128, 512], BF16) for _ in range(2)]
            pa = [c.ps() for _ in range(4)]
            pb = [c.ps() for _ in range(4)]
            cnt = {"pan": 0, "pa": 0, "t": 0}
            wo_v = wo_s.rearrange("(kc p) n -> p kc n", p=128)
            wf1_v = wf1_s.rearrange("(kc p) n -> p kc n", p=128)
            wf2_v = wf2_s.rearrange("(kc p) n -> p kc n", p=128)

            def nextpan():
                i = cnt["pan"] % 3
                cnt["pan"] += 1
                return i

            def nextpa():
                i = cnt["pa"] % 4
                cnt["pa"] += 1
                return i

            for i in range(NTM):
                tok0 = TH + i * 512
                c.dma(mixT[:], mixT_s[:, :, i * 512:(i + 1) * 512], w=["mixT"])
                for blk in range(4):
                    xb = blk % 2
                    c.dma(xtok[xb][:], x[tok0 + blk * 128:tok0 + (blk + 1) * 128, :], w=[("xtok", xb)])
                    for k4 in range(NOG):
                        pi_ = nextpa()
                        for q4 in range(4):
                            kc = k4 * 4 + q4
                            c.tr(pa[pi_][:, q4 * 128:(q4 + 1) * 128], xtok[xb][:, kc * 128:(kc + 1) * 128], identf[:],
                                 r=[("xtok", xb)], w=[("pa", pi_)])
                        c.cp("act" if k4 % 2 == 0 else "dve", xT[:, k4 * 4:(k4 + 1) * 4, blk * 128:(blk + 1) * 128],
                             pa[pi_][:].rearrange("p (a b) -> p a b", a=4), r=[("pa", pi_)], w=[("xT", k4 * 4 + q_) for q_ in range(4)])
                c.act(sqa[:, 0:NH, :], mixT[:, 0:NH, :], AF.Square, r=["mixT"], w=["sqa"])
                pi_ = nextpa()
                for hh in range(NH):
                    c.mm(pa[pi_][:], onesb[:], sqa[:, hh, :], hh == 0, hh == NH - 1, r=["sqa"], w=[("pa", pi_)])
                c.act(sd[:], pa[pi_][:], AF.Sqrt, scale=1.0 / W, bias=c_eps, r=[("pa", pi_)], w=["sd"])
                c.recip(rr[:], sd[:], r=["sd"], w=["rr"])
                for hh in range(NH):
                    c.stt(mixT[:, hh, :], mixT[:, hh, :], aonw[:, hh:hh + 1], rr[:], ALU.mult, ALU.mult,
                          r=["mixT", "rr"], w=["mixT"])
                for og in range(NOG):
                    pn = nextpan()
                    c.dma(pan[pn][:, 0:NKC, :], wo_v[:, :, og * 512:(og + 1) * 512], w=[("pan", pn)])
                    for oc4 in range(4):
                        oc = og * 4 + oc4
                        pi_ = nextpa()
                        for kc in range(NKC):
                            c.mm(pa[pi_][:], pan[pn][:, kc, oc4 * 128:(oc4 + 1) * 128], mixT[:, kc, :], kc == 0, kc == NKC - 1,
                                 r=[("pan", pn), "mixT"], w=[("pa", pi_)])
                        c.stt(xT[:, oc, :], pa[pi_][:], gate1[:, oc:oc + 1], xT[:, oc, :], ALU.mult, ALU.add,
                              r=[("pa", pi_), ("xT", oc)], w=[("xT", oc)])
                c.act(sqa[:], xT[:], AF.Square, r=[("xT", k_) for k_ in range(NKC)], w=["sqa"])
                pi_ = nextpa()
                for kc in range(NKC):
                    c.mm(pa[pi_][:], onesb[:], sqa[:, kc, :], kc == 0, kc == NKC - 1, r=["sqa"], w=[("pa", pi_)])
                c.act(sd[:], pa[pi_][:], AF.Sqrt, scale=1.0 / D, bias=c_eps, r=[("pa", pi_)], w=["sd"])
                c.recip(rr[:], sd[:], r=["sd"], w=["rr"])
                for kc in range(NKC):
                    tb = cnt["t"] % 2
                    cnt["t"] += 1
                    c.stt(tmp[tb][:], xT[:, kc, :], g2[:, kc:kc + 1], rr[:], ALU.mult, ALU.mult,
                          r=[("xT", kc), "rr"], w=[("tmp", tb)])
                    c.ts("dve", h2T[:, kc, :], tmp[tb][:], sh2[:, kc:kc + 1], None, ALU.add, r=[("tmp", tb)], w=["h2T"])
                for hf in range(2):
                    for fg in range(HALF // 4):
                        pn = nextpan()
                        col0 = (hf * HALF + fg * 4) * 128
                        c.dma(pan[pn][:, 0:NKC, :], wf1_v[:, :, col0:col0 + 512], w=[("pan", pn)])
                        for f4 in range(4):
                            fl = fg * 4 + f4
                            pi_ = nextpa()
                            for kc in range(NKC):
                                c.mm(pa[pi_][:], pan[pn][:, kc, f4 * 128:(f4 + 1) * 128], h2T[:, kc, :], kc == 0, kc == NKC - 1,
                                     r=[("pan", pn), "h2T"], w=[("pa", pi_)])
                            tb = cnt["t"] % 2
                            cnt["t"] += 1
                            c.act(rl[tb][:], pa[pi_][:], AF.Relu, r=[("pa", pi_)], w=[("rl", tb)])
                            c.tt("pool", uT[:, fl, :], rl[tb][:], rl[tb][:], ALU.mult, r=[("rl", tb)], w=[("uT", fl)])
                    for og in range(NOG):
                        for q in range(HALF // PF):
                            pn = nextpan()
                            k0 = hf * HALF + q * PF
                            c.dma(pan[pn][:, 0:PF, :], wf2_v[:, k0:k0 + PF, og * 512:(og + 1) * 512], w=[("pan", pn)])
                            for oc4 in range(4):
                                for f in range(PF):
                                    fl = q * PF + f
                                    c.mm(pb[oc4][:], pan[pn][:, f, oc4 * 128:(oc4 + 1) * 128], uT[:, fl, :],
                                         q == 0 and f == 0, q == HALF // PF - 1 and f == PF - 1,
                                         r=[("pan", pn), ("uT", fl)], w=[("pb", oc4)])
                        for oc4 in range(4):
                            oc = og * 4 + oc4
                            c.stt(xT[:, oc, :], pb[oc4][:], gate2[:, oc:oc + 1], xT[:, oc, :], ALU.mult, ALU.add,
                                  r=[("pb", oc4), ("xT", oc)], w=[("xT", oc)])
                for blk in range(4):
                    xb = blk % 2
                    for k4 in range(NOG):
                        pi_ = nextpa()
                        for q4 in range(4):
                            kc = k4 * 4 + q4
                            c.tr(pa[pi_][:, q4 * 128:(q4 + 1) * 128], xT[:, kc, blk * 128:(blk + 1) * 128], identf[:],
                                 r=[("xT", kc)], w=[("pa", pi_)])
                        c.cp("act" if k4 % 2 == 0 else "dve", xtok[xb][:, k4 * 512:(k4 + 1) * 512], pa[pi_][:],
                             r=[("pa", pi_)], w=[("xtok", xb)])
                    c.dma(out[i * 512 + blk * 128:i * 512 + (blk + 1) * 128, :], xtok[xb][:], r=[("xtok", xb)])

        phase(phc)
    return nc


def make_consts():
    bf = ml_dtypes.bfloat16
    p = np.arange(128)
    i = np.arange(128)
    mprev = (p[:, None] >= i[None, :]).astype(np.float32)
    mcur = (p[:, None] <= i[None, :]).astype(np.float32)
    maskA = np.concatenate([mprev, mcur], axis=1).astype(bf)
    t = np.arange(64)
    maskH = ((p[:, None] % 64) <= t[None, :]).astype(np.float32).astype(bf)
    rotT = np.zeros((128, 128), np.float32)
    for m in range(16):
        rotT[m + 16, m] = -1.0
    for m in range(16, 32):
        rotT[m - 16, m] = 1.0
    resetm = np.ones((128, 512), np.float32)
    resetm[:, ::64] = 0.0
    invf = np.zeros((128, 1), np.float32)
    half = 16
    fr = (np.float32(ROPE_THETA) ** (-(np.arange(half, dtype=np.float32) * np.float32(2.0)) / np.float32(32))).astype(np.float32)
    invf[0:16, 0] = fr
    invf[16:32, 0] = fr
    cst = np.zeros((128, 4), np.float32)
    cst[:, 0] = EPS
    cst[:, 1] = EPS * 128.0
    cst[:, 2] = -MSHIFT
    return {
        "identb": np.eye(128, dtype=np.float32).astype(bf), "identf": np.eye(128, dtype=np.float32),
        "onesb": np.ones((128, 128), np.float32).astype(bf), "rotT": rotT.astype(bf), "maskA": maskA, "maskH": maskH,
        "resetm": resetm, "invf": invf, "cst": cst,
    }


def pk(v, nkc):
    return np.ascontiguousarray(np.asarray(v, np.float32).reshape(nkc, 128).T)


def make_in_maps(inputs, D, NH, DFF, TM, TH, n_seg):
    NKC = D // 128
    x = np.asarray(inputs["x"], np.float32)
    B, S, _ = x.shape
    posi = np.asarray(inputs["positions"], np.int32)
    consts = make_consts()
    shared = {
        "n1w": pk(inputs["norm1_w"][0], NKC), "n2w": pk(inputs["norm2_w"][0], NKC),
        "bada": np.ascontiguousarray(np.asarray(inputs["b_ada"][0], np.float32).reshape(6 * NKC, 128).T),
        "w_ada": np.ascontiguousarray(np.asarray(inputs["w_ada"][0], np.float32)),
        "w_in": np.ascontiguousarray(np.asarray(inputs["w_in"][0], np.float32)),
        "qnw": pk(inputs["q_norm_w"][0], 1), "knw": pk(inputs["k_norm_w"][0], 1),
        "aonw": pk(inputs["attn_out_norm_w"][0], NH),
        "lbl": np.ascontiguousarray(np.asarray(inputs["hg_lb_logits"], np.float32).reshape(2 * NH, 128).T),
        "hgnw": pk(inputs["hg_norm_w"][0], 1),
        "w_out": np.ascontiguousarray(np.asarray(inputs["w_out"][0], np.float32)),
        "w_ff1": np.ascontiguousarray(np.asarray(inputs["w_ff1"][0], np.float32)),
        "w_ff2": np.ascontiguousarray(np.asarray(inputs["w_ff2"][0], np.float32)),
    }
    shared.update(consts)
    maps = []
    for b in range(B):
        for j in range(n_seg):
            s0 = j * TM
            xin = np.zeros((TH + TM, D), np.float32)
            pin = np.zeros((1, TH + TM), np.int32)
            if j > 0:
                xin[:] = x[b, s0 - TH:s0 + TM]
                pin[0, :] = posi[b, s0 - TH:s0 + TM]
            else:
                xin[TH:] = x[b, 0:TM]
                pin[0, TH:] = posi[b, 0:TM]
            m = dict(shared)
            m["x"] = xin
            m["pos"] = pin
            m["cT"] = pk(inputs["c"][b], NKC)
            m["valid"] = np.full((128, 1), 1.0 if j > 0 else 0.0, np.float32)
            maps.append(m)
    return maps


_NC_CACHE = {}


def kernel(**inputs):
    D, NH, DFF, TM, TH = 2048, 8, 8192, 4096, 2048
    x = np.asarray(inputs["x"])
    B, S, _ = x.shape
    n_seg = S // TM
    key = (D, NH, DFF, TM, TH)
    if key not in _NC_CACHE:
        _NC_CACHE[key] = build_program(*key)
    nc = _NC_CACHE[key]
    in_maps = make_in_maps(inputs, D, NH, DFF, TM, TH, n_seg)
    res = run_bass_kernel_spmd(nc, in_maps, core_ids=list(range(len(in_maps))))
    outp = np.empty((B, S, D), np.float32)
    k = 0
    for b in range(B):
        for j in range(n_seg):
            outp[b, j * TM:(j + 1) * TM] = np.asarray(res.results[k]["out"], np.float32)
            k += 1
    return outp
```

```python
import math
from contextlib import ExitStack
import numpy as np
import ml_dtypes
import concourse.bass as bass
import concourse.mybir as mybir
from concourse.bass_utils import run_bass_kernel_spmd

F32 = mybir.dt.float32
BF16 = mybir.dt.bfloat16
I32 = mybir.dt.int32
AF = mybir.ActivationFunctionType
ALU = mybir.AluOpType

EPS = 1e-6
MSHIFT = 8.0
ROPE_THETA = 500000.0
TWO_PI = 2.0 * math.pi
CW1 = 6.28125
CW2 = TWO_PI - CW1


class Op:
    __slots__ = ("eng", "fn", "deps", "sig", "sem", "val", "idx")

    def __init__(self, eng, fn):
        self.eng = eng
        self.fn = fn
        self.deps = []
        self.sig = False
        self.sem = None
        self.val = 0
        self.idx = 0


class Sched:
    G = None
    ENGS = ("pe", "act", "dve", "pool", "sp")
    NS_DMA = 24
    ROT = 30000

    def __init__(self, nc):
        self.nc = nc
        self.ops = {e: [] for e in self.ENGS}
        self.last_writer = {}
        self.readers = {}

    def add(self, eng, fn, reads=(), writes=()):
        op = Op(eng, fn)
        ds = {}
        for r in reads:
            w = self.last_writer.get(r)
            if w is not None:
                ds[id(w)] = w
        for w in writes:
            lw = self.last_writer.get(w)
            if lw is not None:
                ds[id(lw)] = lw
            for rd in self.readers.get(w, ()):
                ds[id(rd)] = rd
            self.readers[w] = []
            self.last_writer[w] = op
        for r in reads:
            self.readers.setdefault(r, []).append(op)
        for d in ds.values():
            if d is op:
                continue
            if d.eng == "pe" and eng == "pe":
                continue
            op.deps.append(d)
            d.sig = True
        if eng == "sp":
            op.sig = True
        self.ops[eng].append(op)
        return op

    def emit(self):
        nc = self.nc
        G = Sched.G
        with ExitStack() as st:
            for e in ("pe", "act", "dve", "pool"):
                sl = G["sems"][e]
                c = G["cnt"][e]
                for o in self.ops[e]:
                    if o.sig:
                        o.sem = sl[c // self.ROT]
                        o.val = c % self.ROT + 1
                        c += 1
                G["cnt"][e] = c
            ring = G["ring"]
            NS = self.NS_DMA
            base = G["dma"]
            for i, o in enumerate(self.ops["sp"]):
                gi = base + i
                o.sem = ring[gi % NS]
                o.val = 16 * (gi // NS + 1)
                o.idx = gi
            G["dma"] = base + len(self.ops["sp"])
            ops = self.ops

            def stream(ename, e):
                waited = {}

                def wait(sem, val):
                    k = id(sem)
                    if waited.get(k, 0) >= val:
                        return
                    waited[k] = val
                    e.wait_ge(sem, val)

                for o in ops[ename]:
                    for d in o.deps:
                        wait(d.sem, d.val)
                    if ename == "sp" and o.idx >= NS:
                        wait(o.sem, o.val - 16)
                    ins = o.fn(e)
                    if o.sig:
                        ins.then_inc(o.sem, 16 if ename == "sp" else 1)
                if ename == "sp":
                    n = len(ops["sp"])
                    for i in range(max(0, n - NS), n):
                        o = ops["sp"][i]
                        wait(o.sem, o.val)

            blk = st.enter_context(nc.Block())

            @blk.tensor
            def _(e):
                stream("pe", e)

            @blk.scalar
            def _(e):
                stream("act", e)

            @blk.vector
            def _(e):
                stream("dve", e)

            @blk.gpsimd
            def _(e):
                stream("pool", e)

            @blk.sync
            def _(e):
                stream("sp", e)


class Ctx:
    def __init__(self, nc, st):
        self.nc = nc
        self.st = st
        self.S = Sched(nc)
        self.n = 0

    CNT = [0]

    def sb(self, shape, dt, name=None):
        Ctx.CNT[0] += 1
        return self.st.enter_context(self.nc.sbuf_tensor(f"t{Ctx.CNT[0]}", list(shape), dt))

    def ps(self, shape=(128, 512), dt=F32, name=None):
        Ctx.CNT[0] += 1
        return self.st.enter_context(self.nc.psum_tensor(f"p{Ctx.CNT[0]}", list(shape), dt))

    def dma(self, out, in_, r=(), w=()):
        return self.S.add("sp", lambda e: e.dma_start(out=out, in_=in_), r, w)

    def mm(self, out, lhsT, rhs, start=True, stop=True, r=(), w=()):
        return self.S.add("pe", lambda e: e.matmul(out, lhsT=lhsT, rhs=rhs, start=start, stop=stop,
                                                    skip_group_check=True), r, w)

    def tr(self, out, in_, ident, r=(), w=()):
        return self.S.add("pe", lambda e: e.transpose(out, in_, ident), r, w)

    def act(self, out, in_, func, scale=1.0, bias=None, accum=None, r=(), w=()):
        def f(e):
            kw = {}
            if bias is not None:
                kw["bias"] = bias
            if accum is not None:
                kw["accum_out"] = accum
            return e.activation(out=out, in_=in_, func=func, scale=scale, **kw)
        return self.S.add("act", f, r, w)

    def tt(self, eng, out, in0, in1, op, r=(), w=()):
        return self.S.add(eng, lambda e: e.tensor_tensor(out=out, in0=in0, in1=in1, op=op), r, w)

    def ts(self, eng, out, in0, s1, s2, op0, op1=None, r=(), w=()):
        if op1 is None:
            return self.S.add(eng, lambda e: e.tensor_scalar(out=out, in0=in0, scalar1=s1, scalar2=None, op0=op0), r, w)
        return self.S.add(eng, lambda e: e.tensor_scalar(out=out, in0=in0, scalar1=s1, scalar2=s2, op0=op0, op1=op1), r, w)

    def stt(self, out, in0, scalar, in1, op0, op1, r=(), w=()):
        return self.S.add("dve", lambda e: e.scalar_tensor_tensor(out=out, in0=in0, scalar=scalar, in1=in1,
                                                                   op0=op0, op1=op1), r, w)

    def cp(self, eng, out, in_, r=(), w=()):
        if eng == "act":
            return self.act(out, in_, AF.Copy, r=r, w=w)
        return self.S.add(eng, lambda e: e.tensor_copy(out=out, in_=in_), r, w)

    def recip(self, out, in_, r=(), w=()):
        return self.S.add("dve", lambda e: e.reciprocal(out=out, in_=in_), r, w)


def build_program(D, NH, DFF, TM, TH, only=None):
    NKC = D // 128
    W = 128 * NH
    assert 2 * W == D
    TT = TM + TH
    NFF = DFF // 128
    NT = TT // 512
    NTH = TH // 512
    NTM = TM // 512
    NSPAN = TM // 2048
    assert TH == 2048 and TM % 2048 == 0
    ISQ = 1.0 / math.sqrt(128.0)

    nc = bass.Bass("TRN2", target_bir_lowering=False)

    def din(name, shape, dt=F32):
        return nc.dram_tensor(name, list(shape), dt, kind="ExternalInput").ap()

    x = din("x", [TT, D])
    pos = din("pos", [1, TT], I32)
    cT = din("cT", [128, NKC])
    valid_d = din("valid", [128, 1])
    n1w_d = din("n1w", [128, NKC])
    n2w_d = din("n2w", [128, NKC])
    bada_d = din("bada", [128, 6 * NKC])
    w_ada = din("w_ada", [D, 6 * D])
    w_in = din("w_in", [D, 7 * W])
    qnw_d = din("qnw", [128, 1])
    knw_d = din("knw", [128, 1])
    aonw_d = din("aonw", [128, NH])
    lbl_d = din("lbl", [128, 2 * NH])
    hgnw_d = din("hgnw", [128, 1])
    w_out = din("w_out", [D, D])
    w_ff1 = din("w_ff1", [D, DFF])
    w_ff2 = din("w_ff2", [DFF, D])
    identb_d = din("identb", [128, 128], BF16)
    identf_d = din("identf", [128, 128])
    onesb_d = din("onesb", [128, 128], BF16)
    rotT_d = din("rotT", [128, 128], BF16)
    maskA_d = din("maskA", [128, 256], BF16)
    maskH_d = din("maskH", [128, 64], BF16)
    resetm_d = din("resetm", [128, 512])
    invf_d = din("invf", [128, 1])
    cst_d = din("cst", [128, 4])
    out = nc.dram_tensor("out", [TM, D], F32, kind="ExternalOutput").ap()

    hT_s = nc.dram_tensor("hT_s", [TT // 512, 128, NKC, 512], BF16).ap()
    mixT_s = nc.dram_tensor("mixT_s", [128, NKC, TM], BF16).ap()
    cs_s = nc.dram_tensor("cs_s", [2, 128, TT], F32).ap()
    PF_ = min(16, (DFF // 128) // 2)
    wo_s = nc.dram_tensor("wo_s", [D // 512, 128, NKC, 512], BF16).ap()
    wf1_s = nc.dram_tensor("wf1_s", [DFF // 512, 128, NKC, 512], BF16).ap()
    wf2_s = nc.dram_tensor("wf2_s", [D // 512, DFF // (128 * PF_), 128, PF_, 512], BF16).ap()

    with ExitStack() as gst:
        Sched.G = {"sems": {e: [gst.enter_context(nc.semaphore(f"s_{e}{i}")) for i in range(5)]
                            for e in ("pe", "act", "dve", "pool")},
                   "cnt": {e: 0 for e in ("pe", "act", "dve", "pool")},
                   "ring": [gst.enter_context(nc.semaphore(f"s_dma{i}")) for i in range(Sched.NS_DMA)],
                   "dma": 0}

        def gsb(shape, dt, name):
            return gst.enter_context(nc.sbuf_tensor(name + "_sb", list(shape), dt))

        modv = gsb([128, 6 * NKC], F32, "modv")
        g1 = gsb([128, NKC], F32, "g1")
        g2 = gsb([128, NKC], F32, "g2")
        lb = gsb([128, NH], F32, "lb")
        omlb = gsb([128, NH], F32, "omlb")
        omlbv = gsb([128, NH], F32, "omlbv")
        qnw = gsb([128, 1], F32, "qnw")
        knw = gsb([128, 1], F32, "knw")
        aonw = gsb([128, NH], F32, "aonw")
        hgnw = gsb([128, 1], F32, "hgnw")
        valid = gsb([128, 1], F32, "validt")
        invf = gsb([128, 1], F32, "invft")
        cst = gsb([128, 4], F32, "cstt")
        identb = gsb([128, 128], BF16, "identbt")
        identf = gsb([128, 128], F32, "identft")
        onesb = gsb([128, 128], BF16, "onesbt")
        rotT = gsb([128, 128], BF16, "rotTt")
        maskA = gsb([128, 256], BF16, "maskAt")
        maskAh = gsb([128, 256], BF16, "maskAht")
        maskH = gsb([128, 64], BF16, "maskHt")
        resetm = gsb([128, 512], F32, "resetmt")
        sh1 = modv[:, 0 * NKC:1 * NKC]
        sc1 = modv[:, 1 * NKC:2 * NKC]
        gate1 = modv[:, 2 * NKC:3 * NKC]
        sh2 = modv[:, 3 * NKC:4 * NKC]
        sc2 = modv[:, 4 * NKC:5 * NKC]
        gate2 = modv[:, 5 * NKC:6 * NKC]
        c_eps = cst[:, 0:1]
        c_eps128 = cst[:, 1:2]
        c_mshift = cst[:, 2:3]

        pcount = [0]

        def phase(fn):
            pcount[0] += 1
            if only is not None and pcount[0] not in only:
                return
            with ExitStack() as st:
                c = Ctx(nc, st)
                fn(c)
                c.S.emit()
            pass

        def ph0(c):
            for t, d in ((valid, valid_d), (invf, invf_d), (cst, cst_d), (identb, identb_d), (identf, identf_d),
                         (onesb, onesb_d), (rotT, rotT_d), (maskA, maskA_d), (maskH, maskH_d), (resetm, resetm_d),
                         (qnw, qnw_d), (knw, knw_d), (aonw, aonw_d), (hgnw, hgnw_d)):
                c.dma(t[:], d, w=[("g", id(t))])
            n1w = c.sb([128, NKC], F32)
            n2w = c.sb([128, NKC], F32)
            bada = c.sb([128, 6 * NKC], F32)
            lbl = c.sb([128, 2 * NH], F32)
            cTt = c.sb([128, NKC], F32)
            c.dma(n1w[:], n1w_d, w=["n1w"])
            c.dma(n2w[:], n2w_d, w=["n2w"])
            c.dma(bada[:], bada_d, w=["bada"])
            c.dma(lbl[:], lbl_d, w=["lbl"])
            c.dma(cTt[:], cT, w=["cT"])
            c.cp("dve", maskAh[:, 128:256], maskA[:, 128:256], r=[("g", id(maskA))], w=["mAh1"])
            c.ts("dve", maskAh[:, 0:128], maskA[:, 0:128], valid[:, 0:1], None, ALU.mult,
                 r=[("g", id(maskA)), ("g", id(valid))], w=["mAh0"])
            dl = c.sb([128, NH], F32)
            c.tt("dve", dl[:], lbl[:, 0:NH], lbl[:, NH:2 * NH], ALU.subtract, r=["lbl"], w=["dl"])
            c.act(lb[:], dl[:], AF.Sigmoid, r=["dl"], w=["lb"])
            c.ts("dve", omlb[:], lb[:], -1.0, 1.0, ALU.mult, ALU.add, r=["lb"], w=["omlb"])
            c.ts("dve", omlbv[:], omlb[:], valid[:, 0:1], None, ALU.mult, r=["omlb", ("g", id(valid))], w=["omlbv"])
            scs = c.sb([128, NKC], F32)
            c.act(scs[:], cTt[:], AF.Silu, r=["cT"], w=["scs"])
            zer = c.sb([128, 128], F32)
            c.S.add("pool", lambda e: e.memset(zer[:], 0.0), (), ["zer"])
            pm = c.ps([128, 512], F32)
            NJ = 6 * NKC
            c.mm(pm[:, 0:NJ], zer[:, 0:128], zer[:, 0:NJ], True, False, r=["zer"], w=["pm"])
            CH = min(6 * D, 6144)
            NCH = (6 * D) // CH
            wa = [c.sb([128, CH], F32) for _ in range(2)]
            it = 0
            for kc in range(NKC):
                for ch in range(NCH):
                    b = it % 2
                    it += 1
                    c.dma(wa[b][:], w_ada[kc * 128:(kc + 1) * 128, ch * CH:(ch + 1) * CH], w=[("wa", b)])
                    for jj in range(CH // 128):
                        j = ch * (CH // 128) + jj
                        last = (kc == NKC - 1)
                        c.mm(pm[:, j:j + 1], wa[b][:, jj * 128:(jj + 1) * 128], scs[:, kc:kc + 1], False, last,
                             r=[("wa", b), "scs"], w=["pm"])
            c.tt("dve", modv[:], pm[:, 0:NJ], bada[:], ALU.add, r=["pm", "bada"], w=["modv"])
            c.stt(g1[:], sc1, 1.0, n1w[:], ALU.add, ALU.mult, r=["modv", "n1w"], w=["g1"])
            c.stt(g2[:], sc2, 1.0, n2w[:], ALU.add, ALU.mult, r=["modv", "n2w"], w=["g2"])

        phase(ph0)

        def php(c):
            CW = 4096
            f = [c.sb([128, CW], F32) for _ in range(3)]
            g = [c.sb([128, CW], BF16) for _ in range(3)]
            engs = ("act", "dve", "pool")
            it = 0
            for src, dst, R, Cn in ((w_out, wo_s, D, D), (w_ff1, wf1_s, D, DFF), (w_ff2, wf2_s, DFF, D)):
                for r0 in range(0, R, 128):
                    for c0 in range(0, Cn, CW):
                        cw = min(CW, Cn - c0)
                        b = it % 3
                        c.dma(f[b][:, 0:cw], src[r0:r0 + 128, c0:c0 + cw], w=[("f", b)])
                        c.cp(engs[b], g[b][:, 0:cw], f[b][:, 0:cw], r=[("f", b)], w=[("g", b)])
                        c.dma(dst[r0:r0 + 128, c0:c0 + cw], g[b][:, 0:cw], r=[("g", b)])
                        it += 1


        def pha(c):
            xa = [c.sb([128, D], F32) for _ in range(2)]
            junk = c.sb([128, D], F32)
            xs = [c.sb([128, D], BF16) for _ in range(2)]
            sm = [c.sb([128, 4], F32) for _ in range(2)]
            hst = [c.sb([128, NKC, 512], BF16) for _ in range(2)]
            tp = [c.ps([128, 1024], BF16) for _ in range(2 * ((NKC + 7) // 8))]
            NB = (NKC + 7) // 8
            for i in range(TT // 128):
                b = i % 2
                gI = i // 4
                slot = i % 4
                hb = gI % 2
                c.dma(xa[b][:], x[i * 128:(i + 1) * 128, :], w=[("xa", b)])
                c.act(junk[:], xa[b][:], AF.Square, r=[("xa", b)], w=["junk"])
                c.S.add("dve", lambda e, o_=sm[b][:, 0:1], i_=junk[:]: e.reduce_sum(out=o_, in_=i_, axis=mybir.AxisListType.X),
                        ["junk"], [("sm0", b)])
                c.act(sm[b][:, 1:2], sm[b][:, 0:1], AF.Sqrt, scale=1.0 / D, bias=c_eps, r=[("sm0", b)], w=[("sm1", b)])
                c.recip(sm[b][:, 2:3], sm[b][:, 1:2], r=[("sm1", b)], w=[("sm2", b)])
                c.ts("dve", xs[b][:], xa[b][:], sm[b][:, 2:3], None, ALU.mult, r=[("xa", b), ("sm2", b)], w=[("xs", b)])
                for kc in range(NKC):
                    bank = tp[b * NB + kc // 8]
                    tv = bank
                    k8 = kc % 8
                    c.tr(tv[:, k8 * 128:(k8 + 1) * 128], xs[b][:, kc * 128:(kc + 1) * 128], identb[:],
                         r=[("xs", b)], w=[("tp", b, kc // 8)])
                for kc in range(NKC):
                    bank = tp[b * NB + kc // 8]
                    tv = bank
                    k8 = kc % 8
                    dst = hst[hb][:, kc, slot * 128:(slot + 1) * 128]
                    if True:
                        c.ts("dve", dst, tv[:, k8 * 128:(k8 + 1) * 128], g1[:, kc:kc + 1], sh1[:, kc:kc + 1],
                             ALU.mult, ALU.add, r=[("tp", b, kc // 8)], w=[("hst", hb, slot, kc)])
                if slot == 3:
                    c.dma(hT_s[gI], hst[hb][:],
                          r=[("hst", hb, s_, k_) for s_ in range(4) for k_ in range(NKC)])

        phase(pha)

        def phr(c):
            PI_S = 3.1415925
            pis = [c.sb([128, 512], I32) for _ in range(2)]
            tmps = [[c.sb([128, 512], F32) for _ in range(5)] for _ in range(2)]
            kis = [c.sb([128, 512], I32) for _ in range(2)]
            for j in range(NT):
                b = j % 2
                pi_ = pis[b]
                ang, u, kf, rr, m = tmps[b]
                ki = kis[b]
                c.dma(pi_[:], pos[:, j * 512:(j + 1) * 512].partition_broadcast(128), w=[("pi", b)])
                c.cp("dve", ang[:], pi_[:], r=[("pi", b)], w=[("ang", b)])
                c.ts("dve", ang[:], ang[:], invf[:, 0:1], None, ALU.mult, r=[("ang", b)], w=[("ang", b)])
                for which, shift in ((0, math.pi / 2), (1, 0.0)):
                    c.ts("dve", u[:], ang[:], 1.0 / TWO_PI, shift / TWO_PI + 0.5, ALU.mult, ALU.add,
                         r=[("ang", b)], w=[("u", b)])
                    c.cp("dve", ki[:], u[:], r=[("u", b)], w=[("ki", b)])
                    c.cp("dve", kf[:], ki[:], r=[("ki", b)], w=[("kf", b)])
                    c.ts("dve", rr[:], ang[:], shift, None, ALU.add, r=[("ang", b)], w=[("rr", b)])
                    c.stt(rr[:], kf[:], -CW1, rr[:], ALU.mult, ALU.add, r=[("kf", b), ("rr", b)], w=[("rr", b)])
                    c.stt(rr[:], kf[:], -CW2, rr[:], ALU.mult, ALU.add, r=[("kf", b), ("rr", b)], w=[("rr", b)])
                    c.ts("dve", m[:], rr[:], math.pi, -TWO_PI, ALU.is_gt, ALU.mult, r=[("rr", b)], w=[("m", b)])
                    c.tt("dve", rr[:], rr[:], m[:], ALU.add, r=[("rr", b), ("m", b)], w=[("rr", b)])
                    c.ts("dve", m[:], rr[:], -math.pi, TWO_PI, ALU.is_lt, ALU.mult, r=[("rr", b)], w=[("m", b)])
                    c.tt("dve", rr[:], rr[:], m[:], ALU.add, r=[("rr", b), ("m", b)], w=[("rr", b)])
                    c.ts("dve", rr[:], rr[:], PI_S, -PI_S, ALU.min, ALU.max, r=[("rr", b)], w=[("rr", b)])
                    c.act(u[:], rr[:], AF.Sin, r=[("rr", b)], w=[("u", b)])
                    c.dma(cs_s[which, :, j * 512:(j + 1) * 512], u[:], r=[("u", b)])
        phase(phr)

        w_in_v = w_in.rearrange("(kc p) n -> p kc n", p=128)

        def load_head_w(c, Wb, W32, h, groups, tag):
            engs = ("act", "dve", "pool")
            for gi_, g in enumerate(groups):
                b = gi_ % 2
                rk = ("W32", id(W32[b]))
                c.dma(W32[b][:], w_in_v[:, :, g * W + h * 128:g * W + (h + 1) * 128], w=[rk])
                c.cp(engs[gi_ % 3], Wb[:, :, gi_, :], W32[b][:], r=[rk], w=[("Wb", gi_)])

        def proj(c, pg, Wb, gi_, hTt, hb, cols=slice(0, 512), w=()):
            for kc in range(NKC):
                c.mm(pg, Wb[:, kc, gi_, :], hTt[hb][:, kc, cols], kc == 0, kc == NKC - 1,
                     r=[("Wb", gi_), ("hTt", hb)], w=list(w))

        def phb1(c):
            Wb = c.sb([128, NKC, 3, 128], BF16)
            W32_ = c.sb([128, NKC, 128], F32)
            W32 = [W32_, W32_]
            hTt = [c.sb([128, NKC, 512], BF16) for _ in range(2)]
            Ct = [c.sb([128, 512], F32) for _ in range(2)]
            St = [c.sb([128, 512], F32) for _ in range(2)]
            KT1 = c.sb([128, TT], BF16)
            KT2 = c.sb([128, TT], BF16)
            KT3 = c.sb([128, TT], BF16)
            QT = c.sb([128, TM], BF16)
            Vb1 = c.sb([128, TT // 128, 128], BF16)
            Vb2 = c.sb([128, TT // 128, 128], BF16)
            Vb3 = c.sb([128, TT // 128, 128], BF16)
            VTs = [c.sb([128, 512], BF16) for _ in range(2)]
            VT2 = [c.sb([128, 512], BF16) for _ in range(2)]
            VT3 = c.sb([128, 2048], BF16)
            acc = c.sb([128, 2, 2048], F32)
            rl = c.sb([128, 2048], F32)
            mst = c.sb([128, 2048], BF16)
            raw = c.sb([128, 512], F32)
            sq = c.sb([128, 512], BF16)
            sd = c.sb([128, 512], F32)
            rs = c.sb([128, 512], F32)
            qn = c.sb([128, 512], F32)
            qnb = c.sb([128, 512], BF16)
            t1 = c.sb([128, 512], F32)
            t2 = c.sb([128, 512], F32)
            Pt = [c.sb([128, 256], BF16) for _ in range(2)]
            Pm = [c.sb([128, 256], BF16) for _ in range(2)]
            pg = [c.ps() for _ in range(2)]
            px = [c.ps() for _ in range(2)]
            pT = c.ps([128, 1024], BF16)
            psS1 = c.ps()
            psS = [psS1, px[1]]
            psO = [c.ps() for _ in range(2)]
            cnt = {"pg": 0, "px": 0, "u": 0}

            def qk_path(pgt, pgk, wcol, dst, cb):
                c.cp("act", raw[:], pgt[:], r=[pgk], w=["raw"])
                c.act(sq[:], pgt[:], AF.Square, r=[pgk], w=["sq"])
                pxi = cnt["px"] % 2
                cnt["px"] += 1
                c.mm(px[pxi][:], onesb[:], sq[:], r=["sq"], w=[("bank", id(px[pxi]))])
                c.act(sd[:], px[pxi][:], AF.Sqrt, scale=1.0 / 128.0, bias=c_eps, r=[("bank", id(px[pxi]))], w=["sd"])
                c.recip(rs[:], sd[:], r=["sd"], w=["rs"])
                c.stt(qn[:], raw[:], wcol, rs[:], ALU.mult, ALU.mult, r=["raw", "rs"], w=["qn"])
                c.cp("act", qnb[:], qn[:], r=["qn"], w=["qnb"])
                pxi = cnt["px"] % 2
                cnt["px"] += 1
                c.mm(px[pxi][:], rotT[:], qnb[:], r=["qnb"], w=[("bank", id(px[pxi]))])
                c.tt("dve", t1[:], qn[:], Ct[cb][:], ALU.mult, r=["qn", ("Ct", cb)], w=["t1"])
                c.tt("dve", t2[:], px[pxi][:], St[cb][:], ALU.mult, r=[("bank", id(px[pxi])), ("St", cb)], w=["t2"])
                c.tt("dve", dst, t1[:], t2[:], ALU.add, r=["t1", "t2"], w=["KQ"])

            for h in range(NH):
                load_head_w(c, Wb, W32, h, (0, 1, 2), "a")
                for j in range(NT):
                    hb = j % 2
                    main = j >= NTH
                    c.dma(hTt[hb][:], hT_s[j], w=[("hTt", hb)])
                    c.dma(Ct[hb][:], cs_s[0, :, j * 512:(j + 1) * 512], w=[("Ct", hb)])
                    c.dma(St[hb][:], cs_s[1, :, j * 512:(j + 1) * 512], w=[("St", hb)])
                    tk = slice(j * 512, (j + 1) * 512)
                    pi_ = cnt["pg"] % 2
                    cnt["pg"] += 1
                    proj(c, pg[pi_][:], Wb, 1, hTt, hb, w=[("pg", pi_)])
                    qk_path(pg[pi_], ("pg", pi_), knw[:, 0:1], KT1[:, tk], hb)
                    c.cp("pool", KT2[:, tk].rearrange("p (r m) -> p r m", r=4),
                         KT1[:, tk].rearrange("p (m r) -> p r m", r=4), r=["KQ"], w=["K2"])
                    sp3 = (j // 4) * 2048
                    jj = j % 4
                    c.cp("pool", KT3[:, sp3:sp3 + 2048].rearrange("p (r m) -> p r m", r=16)[:, :, jj * 32:(jj + 1) * 32],
                         KT1[:, tk].rearrange("p (m r) -> p r m", r=16), r=["KQ"], w=["K3"])
                    pi_ = cnt["pg"] % 2
                    cnt["pg"] += 1
                    proj(c, pg[pi_][:], Wb, 2, hTt, hb, w=[("pg", pi_)])
                    c.cp("act", VTs[hb][:], pg[pi_][:], r=[("pg", pi_)], w=[("VTs", hb)])
                    c.cp("pool", VT2[hb][:].rearrange("p (r m) -> p r m", r=4),
                         VTs[hb][:].rearrange("p (m r) -> p r m", r=4), r=[("VTs", hb)], w=[("VT2", hb)])
                    c.cp("pool", VT3[:].rearrange("p (r m) -> p r m", r=16)[:, :, jj * 32:(jj + 1) * 32],
                         VTs[hb][:].rearrange("p (m r) -> p r m", r=16), r=[("VTs", hb)], w=["VT3"])
                    for src, srck, dstV in ((VTs[hb], ("VTs", hb), Vb1), (VT2[hb], ("VT2", hb), Vb2)):
                        pv = pT
                        for q4 in range(4):
                            c.tr(pv[:, q4 * 128:(q4 + 1) * 128], src[:, q4 * 128:(q4 + 1) * 128], identb[:],
                                 r=[srck], w=["pT"])
                        c.cp("dve", dstV[:, j * 4:(j + 1) * 4, :], pv[:, 0:512].rearrange("p (a b) -> p a b", a=4),
                             r=["pT"], w=["Vb"])
                    if jj == 3:
                        for q16 in range(4):
                            pv = pT
                            for q4 in range(4):
                                r_ = q16 * 4 + q4
                                c.tr(pv[:, q4 * 128:(q4 + 1) * 128], VT3[:, r_ * 128:(r_ + 1) * 128], identb[:],
                                     r=["VT3"], w=["pT"])
                            t0 = (j // 4) * 16 + q16 * 4
                            c.cp("dve", Vb3[:, t0:t0 + 4, :], pv[:, 0:512].rearrange("p (a b) -> p a b", a=4),
                                 r=["pT"], w=["Vb"])
                    if main:
                        pi_ = cnt["pg"] % 2
                        cnt["pg"] += 1
                        proj(c, pg[pi_][:], Wb, 0, hTt, hb, w=[("pg", pi_)])
                        jm = j - NTH
                        qk_path(pg[pi_], ("pg", pi_), qnw[:, 0:1], QT[:, jm * 512:(jm + 1) * 512], hb)
                for n in range(NSPAN):
                    ng = n + 1
                    q0 = n * 2048
                    units = []
                    for br in (1, 2, 3):
                        for u in range(16):
                            if br == 1:
                                tcur = ng * 16 + u
                                kcur = KT1[:, tcur * 128:(tcur + 1) * 128]
                                kprev = KT1[:, (tcur - 1) * 128:tcur * 128]
                                vcur, vprev = Vb1[:, tcur, :], Vb1[:, tcur - 1, :]
                                qv = QT[:, q0 + u * 128:q0 + (u + 1) * 128]
                                av = acc[:, :, u * 128:(u + 1) * 128]
                                halo_prev = (n == 0 and u == 0)
                            elif br == 2:
                                s_, r_ = u // 4, u % 4
                                sg = ng * 4 + s_
                                tcur, tprev = sg * 4 + r_, (sg - 1) * 4 + r_
                                kcur = KT2[:, tcur * 128:(tcur + 1) * 128]
                                kprev = KT2[:, tprev * 128:(tprev + 1) * 128]
                                vcur, vprev = Vb2[:, tcur, :], Vb2[:, tprev, :]
                                b0 = q0 + s_ * 512 + r_
                                qv = QT[:, b0:b0 + 509:4]
                                a0 = s_ * 512 + r_
                                av = acc[:, :, a0:a0 + 509:4]
                                halo_prev = (n == 0 and s_ == 0)
                            else:
                                r_ = u
                                tcur, tprev = ng * 16 + r_, (ng - 1) * 16 + r_
                                kcur = KT3[:, tcur * 128:(tcur + 1) * 128]
                                kprev = KT3[:, tprev * 128:(tprev + 1) * 128]
                                vcur, vprev = Vb3[:, tcur, :], Vb3[:, tprev, :]
                                qv = QT[:, q0 + r_:q0 + r_ + 2033:16]
                                av = acc[:, :, r_:r_ + 2033:16]
                                halo_prev = (n == 0)
                            units.append((br, kprev, kcur, vprev, vcur, qv, av, halo_prev))

                    def stage1(un, ub):
                        br, kprev, kcur, vprev, vcur, qv, av, halo_prev = un
                        bk = ("bank", id(psS[ub]))
                        c.mm(psS[ub][:, 0:128], kprev, qv, r=["K2", "K3", "KQ"], w=[bk])
                        c.mm(psS[ub][:, 128:256], kcur, qv, r=["K2", "K3", "KQ"], w=[bk])
                        c.act(Pt[ub][:], psS[ub][:, 0:256], AF.Exp, scale=ISQ, bias=c_mshift, r=[bk], w=[("Pt", ub)])
                        mk = maskAh if halo_prev else maskA
                        c.tt("pool", Pm[ub][:], Pt[ub][:], mk[:], ALU.mult, r=[("Pt", ub)], w=[("Pm", ub)])

                    def stage2(un, ub):
                        br, kprev, kcur, vprev, vcur, qv, av, halo_prev = un
                        c.mm(psO[ub][:, 0:128], vprev, Pm[ub][:, 0:128], True, False, r=["Vb", ("Pm", ub)], w=[("psO", ub)])
                        c.mm(psO[ub][:, 0:128], vcur, Pm[ub][:, 128:256], False, True, r=["Vb", ("Pm", ub)], w=[("psO", ub)])
                        c.mm(psO[ub][:, 128:256], onesb[:], Pm[ub][:, 0:128], True, False, r=[("Pm", ub)], w=[("psO", ub)])
                        c.mm(psO[ub][:, 128:256], onesb[:], Pm[ub][:, 128:256], False, True, r=[("Pm", ub)], w=[("psO", ub)])
                        pso = psO[ub][:, 0:256].rearrange("p (a b) -> p a b", a=2)
                        if br == 1:
                            c.cp("dve", av, pso, r=[("psO", ub)], w=["acc"])
                        else:
                            c.tt("dve", av, pso, av, ALU.add, r=[("psO", ub), "acc"], w=["acc"])

                    stage1(units[0], 0)
                    for ui in range(len(units)):
                        if ui + 1 < len(units):
                            stage1(units[ui + 1], (ui + 1) % 2)
                        stage2(units[ui], ui % 2)
                    c.recip(rl[:], acc[:, 1, :], r=["acc"], w=["rl"])
                    c.tt("dve", mst[:], acc[:, 0, :], rl[:], ALU.mult, r=["acc", "rl"], w=["mst"])
                    c.dma(mixT_s[:, h, n * 2048:(n + 1) * 2048], mst[:], r=["mst"])

        phase(phb1)

        def phb2(c):
            NCH = TT // 64
            NCHH = TH // 64
            Wb = c.sb([128, NKC, 4, 128], BF16)
            W32 = [c.sb([128, NKC, 128], F32) for _ in range(2)]
            hTt = [c.sb([128, NKC, 512], BF16) for _ in range(2)]
            KeT = c.sb([128, TT], BF16)
            QeT = c.sb([128, TM], BF16)
            gT = c.sb([128, TM], BF16)
            oT = c.sb([128, TM], F32)
            Kem = c.sb([128, TT // 128, 128], BF16)
            Vm = c.sb([128, TT // 128, 128], BF16)
            eref = c.sb([128, NCH + 1], F32)
            elast = c.sb([128, NCH], F32)
            e2 = c.sb([128, NCH], F32)
            dl2 = c.sb([128, 8], F32)
            sg = c.sb([128, 512], F32)
            sgn = c.sb([128, 512], F32)
            ff = c.sb([128, 512], F32)
            lf = c.sb([128, 512], F32)
            bb = c.sb([128, 512], F32)
            dd = c.sb([128, 512], F32)
            eq = c.sb([128, 512], F32)
            ek = c.sb([128, 512], F32)
            qs = c.sb([128, 512], F32)
            Sst = c.sb([128, 128], F32)
            Sb = c.sb([128, 128], BF16)
            aT = [c.sb([128, 64], BF16) for _ in range(2)]
            sq = c.sb([128, 512], BF16)
            sd = c.sb([128, 512], F32)
            rs = c.sb([128, 512], F32)
            tt_ = c.sb([128, 512], F32)
            hst = [c.sb([128, 512], BF16) for _ in range(2)]
            pg = [c.ps() for _ in range(2)]
            px0 = c.ps()
            px = [px0, px0]
            pT = c.ps([128, 1024], BF16)
            psA = [c.ps() for _ in range(2)]
            psU = [c.ps() for _ in range(2)]
            cnt = {"pg": 0, "px": 0}
            CWc = 2048
            cf = [c.sb([128, CWc], F32) for _ in range(2)]
            cg = [c.sb([128, CWc], BF16) for _ in range(2)]
            jobs = []
            for wi, (src, R, Cn) in enumerate(((w_out, D, D), (w_ff1, D, DFF), (w_ff2, DFF, D))):
                for r0 in range(0, R, 128):
                    for c0 in range(0, Cn, CWc):
                        cw = min(CWc, Cn - c0)
                        rb = r0 // 128
                        g0, g1_ = c0 // 512, (c0 + cw) // 512
                        if wi == 0:
                            dst = wo_s[g0:g1_, :, rb, :]
                        elif wi == 1:
                            dst = wf1_s[g0:g1_, :, rb, :]
                        else:
                            dst = wf2_s[g0:g1_, rb // PF_, :, rb % PF_, :]
                        jobs.append((src, dst.rearrange("g p n -> p g n"), r0, c0, cw))
            jstate = {"i": 0}
            per_tile = -(-len(jobs) // (NH * NT))

            def emit_casts(k):
                for _ in range(k):
                    i = jstate["i"]
                    if i >= len(jobs):
                        return
                    jstate["i"] += 1
                    src, dst, r0, c0, cw = jobs[i]
                    b = i % 2
                    c.dma(cf[b][:, 0:cw], src[r0:r0 + 128, c0:c0 + cw], w=[("cf", b)])
                    c.cp("pool", cg[b][:, 0:cw], cf[b][:, 0:cw], r=[("cf", b)], w=[("cg", b)])
                    c.dma(dst, cg[b][:, 0:cw].rearrange("p (g n) -> p g n", n=512), r=[("cg", b)])

            def nextpg():
                i = cnt["pg"] % 2
                cnt["pg"] += 1
                return i

            for h in range(NH):
                load_head_w(c, Wb, W32, h, (3, 4, 5, 6), "g")
                c.S.add("pool", lambda e: e.memset(eref[:, NCH:NCH + 1], 1.0), (), ["eref"])
                for j in range(NT):
                    hb = j % 2
                    main = j >= NTH
                    jm = j - NTH
                    c.dma(hTt[hb][:], hT_s[j], w=[("hTt", hb)])
                    tk = slice(j * 512, (j + 1) * 512)
                    pi_ = nextpg()
                    proj(c, pg[pi_][:], Wb, 1, hTt, hb, w=[("pg", pi_)])
                    c.act(sg[:], pg[pi_][:], AF.Sigmoid, r=[("pg", pi_)], w=["sg"])
                    c.act(sgn[:], pg[pi_][:], AF.Sigmoid, scale=-1.0, r=[("pg", pi_)], w=["sgn"])
                    c.ts("dve", ff[:], sg[:], omlb[:, h:h + 1], lb[:, h:h + 1], ALU.mult, ALU.add, r=["sg"], w=["ff"])
                    c.act(lf[:], ff[:], AF.Ln, r=["ff"], w=["lf"])
                    c.S.add("dve", lambda e: e.tensor_tensor_scan(out=bb[:], data0=resetm[:], data1=lf[:], initial=0.0,
                                                                  op0=ALU.mult, op1=ALU.add), ["lf"], ["bb"])
                    bv = bb[:].rearrange("p (c j) -> p c j", j=64)
                    c.tt("dve", dd[:].rearrange("p (c j) -> p c j", j=64), bv, bv[:, :, 31:32].to_broadcast([128, 8, 64]),
                         ALU.subtract, r=["bb"], w=["dd"])
                    c.act(ek[:], dd[:], AF.Exp, scale=-1.0, r=["dd"], w=["ek"])
                    om = omlb if main else omlbv
                    c.stt(KeT[:, tk], sgn[:], om[:, h:h + 1], ek[:], ALU.mult, ALU.mult, r=["sgn", "ek"], w=["KeT"])
                    c.act(eref[:, j * 8:(j + 1) * 8], bv[:, :, 31], AF.Exp, r=["bb"], w=["eref"])
                    c.act(elast[:, j * 8:(j + 1) * 8], bv[:, :, 63], AF.Exp, r=["bb"], w=["elast"])
                    c.tt("dve", dl2[:], bv[:, :, 63], bv[:, :, 31], ALU.subtract, r=["bb"], w=["dl2"])
                    c.act(e2[:, j * 8:(j + 1) * 8], dl2[:], AF.Exp, r=["dl2"], w=["e2"])
                    pv = pT
                    for q4 in range(4):
                        c.tr(pv[:, q4 * 128:(q4 + 1) * 128], KeT[:, j * 512 + q4 * 128:j * 512 + (q4 + 1) * 128], identb[:],
                             r=["KeT"], w=["pT"])
                    c.cp("act", Kem[:, j * 4:(j + 1) * 4, :], pv[:, 0:512].rearrange("p (a b) -> p a b", a=4),
                         r=["pT"], w=["Kem"])
                    pi_ = nextpg()
                    for q4 in range(4):
                        for kc in range(NKC):
                            c.mm(pg[pi_][:, q4 * 128:(q4 + 1) * 128], hTt[hb][:, kc, q4 * 128:(q4 + 1) * 128],
                                 Wb[:, kc, 2, :], kc == 0, kc == NKC - 1, r=[("Wb", 2), ("hTt", hb)], w=[("pg", pi_)])
                    c.cp("act", Vm[:, j * 4:(j + 1) * 4, :], pg[pi_][:].rearrange("p (a b) -> p a b", a=4),
                         r=[("pg", pi_)], w=["Vm"])
                    if main:
                        c.act(eq[:], dd[:], AF.Exp, r=["dd"], w=["eq"])
                        pi_ = nextpg()
                        proj(c, pg[pi_][:], Wb, 0, hTt, hb, w=[("pg", pi_)])
                        c.act(qs[:], pg[pi_][:], AF.Silu, r=[("pg", pi_)], w=["qs"])
                        c.tt("dve", QeT[:, jm * 512:(jm + 1) * 512], qs[:], eq[:], ALU.mult, r=["qs", "eq"], w=["QeT"])
                        pi_ = nextpg()
                        proj(c, pg[pi_][:], Wb, 3, hTt, hb, w=[("pg", pi_)])
                        c.act(gT[:, jm * 512:(jm + 1) * 512], pg[pi_][:], AF.Silu, r=[("pg", pi_)], w=["gT"])
                    emit_casts(per_tile)
                c.S.add("pool", lambda e: e.memset(Sst[:], 0.0), (), ["S"])
                c.S.add("pool", lambda e: e.memset(Sb[:], 0.0), (), ["Sb"])
                def emit_U(ch):
                    t128, po, ub = ch // 2, 64 * (ch % 2), ch % 2
                    c.mm(psU[ub][:, 0:128], Kem[po:po + 64, t128, :], Vm[po:po + 64, t128, :],
                         r=["Kem", "Vm"], w=[("psU", ub)])

                def emit_a(ch):
                    t128, po, ub = ch // 2, 64 * (ch % 2), ch % 2
                    cm = ch - NCHH
                    c.mm(psA[ub][po:po + 64, 0:64], KeT[:, ch * 64:(ch + 1) * 64], QeT[:, cm * 64:(cm + 1) * 64],
                         r=["KeT", "QeT"], w=[("psA", ub)])
                    c.tt("dve", aT[ub][po:po + 64, :], psA[ub][po:po + 64, 0:64], maskH[po:po + 64, :], ALU.mult,
                         r=[("psA", ub)], w=[("aT", ub)])

                def emit_o(ch):
                    t128, po, ub = ch // 2, 64 * (ch % 2), ch % 2
                    cm = ch - NCHH
                    qcols = QeT[:, cm * 64:(cm + 1) * 64]
                    c.mm(psA[ub][:, 128:192], Vm[po:po + 64, t128, :], aT[ub][po:po + 64, :], True, False,
                         r=["Vm", ("aT", ub)], w=[("psA", ub)])
                    c.mm(psA[ub][:, 128:192], Sb[:], qcols, False, True, r=["Sb", "QeT"], w=[("psA", ub)])
                    c.cp("act", oT[:, cm * 64:(cm + 1) * 64], psA[ub][:, 128:192], r=[("psA", ub)], w=["oT"])

                emit_U(0)
                for ch in range(NCH):
                    ub = ch % 2
                    if ch + 1 < NCH:
                        emit_U(ch + 1)
                        if ch + 1 >= NCHH:
                            emit_a(ch + 1)
                    if ch >= NCHH:
                        emit_o(ch)
                    c.ts("dve", Sst[:], Sst[:], elast[:, ch:ch + 1], None, ALU.mult, r=["S", "elast"], w=["S"])
                    c.stt(Sst[:], psU[ub][:, 0:128], e2[:, ch:ch + 1], Sst[:], ALU.mult, ALU.add,
                          r=[("psU", ub), "S", "e2"], w=["S"])
                    if ch + 1 >= NCHH and ch + 1 < NCH:
                        c.ts("dve", Sb[:], Sst[:], eref[:, ch + 1:ch + 2], None, ALU.mult, r=["S", "eref"], w=["Sb"])
                for jm in range(NTM):
                    sb_ = jm % 2
                    cs = slice(jm * 512, (jm + 1) * 512)
                    c.act(sq[:], oT[:, cs], AF.Square, r=["oT"], w=["sq"])
                    pxi = cnt["px"] % 2
                    cnt["px"] += 1
                    c.mm(px[pxi][:], onesb[:], sq[:], r=["sq"], w=[("bank", id(px[pxi]))])
                    c.act(sd[:], px[pxi][:], AF.Sqrt, scale=1.0 / 128.0, bias=c_eps128, r=[("bank", id(px[pxi]))], w=["sd"])
                    c.recip(rs[:], sd[:], r=["sd"], w=["rs"])
                    c.stt(tt_[:], oT[:, cs], hgnw[:, 0:1], rs[:], ALU.mult, ALU.mult, r=["oT", "rs"], w=["tt"])
                    c.tt("dve", hst[sb_][:], tt_[:], gT[:, cs], ALU.mult, r=["tt", "gT"], w=[("hst", sb_)])
                    c.dma(mixT_s[:, NH + h, cs], hst[sb_][:], r=[("hst", sb_)])
            emit_casts(len(jobs))

        phase(phb2)

        def phc(c):
            NOG = NKC // 4
            HALF = NFF // 2
            PF = min(16, HALF)
            mixT = c.sb([128, NKC, 512], BF16)
            xT = c.sb([128, NKC, 512], F32)
            xtok = [c.sb([128, D], F32) for _ in range(2)]
            h2T = c.sb([128, NKC, 512], BF16)
            uT = c.sb([128, HALF, 512], BF16)
            pan = [c.sb([128, 16, 512], BF16) for _ in range(3)]
            sqa = c.sb([128, NKC, 512], BF16)
            sd = c.sb([128, 512], F32)
            rr = c.sb([128, 512], F32)
            tmp = [c.sb([128, 512], F32) for _ in range(2)]
            rl = [c.sb([128, 512], BF16) for _ in range(2)]
            pa = [c.ps() for _ in range(4)]
            pb = [c.ps() for _ in range(4)]
            cnt = {"pan": 0, "pa": 0, "t": 0}

            def nextpan():
                i = cnt["pan"] % 3
                cnt["pan"] += 1
                return i

            def nextpa():
                i = cnt["pa"] % 4
                cnt["pa"] += 1
                return i

            for i in range(NTM):
                tok0 = TH + i * 512
                c.dma(mixT[:], mixT_s[:, :, i * 512:(i + 1) * 512], w=["mixT"])
                for blk in range(4):
                    xb = blk % 2
                    c.dma(xtok[xb][:], x[tok0 + blk * 128:tok0 + (blk + 1) * 128, :], w=[("xtok", xb)])
                    for k4 in range(NOG):
                        pi_ = nextpa()
                        for q4 in range(4):
                            kc = k4 * 4 + q4
                            c.tr(pa[pi_][:, q4 * 128:(q4 + 1) * 128], xtok[xb][:, kc * 128:(kc + 1) * 128], identf[:],
                                 r=[("xtok", xb)], w=[("pa", pi_)])
                        c.cp("act" if k4 % 2 == 0 else "dve", xT[:, k4 * 4:(k4 + 1) * 4, blk * 128:(blk + 1) * 128],
                             pa[pi_][:].rearrange("p (a b) -> p a b", a=4), r=[("pa", pi_)], w=[("xT", k4 * 4 + q_) for q_ in range(4)])
                c.act(sqa[:, 0:NH, :], mixT[:, 0:NH, :], AF.Square, r=["mixT"], w=["sqa"])
                pi_ = nextpa()
                for hh in range(NH):
                    c.mm(pa[pi_][:], onesb[:], sqa[:, hh, :], hh == 0, hh == NH - 1, r=["sqa"], w=[("pa", pi_)])
                c.act(sd[:], pa[pi_][:], AF.Sqrt, scale=1.0 / W, bias=c_eps, r=[("pa", pi_)], w=["sd"])
                c.recip(rr[:], sd[:], r=["sd"], w=["rr"])
                for hh in range(NH):
                    c.stt(mixT[:, hh, :], mixT[:, hh, :], aonw[:, hh:hh + 1], rr[:], ALU.mult, ALU.mult,
                          r=["mixT", "rr"], w=["mixT"])
                for og in range(NOG):
                    pn = nextpan()
                    c.dma(pan[pn][:, 0:NKC, :], wo_s[og], w=[("pan", pn)])
                    for oc4 in range(4):
                        oc = og * 4 + oc4
                        pi_ = nextpa()
                        for kc in range(NKC):
                            c.mm(pa[pi_][:], pan[pn][:, kc, oc4 * 128:(oc4 + 1) * 128], mixT[:, kc, :], kc == 0, kc == NKC - 1,
                                 r=[("pan", pn), "mixT"], w=[("pa", pi_)])
                        c.stt(xT[:, oc, :], pa[pi_][:], gate1[:, oc:oc + 1], xT[:, oc, :], ALU.mult, ALU.add,
                              r=[("pa", pi_), ("xT", oc)], w=[("xT", oc)])
                c.act(sqa[:], xT[:], AF.Square, r=[("xT", k_) for k_ in range(NKC)], w=["sqa"])
                pi_ = nextpa()
                for kc in range(NKC):
                    c.mm(pa[pi_][:], onesb[:], sqa[:, kc, :], kc == 0, kc == NKC - 1, r=["sqa"], w=[("pa", pi_)])
                c.act(sd[:], pa[pi_][:], AF.Sqrt, scale=1.0 / D, bias=c_eps, r=[("pa", pi_)], w=["sd"])
                c.recip(rr[:], sd[:], r=["sd"], w=["rr"])
                for kc in range(NKC):
                    tb = cnt["t"] % 2
                    cnt["t"] += 1
                    c.stt(tmp[tb][:], xT[:, kc, :], g2[:, kc:kc + 1], rr[:], ALU.mult, ALU.mult,
                          r=[("xT", kc), "rr"], w=[("tmp", tb)])
                    c.ts("dve", h2T[:, kc, :], tmp[tb][:], sh2[:, kc:kc + 1], None, ALU.add, r=[("tmp", tb)], w=["h2T"])
                for hf in range(2):
                    for fg in range(HALF // 4):
                        pn = nextpan()
                        col0 = (hf * HALF + fg * 4) * 128
                        c.dma(pan[pn][:, 0:NKC, :], wf1_s[col0 // 512], w=[("pan", pn)])
                        for f4 in range(4):
                            fl = fg * 4 + f4
                            pi_ = nextpa()
                            for kc in range(NKC):
                                c.mm(pa[pi_][:], pan[pn][:, kc, f4 * 128:(f4 + 1) * 128], h2T[:, kc, :], kc == 0, kc == NKC - 1,
                                     r=[("pan", pn), "h2T"], w=[("pa", pi_)])
                            tb = cnt["t"] % 2
                            cnt["t"] += 1
                            c.act(rl[tb][:], pa[pi_][:], AF.Relu, r=[("pa", pi_)], w=[("rl", tb)])
                            c.tt("pool", uT[:, fl, :], rl[tb][:], rl[tb][:], ALU.mult, r=[("rl", tb)], w=[("uT", fl)])
                    for og in range(NOG):
                        for q in range(HALF // PF):
                            pn = nextpan()
                            k0 = hf * HALF + q * PF
                            c.dma(pan[pn][:, 0:PF, :], wf2_s[og, k0 // PF], w=[("pan", pn)])
                            for oc4 in range(4):
                                for f in range(PF):
                                    fl = q * PF + f
                                    c.mm(pb[oc4][:], pan[pn][:, f, oc4 * 128:(oc4 + 1) * 128], uT[:, fl, :],
                                         q == 0 and f == 0, q == HALF // PF - 1 and f == PF - 1,
                                         r=[("pan", pn), ("uT", fl)], w=[("pb", oc4)])
                        for oc4 in range(4):
                            oc = og * 4 + oc4
                            c.stt(xT[:, oc, :], pb[oc4][:], gate2[:, oc:oc + 1], xT[:, oc, :], ALU.mult, ALU.add,
                                  r=[("pb", oc4), ("xT", oc)], w=[("xT", oc)])
                for blk in range(4):
                    xb = blk % 2
                    for k4 in range(NOG):
                        pi_ = nextpa()
                        for q4 in range(4):
                            kc = k4 * 4 + q4
                            c.tr(pa[pi_][:, q4 * 128:(q4 + 1) * 128], xT[:, kc, blk * 128:(blk + 1) * 128], identf[:],
                                 r=[("xT", kc)], w=[("pa", pi_)])
                        c.cp("act" if k4 % 2 == 0 else "dve", xtok[xb][:, k4 * 512:(k4 + 1) * 512], pa[pi_][:],
                             r=[("pa", pi_)], w=[("xtok", xb)])
                    c.dma(out[i * 512 + blk * 128:i * 512 + (blk + 1) * 128, :], xtok[xb][:], r=[("xtok", xb)])

        phase(phc)
    return nc


def make_consts():
    bf = ml_dtypes.bfloat16
    p = np.arange(128)
    i = np.arange(128)
    mprev = (p[:, None] >= i[None, :]).astype(np.float32)
    mcur = (p[:, None] <= i[None, :]).astype(np.float32)
    maskA = np.concatenate([mprev, mcur], axis=1).astype(bf)
    t = np.arange(64)
    maskH = ((p[:, None] % 64) <= t[None, :]).astype(np.float32).astype(bf)
    rotT = np.zeros((128, 128), np.float32)
    for m in range(16):
        rotT[m + 16, m] = -1.0
    for m in range(16, 32):
        rotT[m - 16, m] = 1.0
    resetm = np.ones((128, 512), np.float32)
    resetm[:, ::64] = 0.0
    invf = np.zeros((128, 1), np.float32)
    half = 16
    fr = (np.float32(ROPE_THETA) ** (-(np.arange(half, dtype=np.float32) * np.float32(2.0)) / np.float32(32))).astype(np.float32)
    invf[0:16, 0] = fr
    invf[16:32, 0] = fr
    cst = np.zeros((128, 4), np.float32)
    cst[:, 0] = EPS
    cst[:, 1] = EPS * 128.0
    cst[:, 2] = -MSHIFT
    return {
        "identb": np.eye(128, dtype=np.float32).astype(bf), "identf": np.eye(128, dtype=np.float32),
        "onesb": np.ones((128, 128), np.float32).astype(bf), "rotT": rotT.astype(bf), "maskA": maskA, "maskH": maskH,
        "resetm": resetm, "invf": invf, "cst": cst,
    }


def pk(v, nkc):
    return np.ascontiguousarray(np.asarray(v, np.float32).reshape(nkc, 128).T)


def make_in_maps(inputs, D, NH, DFF, TM, TH, n_seg):
    NKC = D // 128
    x = np.asarray(inputs["x"], np.float32)
    B, S, _ = x.shape
    posi = np.asarray(inputs["positions"], np.int32)
    consts = make_consts()
    shared = {
        "n1w": pk(inputs["norm1_w"][0], NKC), "n2w": pk(inputs["norm2_w"][0], NKC),
        "bada": np.ascontiguousarray(np.asarray(inputs["b_ada"][0], np.float32).reshape(6 * NKC, 128).T),
        "w_ada": np.ascontiguousarray(np.asarray(inputs["w_ada"][0], np.float32)),
        "w_in": np.ascontiguousarray(np.asarray(inputs["w_in"][0], np.float32)),
        "qnw": pk(inputs["q_norm_w"][0], 1), "knw": pk(inputs["k_norm_w"][0], 1),
        "aonw": pk(inputs["attn_out_norm_w"][0], NH),
        "lbl": np.ascontiguousarray(np.asarray(inputs["hg_lb_logits"], np.float32).reshape(2 * NH, 128).T),
        "hgnw": pk(inputs["hg_norm_w"][0], 1),
        "w_out": np.ascontiguousarray(np.asarray(inputs["w_out"][0], np.float32)),
        "w_ff1": np.ascontiguousarray(np.asarray(inputs["w_ff1"][0], np.float32)),
        "w_ff2": np.ascontiguousarray(np.asarray(inputs["w_ff2"][0], np.float32)),
    }
    shared.update(consts)
    maps = []
    for b in range(B):
        for j in range(n_seg):
            s0 = j * TM
            xin = np.zeros((TH + TM, D), np.float32)
            pin = np.zeros((1, TH + TM), np.int32)
            if j > 0:
                xin[:] = x[b, s0 - TH:s0 + TM]
                pin[0, :] = posi[b, s0 - TH:s0 + TM]
            else:
                xin[TH:] = x[b, 0:TM]
                pin[0, TH:] = posi[b, 0:TM]
            m = dict(shared)
            m["x"] = xin
            m["pos"] = pin
            m["cT"] = pk(inputs["c"][b], NKC)
            m["valid"] = np.full((128, 1), 1.0 if j > 0 else 0.0, np.float32)
            maps.append(m)
    return maps


_NC_CACHE = {}


def kernel(**inputs):
    D, NH, DFF, TM, TH = 2048, 8, 8192, 4096, 2048
    x = np.asarray(inputs["x"])
    B, S, _ = x.shape
    n_seg = S // TM
    key = (D, NH, DFF, TM, TH)
    if key not in _NC_CACHE:
        _NC_CACHE[key] = build_program(*key)
    nc = _NC_CACHE[key]
    in_maps = make_in_maps(inputs, D, NH, DFF, TM, TH, n_seg)
    res = run_bass_kernel_spmd(nc, in_maps, core_ids=list(range(len(in_maps))))
    outp = np.empty((B, S, D), np.float32)
    k = 0
    for b in range(B):
        for j in range(n_seg):
            outp[b, j * TM:(j + 1) * TM] = np.asarray(res.results[k]["out"], np.float32)
            k += 1
    return outp
```

```python
import math
from contextlib import ExitStack
import numpy as np
import ml_dtypes
import concourse.bass as bass
import concourse.mybir as mybir
from concourse.bass_utils import run_bass_kernel_spmd

F32 = mybir.dt.float32
BF16 = mybir.dt.bfloat16
I32 = mybir.dt.int32
AF = mybir.ActivationFunctionType
ALU = mybir.AluOpType

EPS = 1e-6
MSHIFT = 8.0
ROPE_THETA = 500000.0
TWO_PI = 2.0 * math.pi
CW1 = 6.28125
CW2 = TWO_PI - CW1


class Op:
    __slots__ = ("eng", "fn", "deps", "sig", "sem", "val", "idx")

    def __init__(self, eng, fn):
        self.eng = eng
        self.fn = fn
        self.deps = []
        self.sig = False
        self.sem = None
        self.val = 0
        self.idx = 0


class Sched:
    G = None
    ENGS = ("pe", "act", "dve", "pool", "sp")
    NS_DMA = 24
    ROT = 30000

    def __init__(self, nc):
        self.nc = nc
        self.ops = {e: [] for e in self.ENGS}
        self.last_writer = {}
        self.readers = {}

    def add(self, eng, fn, reads=(), writes=()):
        op = Op(eng, fn)
        ds = {}
        for r in reads:
            w = self.last_writer.get(r)
            if w is not None:
                ds[id(w)] = w
        for w in writes:
            lw = self.last_writer.get(w)
            if lw is not None:
                ds[id(lw)] = lw
            for rd in self.readers.get(w, ()):
                ds[id(rd)] = rd
            self.readers[w] = []
            self.last_writer[w] = op
        for r in reads:
            self.readers.setdefault(r, []).append(op)
        for d in ds.values():
            if d is op:
                continue
            if d.eng == "pe" and eng == "pe":
                continue
            op.deps.append(d)
            d.sig = True
        if eng == "sp":
            op.sig = True
        self.ops[eng].append(op)
        return op

    def emit(self):
        nc = self.nc
        G = Sched.G
        with ExitStack() as st:
            for e in ("pe", "act", "dve", "pool"):
                sl = G["sems"][e]
                c = G["cnt"][e]
                for o in self.ops[e]:
                    if o.sig:
                        o.sem = sl[c // self.ROT]
                        o.val = c % self.ROT + 1
                        c += 1
                G["cnt"][e] = c
            ring = G["ring"]
            NS = self.NS_DMA
            base = G["dma"]
            for i, o in enumerate(self.ops["sp"]):
                gi = base + i
                o.sem = ring[gi % NS]
                o.val = 16 * (gi // NS + 1)
                o.idx = gi
            G["dma"] = base + len(self.ops["sp"])
            ops = self.ops

            def stream(ename, e):
                waited = {}

                def wait(sem, val):
                    k = id(sem)
                    if waited.get(k, 0) >= val:
                        return
                    waited[k] = val
                    e.wait_ge(sem, val)

                for o in ops[ename]:
                    for d in o.deps:
                        wait(d.sem, d.val)
                    if ename == "sp" and o.idx >= NS:
                        wait(o.sem, o.val - 16)
                    ins = o.fn(e)
                    if o.sig:
                        ins.then_inc(o.sem, 16 if ename == "sp" else 1)
                if ename == "sp":
                    n = len(ops["sp"])
                    for i in range(max(0, n - NS), n):
                        o = ops["sp"][i]
                        wait(o.sem, o.val)

            blk = st.enter_context(nc.Block())

            @blk.tensor
            def _(e):
                stream("pe", e)

            @blk.scalar
            def _(e):
                stream("act", e)

            @blk.vector
            def _(e):
                stream("dve", e)

            @blk.gpsimd
            def _(e):
                stream("pool", e)

            @blk.sync
            def _(e):
                stream("sp", e)


class Ctx:
    def __init__(self, nc, st):
        self.nc = nc
        self.st = st
        self.S = Sched(nc)
        self.n = 0

    CNT = [0]

    def sb(self, shape, dt, name=None):
        Ctx.CNT[0] += 1
        return self.st.enter_context(self.nc.sbuf_tensor(f"t{Ctx.CNT[0]}", list(shape), dt))

    def ps(self, shape=(128, 512), dt=F32, name=None):
        Ctx.CNT[0] += 1
        return self.st.enter_context(self.nc.psum_tensor(f"p{Ctx.CNT[0]}", list(shape), dt))

    def dma(self, out, in_, r=(), w=()):
        return self.S.add("sp", lambda e: e.dma_start(out=out, in_=in_), r, w)

    def mm(self, out, lhsT, rhs, start=True, stop=True, r=(), w=()):
        return self.S.add("pe", lambda e: e.matmul(out, lhsT=lhsT, rhs=rhs, start=start, stop=stop,
                                                    skip_group_check=True), r, w)

    def tr(self, out, in_, ident, r=(), w=()):
        return self.S.add("pe", lambda e: e.transpose(out, in_, ident), r, w)

    def act(self, out, in_, func, scale=1.0, bias=None, accum=None, r=(), w=()):
        def f(e):
            kw = {}
            if bias is not None:
                kw["bias"] = bias
            if accum is not None:
                kw["accum_out"] = accum
            return e.activation(out=out, in_=in_, func=func, scale=scale, **kw)
        return self.S.add("act", f, r, w)

    def tt(self, eng, out, in0, in1, op, r=(), w=()):
        return self.S.add(eng, lambda e: e.tensor_tensor(out=out, in0=in0, in1=in1, op=op), r, w)

    def ts(self, eng, out, in0, s1, s2, op0, op1=None, r=(), w=()):
        if op1 is None:
            return self.S.add(eng, lambda e: e.tensor_scalar(out=out, in0=in0, scalar1=s1, scalar2=None, op0=op0), r, w)
        return self.S.add(eng, lambda e: e.tensor_scalar(out=out, in0=in0, scalar1=s1, scalar2=s2, op0=op0, op1=op1), r, w)

    def stt(self, out, in0, scalar, in1, op0, op1, r=(), w=()):
        return self.S.add("dve", lambda e: e.scalar_tensor_tensor(out=out, in0=in0, scalar=scalar, in1=in1,
                                                                   op0=op0, op1=op1), r, w)

    def cp(self, eng, out, in_, r=(), w=()):
        if eng == "act":
            return self.act(out, in_, AF.Copy, r=r, w=w)
        return self.S.add(eng, lambda e: e.tensor_copy(out=out, in_=in_), r, w)

    def recip(self, out, in_, r=(), w=()):
        return self.S.add("dve", lambda e: e.reciprocal(out=out, in_=in_), r, w)


def build_program(D, NH, DFF, TM, TH, only=None):
    NKC = D // 128
    W = 128 * NH
    assert 2 * W == D
    TT = TM + TH
    NFF = DFF // 128
    NT = TT // 512
    NTH = TH // 512
    NTM = TM // 512
    NSPAN = TM // 2048
    assert TH == 2048 and TM % 2048 == 0
    ISQ = 1.0 / math.sqrt(128.0)

    nc = bass.Bass("TRN2", target_bir_lowering=False)

    def din(name, shape, dt=F32):
        return nc.dram_tensor(name, list(shape), dt, kind="ExternalInput").ap()

    x = din("x", [TT, D])
    pos = din("pos", [1, TT], I32)
    cT = din("cT", [128, NKC])
    valid_d = din("valid", [128, 1])
    n1w_d = din("n1w", [128, NKC])
    n2w_d = din("n2w", [128, NKC])
    bada_d = din("bada", [128, 6 * NKC])
    w_ada = din("w_ada", [D, 6 * D])
    w_in = din("w_in", [D, 7 * W])
    qnw_d = din("qnw", [128, 1])
    knw_d = din("knw", [128, 1])
    aonw_d = din("aonw", [128, NH])
    lbl_d = din("lbl", [128, 2 * NH])
    hgnw_d = din("hgnw", [128, 1])
    w_out = din("w_out", [D, D])
    w_ff1 = din("w_ff1", [D, DFF])
    w_ff2 = din("w_ff2", [DFF, D])
    identb_d = din("identb", [128, 128], BF16)
    identf_d = din("identf", [128, 128])
    onesb_d = din("onesb", [128, 128], BF16)
    rotT_d = din("rotT", [128, 128], BF16)
    maskA_d = din("maskA", [128, 256], BF16)
    maskH_d = din("maskH", [128, 64], BF16)
    resetm_d = din("resetm", [128, 512])
    invf_d = din("invf", [128, 1])
    cst_d = din("cst", [128, 4])
    out = nc.dram_tensor("out", [TM, D], F32, kind="ExternalOutput").ap()

    hT_s = nc.dram_tensor("hT_s", [TT // 512, 128, NKC, 512], BF16).ap()
    mixT_s = nc.dram_tensor("mixT_s", [128, NKC, TM], BF16).ap()
    cs_s = nc.dram_tensor("cs_s", [2, 128, TT], F32).ap()
    wo_s = nc.dram_tensor("wo_s", [D, D], BF16).ap()
    wf1_s = nc.dram_tensor("wf1_s", [D, DFF], BF16).ap()
    wf2_s = nc.dram_tensor("wf2_s", [DFF, D], BF16).ap()

    with ExitStack() as gst:
        Sched.G = {"sems": {e: [gst.enter_context(nc.semaphore(f"s_{e}{i}")) for i in range(5)]
                            for e in ("pe", "act", "dve", "pool")},
                   "cnt": {e: 0 for e in ("pe", "act", "dve", "pool")},
                   "ring": [gst.enter_context(nc.semaphore(f"s_dma{i}")) for i in range(Sched.NS_DMA)],
                   "dma": 0}

        def gsb(shape, dt, name):
            return gst.enter_context(nc.sbuf_tensor(name + "_sb", list(shape), dt))

        modv = gsb([128, 6 * NKC], F32, "modv")
        g1 = gsb([128, NKC], F32, "g1")
        g2 = gsb([128, NKC], F32, "g2")
        lb = gsb([128, NH], F32, "lb")
        omlb = gsb([128, NH], F32, "omlb")
        omlbv = gsb([128, NH], F32, "omlbv")
        qnw = gsb([128, 1], F32, "qnw")
        knw = gsb([128, 1], F32, "knw")
        aonw = gsb([128, NH], F32, "aonw")
        hgnw = gsb([128, 1], F32, "hgnw")
        valid = gsb([128, 1], F32, "validt")
        invf = gsb([128, 1], F32, "invft")
        cst = gsb([128, 4], F32, "cstt")
        identb = gsb([128, 128], BF16, "identbt")
        identf = gsb([128, 128], F32, "identft")
        onesb = gsb([128, 128], BF16, "onesbt")
        rotT = gsb([128, 128], BF16, "rotTt")
        maskA = gsb([128, 256], BF16, "maskAt")
        maskAh = gsb([128, 256], BF16, "maskAht")
        maskH = gsb([128, 64], BF16, "maskHt")
        resetm = gsb([128, 512], F32, "resetmt")
        sh1 = modv[:, 0 * NKC:1 * NKC]
        sc1 = modv[:, 1 * NKC:2 * NKC]
        gate1 = modv[:, 2 * NKC:3 * NKC]
        sh2 = modv[:, 3 * NKC:4 * NKC]
        sc2 = modv[:, 4 * NKC:5 * NKC]
        gate2 = modv[:, 5 * NKC:6 * NKC]
        c_eps = cst[:, 0:1]
        c_eps128 = cst[:, 1:2]
        c_mshift = cst[:, 2:3]

        pcount = [0]

        def phase(fn):
            pcount[0] += 1
            if only is not None and pcount[0] not in only:
                return
            with ExitStack() as st:
                c = Ctx(nc, st)
                fn(c)
                c.S.emit()
            pass

        def rope_tables(c):
            PI_S = 3.1415925
            pis = [c.sb([128, 512], I32) for _ in range(2)]
            tmps = [[c.sb([128, 512], F32) for _ in range(5)] for _ in range(2)]
            kis = [c.sb([128, 512], I32) for _ in range(2)]
            for j in range(NT):
                b = j % 2
                pi_ = pis[b]
                ang, u, kf, rr, m = tmps[b]
                ki = kis[b]
                c.dma(pi_[:], pos[:, j * 512:(j + 1) * 512].partition_broadcast(128), w=[("pi", b)])
                c.cp("dve", ang[:], pi_[:], r=[("pi", b)], w=[("ang", b)])
                c.ts("dve", ang[:], ang[:], invf[:, 0:1], None, ALU.mult, r=[("ang", b), ("g", id(invf))], w=[("ang", b)])
                for which, shift in ((0, math.pi / 2), (1, 0.0)):
                    c.ts("dve", u[:], ang[:], 1.0 / TWO_PI, shift / TWO_PI + 0.5, ALU.mult, ALU.add,
                         r=[("ang", b)], w=[("u", b)])
                    c.cp("dve", ki[:], u[:], r=[("u", b)], w=[("ki", b)])
                    c.cp("dve", kf[:], ki[:], r=[("ki", b)], w=[("kf", b)])
                    c.ts("dve", rr[:], ang[:], shift, None, ALU.add, r=[("ang", b)], w=[("rr", b)])
                    c.stt(rr[:], kf[:], -CW1, rr[:], ALU.mult, ALU.add, r=[("kf", b), ("rr", b)], w=[("rr", b)])
                    c.stt(rr[:], kf[:], -CW2, rr[:], ALU.mult, ALU.add, r=[("kf", b), ("rr", b)], w=[("rr", b)])
                    c.ts("dve", m[:], rr[:], math.pi, -TWO_PI, ALU.is_gt, ALU.mult, r=[("rr", b)], w=[("m", b)])
                    c.tt("dve", rr[:], rr[:], m[:], ALU.add, r=[("rr", b), ("m", b)], w=[("rr", b)])
                    c.ts("dve", m[:], rr[:], -math.pi, TWO_PI, ALU.is_lt, ALU.mult, r=[("rr", b)], w=[("m", b)])
                    c.tt("dve", rr[:], rr[:], m[:], ALU.add, r=[("rr", b), ("m", b)], w=[("rr", b)])
                    c.ts("dve", rr[:], rr[:], PI_S, -PI_S, ALU.min, ALU.max, r=[("rr", b)], w=[("rr", b)])
                    c.act(u[:], rr[:], AF.Sin, r=[("rr", b)], w=[("u", b)])
                    c.dma(cs_s[which, :, j * 512:(j + 1) * 512], u[:], r=[("u", b)])

        def ph0(c):
            for t, d in ((valid, valid_d), (invf, invf_d), (cst, cst_d), (identb, identb_d), (identf, identf_d),
                         (onesb, onesb_d), (rotT, rotT_d), (maskA, maskA_d), (maskH, maskH_d), (resetm, resetm_d),
                         (qnw, qnw_d), (knw, knw_d), (aonw, aonw_d), (hgnw, hgnw_d)):
                c.dma(t[:], d, w=[("g", id(t))])
            n1w = c.sb([128, NKC], F32)
            n2w = c.sb([128, NKC], F32)
            bada = c.sb([128, 6 * NKC], F32)
            lbl = c.sb([128, 2 * NH], F32)
            cTt = c.sb([128, NKC], F32)
            c.dma(n1w[:], n1w_d, w=["n1w"])
            c.dma(n2w[:], n2w_d, w=["n2w"])
            c.dma(bada[:], bada_d, w=["bada"])
            c.dma(lbl[:], lbl_d, w=["lbl"])
            c.dma(cTt[:], cT, w=["cT"])
            c.cp("dve", maskAh[:, 128:256], maskA[:, 128:256], r=[("g", id(maskA))], w=["mAh1"])
            c.ts("dve", maskAh[:, 0:128], maskA[:, 0:128], valid[:, 0:1], None, ALU.mult,
                 r=[("g", id(maskA)), ("g", id(valid))], w=["mAh0"])
            dl = c.sb([128, NH], F32)
            c.tt("dve", dl[:], lbl[:, 0:NH], lbl[:, NH:2 * NH], ALU.subtract, r=["lbl"], w=["dl"])
            c.act(lb[:], dl[:], AF.Sigmoid, r=["dl"], w=["lb"])
            c.ts("dve", omlb[:], lb[:], -1.0, 1.0, ALU.mult, ALU.add, r=["lb"], w=["omlb"])
            c.ts("dve", omlbv[:], omlb[:], valid[:, 0:1], None, ALU.mult, r=["omlb", ("g", id(valid))], w=["omlbv"])
            scs = c.sb([128, NKC], F32)
            c.act(scs[:], cTt[:], AF.Silu, r=["cT"], w=["scs"])
            zer = c.sb([128, 128], F32)
            c.S.add("pool", lambda e: e.memset(zer[:], 0.0), (), ["zer"])
            pm = c.ps([128, 512], F32)
            NJ = 6 * NKC
            c.mm(pm[:, 0:NJ], zer[:, 0:128], zer[:, 0:NJ], True, False, r=["zer"], w=["pm"])
            CH = min(6 * D, 6144)
            NCH = (6 * D) // CH
            wa = [c.sb([128, CH], F32) for _ in range(2)]
            it = 0
            for kc in range(NKC):
                for ch in range(NCH):
                    b = it % 2
                    it += 1
                    c.dma(wa[b][:], w_ada[kc * 128:(kc + 1) * 128, ch * CH:(ch + 1) * CH], w=[("wa", b)])
                    for jj in range(CH // 128):
                        j = ch * (CH // 128) + jj
                        last = (kc == NKC - 1)
                        c.mm(pm[:, j:j + 1], wa[b][:, jj * 128:(jj + 1) * 128], scs[:, kc:kc + 1], False, last,
                             r=[("wa", b), "scs"], w=["pm"])
            rope_tables(c)
            c.tt("dve", modv[:], pm[:, 0:NJ], bada[:], ALU.add, r=["pm", "bada"], w=["modv"])
            c.stt(g1[:], sc1, 1.0, n1w[:], ALU.add, ALU.mult, r=["modv", "n1w"], w=["g1"])
            c.stt(g2[:], sc2, 1.0, n2w[:], ALU.add, ALU.mult, r=["modv", "n2w"], w=["g2"])

        phase(ph0)

        def php(c):
            CW = 4096
            f = [c.sb([128, CW], F32) for _ in range(3)]
            g = [c.sb([128, CW], BF16) for _ in range(3)]
            engs = ("act", "dve", "pool")
            it = 0
            for src, dst, R, Cn in ((w_out, wo_s, D, D), (w_ff1, wf1_s, D, DFF), (w_ff2, wf2_s, DFF, D)):
                for r0 in range(0, R, 128):
                    for c0 in range(0, Cn, CW):
                        cw = min(CW, Cn - c0)
                        b = it % 3
                        c.dma(f[b][:, 0:cw], src[r0:r0 + 128, c0:c0 + cw], w=[("f", b)])
                        c.cp(engs[b], g[b][:, 0:cw], f[b][:, 0:cw], r=[("f", b)], w=[("g", b)])
                        c.dma(dst[r0:r0 + 128, c0:c0 + cw], g[b][:, 0:cw], r=[("g", b)])
                        it += 1


        def pha(c):
            xa = [c.sb([128, D], F32) for _ in range(2)]
            junk = c.sb([128, D], F32)
            xs = [c.sb([128, D], BF16) for _ in range(2)]
            sm = [c.sb([128, 4], F32) for _ in range(2)]
            hst = [c.sb([128, NKC, 512], BF16) for _ in range(2)]
            tp = [c.ps([128, 1024], BF16) for _ in range(2 * ((NKC + 7) // 8))]
            NB = (NKC + 7) // 8
            for i in range(TT // 128):
                b = i % 2
                gI = i // 4
                slot = i % 4
                hb = gI % 2
                c.dma(xa[b][:], x[i * 128:(i + 1) * 128, :], w=[("xa", b)])
                c.act(junk[:], xa[b][:], AF.Square, r=[("xa", b)], w=["junk"])
                c.S.add("dve", lambda e, o_=sm[b][:, 0:1], i_=junk[:]: e.reduce_sum(out=o_, in_=i_, axis=mybir.AxisListType.X),
                        ["junk"], [("sm0", b)])
                c.act(sm[b][:, 1:2], sm[b][:, 0:1], AF.Sqrt, scale=1.0 / D, bias=c_eps, r=[("sm0", b)], w=[("sm1", b)])
                c.recip(sm[b][:, 2:3], sm[b][:, 1:2], r=[("sm1", b)], w=[("sm2", b)])
                c.ts("dve", xs[b][:], xa[b][:], sm[b][:, 2:3], None, ALU.mult, r=[("xa", b), ("sm2", b)], w=[("xs", b)])
                for kc in range(NKC):
                    bank = tp[b * NB + kc // 8]
                    tv = bank
                    k8 = kc % 8
                    c.tr(tv[:, k8 * 128:(k8 + 1) * 128], xs[b][:, kc * 128:(kc + 1) * 128], identb[:],
                         r=[("xs", b)], w=[("tp", b, kc // 8)])
                for kc in range(NKC):
                    bank = tp[b * NB + kc // 8]
                    tv = bank
                    k8 = kc % 8
                    dst = hst[hb][:, kc, slot * 128:(slot + 1) * 128]
                    if True:
                        c.ts("dve", dst, tv[:, k8 * 128:(k8 + 1) * 128], g1[:, kc:kc + 1], sh1[:, kc:kc + 1],
                             ALU.mult, ALU.add, r=[("tp", b, kc // 8)], w=[("hst", hb, slot, kc)])
                if slot == 3:
                    c.dma(hT_s[gI], hst[hb][:],
                          r=[("hst", hb, s_, k_) for s_ in range(4) for k_ in range(NKC)])

        phase(pha)


        w_in_v = w_in.rearrange("(kc p) n -> p kc n", p=128)

        def load_head_w(c, Wb, W32, h, groups, tag):
            engs = ("act", "dve", "pool")
            for gi_, g in enumerate(groups):
                b = gi_ % 2
                rk = ("W32", id(W32[b]))
                c.dma(W32[b][:], w_in_v[:, :, g * W + h * 128:g * W + (h + 1) * 128], w=[rk])
                c.cp(engs[gi_ % 3], Wb[:, :, gi_, :], W32[b][:], r=[rk], w=[("Wb", gi_)])

        def proj(c, pg, Wb, gi_, hTt, hb, cols=slice(0, 512), w=()):
            for kc in range(NKC):
                c.mm(pg, Wb[:, kc, gi_, :], hTt[hb][:, kc, cols], kc == 0, kc == NKC - 1,
                     r=[("Wb", gi_), ("hTt", hb)], w=list(w))

        def phb1(c):
            Wb = c.sb([128, NKC, 3, 128], BF16)
            W32_ = c.sb([128, NKC, 128], F32)
            W32 = [W32_, W32_]
            hTt = [c.sb([128, NKC, 512], BF16) for _ in range(2)]
            Ct = [c.sb([128, 512], F32) for _ in range(2)]
            St = [c.sb([128, 512], F32) for _ in range(2)]
            KT1 = c.sb([128, TT], BF16)
            KT2 = c.sb([128, TT], BF16)
            KT3 = c.sb([128, TT], BF16)
            QT = c.sb([128, TM], BF16)
            Vb1 = c.sb([128, TT // 128, 128], BF16)
            Vb2 = c.sb([128, TT // 128, 128], BF16)
            Vb3 = c.sb([128, TT // 128, 128], BF16)
            VTs = [c.sb([128, 512], BF16) for _ in range(2)]
            VT2 = [c.sb([128, 512], BF16) for _ in range(2)]
            VT3 = c.sb([128, 2048], BF16)
            acc = c.sb([128, 2, 2048], F32)
            rl = c.sb([128, 2048], F32)
            mst = c.sb([128, 2048], BF16)
            raw = c.sb([128, 512], F32)
            sq = c.sb([128, 512], BF16)
            sd = c.sb([128, 512], F32)
            rs = c.sb([128, 512], F32)
            qn = c.sb([128, 512], F32)
            qnb = c.sb([128, 512], BF16)
            t1 = c.sb([128, 512], F32)
            t2 = c.sb([128, 512], F32)
            Pt = [c.sb([128, 256], BF16) for _ in range(2)]
            Pm = [c.sb([128, 256], BF16) for _ in range(2)]
            pg = [c.ps() for _ in range(2)]
            px = [c.ps() for _ in range(2)]
            pT = c.ps([128, 1024], BF16)
            psS1 = c.ps()
            psS = [psS1, px[1]]
            psO = [c.ps() for _ in range(2)]
            cnt = {"pg": 0, "px": 0, "u": 0}

            def qk_path(pgt, pgk, wcol, dst, cb):
                c.cp("act", raw[:], pgt[:], r=[pgk], w=["raw"])
                c.act(sq[:], pgt[:], AF.Square, r=[pgk], w=["sq"])
                pxi = cnt["px"] % 2
                cnt["px"] += 1
                c.mm(px[pxi][:], onesb[:], sq[:], r=["sq"], w=[("bank", id(px[pxi]))])
                c.act(sd[:], px[pxi][:], AF.Sqrt, scale=1.0 / 128.0, bias=c_eps, r=[("bank", id(px[pxi]))], w=["sd"])
                c.recip(rs[:], sd[:], r=["sd"], w=["rs"])
                c.stt(qn[:], raw[:], wcol, rs[:], ALU.mult, ALU.mult, r=["raw", "rs"], w=["qn"])
                c.cp("act", qnb[:], qn[:], r=["qn"], w=["qnb"])
                pxi = cnt["px"] % 2
                cnt["px"] += 1
                c.mm(px[pxi][:], rotT[:], qnb[:], r=["qnb"], w=[("bank", id(px[pxi]))])
                c.tt("dve", t1[:], qn[:], Ct[cb][:], ALU.mult, r=["qn", ("Ct", cb)], w=["t1"])
                c.tt("dve", t2[:], px[pxi][:], St[cb][:], ALU.mult, r=[("bank", id(px[pxi])), ("St", cb)], w=["t2"])
                c.tt("dve", dst, t1[:], t2[:], ALU.add, r=["t1", "t2"], w=["KQ"])

            for h in range(NH):
                load_head_w(c, Wb, W32, h, (0, 1, 2), "a")
                for j in range(NT):
                    hb = j % 2
                    main = j >= NTH
                    c.dma(hTt[hb][:], hT_s[j], w=[("hTt", hb)])
                    c.dma(Ct[hb][:], cs_s[0, :, j * 512:(j + 1) * 512], w=[("Ct", hb)])
                    c.dma(St[hb][:], cs_s[1, :, j * 512:(j + 1) * 512], w=[("St", hb)])
                    tk = slice(j * 512, (j + 1) * 512)
                    pi_ = cnt["pg"] % 2
                    cnt["pg"] += 1
                    proj(c, pg[pi_][:], Wb, 1, hTt, hb, w=[("pg", pi_)])
                    qk_path(pg[pi_], ("pg", pi_), knw[:, 0:1], KT1[:, tk], hb)
                    c.cp("pool", KT2[:, tk].rearrange("p (r m) -> p r m", r=4),
                         KT1[:, tk].rearrange("p (m r) -> p r m", r=4), r=["KQ"], w=["K2"])
                    sp3 = (j // 4) * 2048
                    jj = j % 4
                    c.cp("pool", KT3[:, sp3:sp3 + 2048].rearrange("p (r m) -> p r m", r=16)[:, :, jj * 32:(jj + 1) * 32],
                         KT1[:, tk].rearrange("p (m r) -> p r m", r=16), r=["KQ"], w=["K3"])
                    pi_ = cnt["pg"] % 2
                    cnt["pg"] += 1
                    proj(c, pg[pi_][:], Wb, 2, hTt, hb, w=[("pg", pi_)])
                    c.cp("act", VTs[hb][:], pg[pi_][:], r=[("pg", pi_)], w=[("VTs", hb)])
                    c.cp("pool", VT2[hb][:].rearrange("p (r m) -> p r m", r=4),
                         VTs[hb][:].rearrange("p (m r) -> p r m", r=4), r=[("VTs", hb)], w=[("VT2", hb)])
                    c.cp("pool", VT3[:].rearrange("p (r m) -> p r m", r=16)[:, :, jj * 32:(jj + 1) * 32],
                         VTs[hb][:].rearrange("p (m r) -> p r m", r=16), r=[("VTs", hb)], w=["VT3"])
                    for src, srck, dstV in ((VTs[hb], ("VTs", hb), Vb1), (VT2[hb], ("VT2", hb), Vb2)):
                        pv = pT
                        for q4 in range(4):
                            c.tr(pv[:, q4 * 128:(q4 + 1) * 128], src[:, q4 * 128:(q4 + 1) * 128], identb[:],
                                 r=[srck], w=["pT"])
                        c.cp("dve", dstV[:, j * 4:(j + 1) * 4, :], pv[:, 0:512].rearrange("p (a b) -> p a b", a=4),
                             r=["pT"], w=["Vb"])
                    if jj == 3:
                        for q16 in range(4):
                            pv = pT
                            for q4 in range(4):
                                r_ = q16 * 4 + q4
                                c.tr(pv[:, q4 * 128:(q4 + 1) * 128], VT3[:, r_ * 128:(r_ + 1) * 128], identb[:],
                                     r=["VT3"], w=["pT"])
                            t0 = (j // 4) * 16 + q16 * 4
                            c.cp("dve", Vb3[:, t0:t0 + 4, :], pv[:, 0:512].rearrange("p (a b) -> p a b", a=4),
                                 r=["pT"], w=["Vb"])
                    if main:
                        pi_ = cnt["pg"] % 2
                        cnt["pg"] += 1
                        proj(c, pg[pi_][:], Wb, 0, hTt, hb, w=[("pg", pi_)])
                        jm = j - NTH
                        qk_path(pg[pi_], ("pg", pi_), qnw[:, 0:1], QT[:, jm * 512:(jm + 1) * 512], hb)
                for n in range(NSPAN):
                    ng = n + 1
                    q0 = n * 2048
                    units = []
                    for br in (1, 2, 3):
                        for u in range(16):
                            if br == 1:
                                tcur = ng * 16 + u
                                kcur = KT1[:, tcur * 128:(tcur + 1) * 128]
                                kprev = KT1[:, (tcur - 1) * 128:tcur * 128]
                                vcur, vprev = Vb1[:, tcur, :], Vb1[:, tcur - 1, :]
                                qv = QT[:, q0 + u * 128:q0 + (u + 1) * 128]
                                av = acc[:, :, u * 128:(u + 1) * 128]
                                halo_prev = (n == 0 and u == 0)
                            elif br == 2:
                                s_, r_ = u // 4, u % 4
                                sg = ng * 4 + s_
                                tcur, tprev = sg * 4 + r_, (sg - 1) * 4 + r_
                                kcur = KT2[:, tcur * 128:(tcur + 1) * 128]
                                kprev = KT2[:, tprev * 128:(tprev + 1) * 128]
                                vcur, vprev = Vb2[:, tcur, :], Vb2[:, tprev, :]
                                b0 = q0 + s_ * 512 + r_
                                qv = QT[:, b0:b0 + 509:4]
                                a0 = s_ * 512 + r_
                                av = acc[:, :, a0:a0 + 509:4]
                                halo_prev = (n == 0 and s_ == 0)
                            else:
                                r_ = u
                                tcur, tprev = ng * 16 + r_, (ng - 1) * 16 + r_
                                kcur = KT3[:, tcur * 128:(tcur + 1) * 128]
                                kprev = KT3[:, tprev * 128:(tprev + 1) * 128]
                                vcur, vprev = Vb3[:, tcur, :], Vb3[:, tprev, :]
                                qv = QT[:, q0 + r_:q0 + r_ + 2033:16]
                                av = acc[:, :, r_:r_ + 2033:16]
                                halo_prev = (n == 0)
                            units.append((br, kprev, kcur, vprev, vcur, qv, av, halo_prev))

                    def stage1(un, ub):
                        br, kprev, kcur, vprev, vcur, qv, av, halo_prev = un
                        bk = ("bank", id(psS[ub]))
                        c.mm(psS[ub][:, 0:128], kprev, qv, r=["K2", "K3", "KQ"], w=[bk])
                        c.mm(psS[ub][:, 128:256], kcur, qv, r=["K2", "K3", "KQ"], w=[bk])
                        c.act(Pt[ub][:], psS[ub][:, 0:256], AF.Exp, scale=ISQ, bias=c_mshift, r=[bk], w=[("Pt", ub)])
                        mk = maskAh if halo_prev else maskA
                        c.tt("pool", Pm[ub][:], Pt[ub][:], mk[:], ALU.mult, r=[("Pt", ub)], w=[("Pm", ub)])

                    def stage2(un, ub):
                        br, kprev, kcur, vprev, vcur, qv, av, halo_prev = un
                        c.mm(psO[ub][:, 0:128], vprev, Pm[ub][:, 0:128], True, False, r=["Vb", ("Pm", ub)], w=[("psO", ub)])
                        c.mm(psO[ub][:, 0:128], vcur, Pm[ub][:, 128:256], False, True, r=["Vb", ("Pm", ub)], w=[("psO", ub)])
                        c.mm(psO[ub][:, 128:256], onesb[:], Pm[ub][:, 0:128], True, False, r=[("Pm", ub)], w=[("psO", ub)])
                        c.mm(psO[ub][:, 128:256], onesb[:], Pm[ub][:, 128:256], False, True, r=[("Pm", ub)], w=[("psO", ub)])
                        pso = psO[ub][:, 0:256].rearrange("p (a b) -> p a b", a=2)
                        if br == 1:
                            c.cp("dve", av, pso, r=[("psO", ub)], w=["acc"])
                        else:
                            c.tt("dve", av, pso, av, ALU.add, r=[("psO", ub), "acc"], w=["acc"])

                    stage1(units[0], 0)
                    for ui in range(len(units)):
                        if ui + 1 < len(units):
                            stage1(units[ui + 1], (ui + 1) % 2)
                        stage2(units[ui], ui % 2)
                    c.recip(rl[:], acc[:, 1, :], r=["acc"], w=["rl"])
                    c.tt("dve", mst[:], acc[:, 0, :], rl[:], ALU.mult, r=["acc", "rl"], w=["mst"])
                    c.dma(mixT_s[:, h, n * 2048:(n + 1) * 2048], mst[:], r=["mst"])

        phase(phb1)

        def phb2(c):
            NCH = TT // 64
            NCHH = TH // 64
            Wb = c.sb([128, NKC, 4, 128], BF16)
            W32 = [c.sb([128, NKC, 128], F32) for _ in range(2)]
            hTt = [c.sb([128, NKC, 512], BF16) for _ in range(2)]
            KeT = c.sb([128, TT], BF16)
            QeT = c.sb([128, TM], BF16)
            gT = c.sb([128, TM], BF16)
            oT = c.sb([128, TM], F32)
            Kem = c.sb([128, TT // 128, 128], BF16)
            Vm = c.sb([128, TT // 128, 128], BF16)
            eref = c.sb([128, NCH + 1], F32)
            elast = c.sb([128, NCH], F32)
            e2 = c.sb([128, NCH], F32)
            dl2 = c.sb([128, 8], F32)
            sg = c.sb([128, 512], F32)
            sgn = c.sb([128, 512], F32)
            ff = c.sb([128, 512], F32)
            lf = c.sb([128, 512], F32)
            bb = c.sb([128, 512], F32)
            dd = c.sb([128, 512], F32)
            eq = c.sb([128, 512], F32)
            ek = c.sb([128, 512], F32)
            qs = c.sb([128, 512], F32)
            Sst = c.sb([128, 128], F32)
            Sb = c.sb([128, 128], BF16)
            aT = [c.sb([128, 64], BF16) for _ in range(2)]
            sq = c.sb([128, 512], BF16)
            sd = c.sb([128, 512], F32)
            rs = c.sb([128, 512], F32)
            tt_ = c.sb([128, 512], F32)
            hst = [c.sb([128, 512], BF16) for _ in range(2)]
            pg = [c.ps() for _ in range(2)]
            px0 = c.ps()
            px = [px0, px0]
            pT = c.ps([128, 1024], BF16)
            psA = [c.ps() for _ in range(2)]
            psU = [c.ps() for _ in range(2)]
            cnt = {"pg": 0, "px": 0}
            CWc = 2048
            cf = [c.sb([128, CWc], F32) for _ in range(2)]
            cg = [c.sb([128, CWc], BF16) for _ in range(2)]
            jobs = []
            for src, dst, R, Cn in ((w_out, wo_s, D, D), (w_ff1, wf1_s, D, DFF), (w_ff2, wf2_s, DFF, D)):
                for r0 in range(0, R, 128):
                    for c0 in range(0, Cn, CWc):
                        jobs.append((src, dst, r0, c0, min(CWc, Cn - c0)))
            jstate = {"i": 0}
            per_tile = -(-len(jobs) // (NH * NT))

            def emit_casts(k):
                for _ in range(k):
                    i = jstate["i"]
                    if i >= len(jobs):
                        return
                    jstate["i"] += 1
                    src, dst, r0, c0, cw = jobs[i]
                    b = i % 2
                    c.dma(cf[b][:, 0:cw], src[r0:r0 + 128, c0:c0 + cw], w=[("cf", b)])
                    c.cp("pool", cg[b][:, 0:cw], cf[b][:, 0:cw], r=[("cf", b)], w=[("cg", b)])
                    c.dma(dst[r0:r0 + 128, c0:c0 + cw], cg[b][:, 0:cw], r=[("cg", b)])

            def nextpg():
                i = cnt["pg"] % 2
                cnt["pg"] += 1
                return i

            for h in range(NH):
                load_head_w(c, Wb, W32, h, (3, 4, 5, 6), "g")
                c.S.add("pool", lambda e: e.memset(eref[:, NCH:NCH + 1], 1.0), (), ["eref"])
                for j in range(NT):
                    hb = j % 2
                    main = j >= NTH
                    jm = j - NTH
                    c.dma(hTt[hb][:], hT_s[j], w=[("hTt", hb)])
                    tk = slice(j * 512, (j + 1) * 512)
                    pi_ = nextpg()
                    proj(c, pg[pi_][:], Wb, 1, hTt, hb, w=[("pg", pi_)])
                    c.act(sg[:], pg[pi_][:], AF.Sigmoid, r=[("pg", pi_)], w=["sg"])
                    c.act(sgn[:], pg[pi_][:], AF.Sigmoid, scale=-1.0, r=[("pg", pi_)], w=["sgn"])
                    c.ts("dve", ff[:], sg[:], omlb[:, h:h + 1], lb[:, h:h + 1], ALU.mult, ALU.add, r=["sg"], w=["ff"])
                    c.act(lf[:], ff[:], AF.Ln, r=["ff"], w=["lf"])
                    c.S.add("dve", lambda e: e.tensor_tensor_scan(out=bb[:], data0=resetm[:], data1=lf[:], initial=0.0,
                                                                  op0=ALU.mult, op1=ALU.add), ["lf"], ["bb"])
                    bv = bb[:].rearrange("p (c j) -> p c j", j=64)
                    c.tt("dve", dd[:].rearrange("p (c j) -> p c j", j=64), bv, bv[:, :, 31:32].to_broadcast([128, 8, 64]),
                         ALU.subtract, r=["bb"], w=["dd"])
                    c.act(ek[:], dd[:], AF.Exp, scale=-1.0, r=["dd"], w=["ek"])
                    om = omlb if main else omlbv
                    c.stt(KeT[:, tk], sgn[:], om[:, h:h + 1], ek[:], ALU.mult, ALU.mult, r=["sgn", "ek"], w=["KeT"])
                    c.act(eref[:, j * 8:(j + 1) * 8], bv[:, :, 31], AF.Exp, r=["bb"], w=["eref"])
                    c.act(elast[:, j * 8:(j + 1) * 8], bv[:, :, 63], AF.Exp, r=["bb"], w=["elast"])
                    c.tt("dve", dl2[:], bv[:, :, 63], bv[:, :, 31], ALU.subtract, r=["bb"], w=["dl2"])
                    c.act(e2[:, j * 8:(j + 1) * 8], dl2[:], AF.Exp, r=["dl2"], w=["e2"])
                    pv = pT
                    for q4 in range(4):
                        c.tr(pv[:, q4 * 128:(q4 + 1) * 128], KeT[:, j * 512 + q4 * 128:j * 512 + (q4 + 1) * 128], identb[:],
                             r=["KeT"], w=["pT"])
                    c.cp("act", Kem[:, j * 4:(j + 1) * 4, :], pv[:, 0:512].rearrange("p (a b) -> p a b", a=4),
                         r=["pT"], w=["Kem"])
                    pi_ = nextpg()
                    for q4 in range(4):
                        for kc in range(NKC):
                            c.mm(pg[pi_][:, q4 * 128:(q4 + 1) * 128], hTt[hb][:, kc, q4 * 128:(q4 + 1) * 128],
                                 Wb[:, kc, 2, :], kc == 0, kc == NKC - 1, r=[("Wb", 2), ("hTt", hb)], w=[("pg", pi_)])
                    c.cp("act", Vm[:, j * 4:(j + 1) * 4, :], pg[pi_][:].rearrange("p (a b) -> p a b", a=4),
                         r=[("pg", pi_)], w=["Vm"])
                    if main:
                        c.act(eq[:], dd[:], AF.Exp, r=["dd"], w=["eq"])
                        pi_ = nextpg()
                        proj(c, pg[pi_][:], Wb, 0, hTt, hb, w=[("pg", pi_)])
                        c.act(qs[:], pg[pi_][:], AF.Silu, r=[("pg", pi_)], w=["qs"])
                        c.tt("dve", QeT[:, jm * 512:(jm + 1) * 512], qs[:], eq[:], ALU.mult, r=["qs", "eq"], w=["QeT"])
                        pi_ = nextpg()
                        proj(c, pg[pi_][:], Wb, 3, hTt, hb, w=[("pg", pi_)])
                        c.act(gT[:, jm * 512:(jm + 1) * 512], pg[pi_][:], AF.Silu, r=[("pg", pi_)], w=["gT"])
                    emit_casts(per_tile)
                c.S.add("pool", lambda e: e.memset(Sst[:], 0.0), (), ["S"])
                c.S.add("pool", lambda e: e.memset(Sb[:], 0.0), (), ["Sb"])
                def emit_U(ch):
                    t128, po, ub = ch // 2, 64 * (ch % 2), ch % 2
                    c.mm(psU[ub][:, 0:128], Kem[po:po + 64, t128, :], Vm[po:po + 64, t128, :],
                         r=["Kem", "Vm"], w=[("psU", ub)])

                def emit_a(ch):
                    t128, po, ub = ch // 2, 64 * (ch % 2), ch % 2
                    cm = ch - NCHH
                    c.mm(psA[ub][po:po + 64, 0:64], KeT[:, ch * 64:(ch + 1) * 64], QeT[:, cm * 64:(cm + 1) * 64],
                         r=["KeT", "QeT"], w=[("psA", ub)])
                    c.tt("dve", aT[ub][po:po + 64, :], psA[ub][po:po + 64, 0:64], maskH[po:po + 64, :], ALU.mult,
                         r=[("psA", ub)], w=[("aT", ub)])

                def emit_o(ch):
                    t128, po, ub = ch // 2, 64 * (ch % 2), ch % 2
                    cm = ch - NCHH
                    qcols = QeT[:, cm * 64:(cm + 1) * 64]
                    c.mm(psA[ub][:, 128:192], Vm[po:po + 64, t128, :], aT[ub][po:po + 64, :], True, False,
                         r=["Vm", ("aT", ub)], w=[("psA", ub)])
                    c.mm(psA[ub][:, 128:192], Sb[:], qcols, False, True, r=["Sb", "QeT"], w=[("psA", ub)])
                    c.cp("act", oT[:, cm * 64:(cm + 1) * 64], psA[ub][:, 128:192], r=[("psA", ub)], w=["oT"])

                emit_U(0)
                for ch in range(NCH):
                    ub = ch % 2
                    if ch + 1 < NCH:
                        emit_U(ch + 1)
                        if ch + 1 >= NCHH:
                            emit_a(ch + 1)
                    if ch >= NCHH:
                        emit_o(ch)
                    c.ts("dve", Sst[:], Sst[:], elast[:, ch:ch + 1], None, ALU.mult, r=["S", "elast"], w=["S"])
                    c.stt(Sst[:], psU[ub][:, 0:128], e2[:, ch:ch + 1], Sst[:], ALU.mult, ALU.add,
                          r=[("psU", ub), "S", "e2"], w=["S"])
                    if ch + 1 >= NCHH and ch + 1 < NCH:
                        c.ts("dve", Sb[:], Sst[:], eref[:, ch + 1:ch + 2], None, ALU.mult, r=["S", "eref"], w=["Sb"])
                for jm in range(NTM):
                    sb_ = jm % 2
                    cs = slice(jm * 512, (jm + 1) * 512)
                    c.act(sq[:], oT[:, cs], AF.Square, r=["oT"], w=["sq"])
                    pxi = cnt["px"] % 2
                    cnt["px"] += 1
                    c.mm(px[pxi][:], onesb[:], sq[:], r=["sq"], w=[("bank", id(px[pxi]))])
                    c.act(sd[:], px[pxi][:], AF.Sqrt, scale=1.0 / 128.0, bias=c_eps128, r=[("bank", id(px[pxi]))], w=["sd"])
                    c.recip(rs[:], sd[:], r=["sd"], w=["rs"])
                    c.stt(tt_[:], oT[:, cs], hgnw[:, 0:1], rs[:], ALU.mult, ALU.mult, r=["oT", "rs"], w=["tt"])
                    c.tt("dve", hst[sb_][:], tt_[:], gT[:, cs], ALU.mult, r=["tt", "gT"], w=[("hst", sb_)])
                    c.dma(mixT_s[:, NH + h, cs], hst[sb_][:], r=[("hst", sb_)])
            emit_casts(len(jobs))

        phase(phb2)

        def phc(c):
            NOG = NKC // 4
            HALF = NFF // 2
            PF = min(16, HALF)
            mixT = c.sb([128, NKC, 512], BF16)
            xT = c.sb([128, NKC, 512], F32)
            xtok = [c.sb([128, D], F32) for _ in range(2)]
            h2T = c.sb([128, NKC, 512], BF16)
            uT = c.sb([128, HALF, 512], BF16)
            pan = [c.sb([128, 16, 512], BF16) for _ in range(3)]
            sqa = c.sb([128, NKC, 512], BF16)
            sd = c.sb([128, 512], F32)
            rr = c.sb([128, 512], F32)
            tmp = [c.sb([128, 512], F32) for _ in range(2)]
            rl = [c.sb([128, 512], BF16) for _ in range(2)]
            pa = [c.ps() for _ in range(4)]
            pb = [c.ps() for _ in range(4)]
            cnt = {"pan": 0, "pa": 0, "t": 0}
            wo_v = wo_s.rearrange("(kc p) n -> p kc n", p=128)
            wf1_v = wf1_s.rearrange("(kc p) n -> p kc n", p=128)
            wf2_v = wf2_s.rearrange("(kc p) n -> p kc n", p=128)

            def nextpan():
                i = cnt["pan"] % 3
                cnt["pan"] += 1
                return i

            def nextpa():
                i = cnt["pa"] % 4
                cnt["pa"] += 1
                return i

            for i in range(NTM):
                tok0 = TH + i * 512
                c.dma(mixT[:], mixT_s[:, :, i * 512:(i + 1) * 512], w=["mixT"])
                for blk in range(4):
                    xb = blk % 2
                    c.dma(xtok[xb][:], x[tok0 + blk * 128:tok0 + (blk + 1) * 128, :], w=[("xtok", xb)])
                    for k4 in range(NOG):
                        pi_ = nextpa()
                        for q4 in range(4):
                            kc = k4 * 4 + q4
                            c.tr(pa[pi_][:, q4 * 128:(q4 + 1) * 128], xtok[xb][:, kc * 128:(kc + 1) * 128], identf[:],
                                 r=[("xtok", xb)], w=[("pa", pi_)])
                        c.cp("act" if k4 % 2 == 0 else "dve", xT[:, k4 * 4:(k4 + 1) * 4, blk * 128:(blk + 1) * 128],
                             pa[pi_][:].rearrange("p (a b) -> p a b", a=4), r=[("pa", pi_)], w=[("xT", k4 * 4 + q_) for q_ in range(4)])
                c.act(sqa[:, 0:NH, :], mixT[:, 0:NH, :], AF.Square, r=["mixT"], w=["sqa"])
                pi_ = nextpa()
                for hh in range(NH):
                    c.mm(pa[pi_][:], onesb[:], sqa[:, hh, :], hh == 0, hh == NH - 1, r=["sqa"], w=[("pa", pi_)])
                c.act(sd[:], pa[pi_][:], AF.Sqrt, scale=1.0 / W, bias=c_eps, r=[("pa", pi_)], w=["sd"])
                c.recip(rr[:], sd[:], r=["sd"], w=["rr"])
                for hh in range(NH):
                    c.stt(mixT[:, hh, :], mixT[:, hh, :], aonw[:, hh:hh + 1], rr[:], ALU.mult, ALU.mult,
                          r=["mixT", "rr"], w=["mixT"])
                for og in range(NOG):
                    pn = nextpan()
                    c.dma(pan[pn][:, 0:NKC, :], wo_v[:, :, og * 512:(og + 1) * 512], w=[("pan", pn)])
                    for oc4 in range(4):
                        oc = og * 4 + oc4
                        pi_ = nextpa()
                        for kc in range(NKC):
                            c.mm(pa[pi_][:], pan[pn][:, kc, oc4 * 128:(oc4 + 1) * 128], mixT[:, kc, :], kc == 0, kc == NKC - 1,
                                 r=[("pan", pn), "mixT"], w=[("pa", pi_)])
                        c.stt(xT[:, oc, :], pa[pi_][:], gate1[:, oc:oc + 1], xT[:, oc, :], ALU.mult, ALU.add,
                              r=[("pa", pi_), ("xT", oc)], w=[("xT", oc)])
                c.act(sqa[:], xT[:], AF.Square, r=[("xT", k_) for k_ in range(NKC)], w=["sqa"])
                pi_ = nextpa()
                for kc in range(NKC):
                    c.mm(pa[pi_][:], onesb[:], sqa[:, kc, :], kc == 0, kc == NKC - 1, r=["sqa"], w=[("pa", pi_)])
                c.act(sd[:], pa[pi_][:], AF.Sqrt, scale=1.0 / D, bias=c_eps, r=[("pa", pi_)], w=["sd"])
                c.recip(rr[:], sd[:], r=["sd"], w=["rr"])
                for kc in range(NKC):
                    tb = cnt["t"] % 2
                    cnt["t"] += 1
                    c.stt(tmp[tb][:], xT[:, kc, :], g2[:, kc:kc + 1], rr[:], ALU.mult, ALU.mult,
                          r=[("xT", kc), "rr"], w=[("tmp", tb)])
                    c.ts("dve", h2T[:, kc, :], tmp[tb][:], sh2[:, kc:kc + 1], None, ALU.add, r=[("tmp", tb)], w=["h2T"])
                for hf in range(2):
                    for fg in range(HALF // 4):
                        pn = nextpan()
                        col0 = (hf * HALF + fg * 4) * 128
                        c.dma(pan[pn][:, 0:NKC, :], wf1_v[:, :, col0:col0 + 512], w=[("pan", pn)])
                        for f4 in range(4):
                            fl = fg * 4 + f4
                            pi_ = nextpa()
                            for kc in range(NKC):
                                c.mm(pa[pi_][:], pan[pn][:, kc, f4 * 128:(f4 + 1) * 128], h2T[:, kc, :], kc == 0, kc == NKC - 1,
                                     r=[("pan", pn), "h2T"], w=[("pa", pi_)])
                            tb = cnt["t"] % 2
                            cnt["t"] += 1
                            c.act(rl[tb][:], pa[pi_][:], AF.Relu, r=[("pa", pi_)], w=[("rl", tb)])
                            c.tt("pool", uT[:, fl, :], rl[tb][:], rl[tb][:], ALU.mult, r=[("rl", tb)], w=[("uT", fl)])
                    for og in range(NOG):
                        for q in range(HALF // PF):
                            pn = nextpan()
                            k0 = hf * HALF + q * PF
                            c.dma(pan[pn][:, 0:PF, :], wf2_v[:, k0:k0 + PF, og * 512:(og + 1) * 512], w=[("pan", pn)])
                            for oc4 in range(4):
                                for f in range(PF):
                                    fl = q * PF + f
                                    c.mm(pb[oc4][:], pan[pn][:, f, oc4 * 128:(oc4 + 1) * 128], uT[:, fl, :],
                                         q == 0 and f == 0, q == HALF // PF - 1 and f == PF - 1,
                                         r=[("pan", pn), ("uT", fl)], w=[("pb", oc4)])
                        for oc4 in range(4):
                            oc = og * 4 + oc4
                            c.stt(xT[:, oc, :], pb[oc4][:], gate2[:, oc:oc + 1], xT[:, oc, :], ALU.mult, ALU.add,
                                  r=[("pb", oc4), ("xT", oc)], w=[("xT", oc)])
                for blk in range(4):
                    xb = blk % 2
                    for k4 in range(NOG):
                        pi_ = nextpa()
                        for q4 in range(4):
                            kc = k4 * 4 + q4
                            c.tr(pa[pi_][:, q4 * 128:(q4 + 1) * 128], xT[:, kc, blk * 128:(blk + 1) * 128], identf[:],
                                 r=[("xT", kc)], w=[("pa", pi_)])
                        c.cp("act" if k4 % 2 == 0 else "dve", xtok[xb][:, k4 * 512:(k4 + 1) * 512], pa[pi_][:],
                             r=[("pa", pi_)], w=[("xtok", xb)])
                    c.dma(out[i * 512 + blk * 128:i * 512 + (blk + 1) * 128, :], xtok[xb][:], r=[("xtok", xb)])

        phase(phc)
    return nc


def make_consts():
    bf = ml_dtypes.bfloat16
    p = np.arange(128)
    i = np.arange(128)
    mprev = (p[:, None] >= i[None, :]).astype(np.float32)
    mcur = (p[:, None] <= i[None, :]).astype(np.float32)
    maskA = np.concatenate([mprev, mcur], axis=1).astype(bf)
    t = np.arange(64)
    maskH = ((p[:, None] % 64) <= t[None, :]).astype(np.float32).astype(bf)
    rotT = np.zeros((128, 128), np.float32)
    for m in range(16):
        rotT[m + 16, m] = -1.0
    for m in range(16, 32):
        rotT[m - 16, m] = 1.0
    resetm = np.ones((128, 512), np.float32)
    resetm[:, ::64] = 0.0
    invf = np.zeros((128, 1), np.float32)
    half = 16
    fr = (np.float32(ROPE_THETA) ** (-(np.arange(half, dtype=np.float32) * np.float32(2.0)) / np.float32(32))).astype(np.float32)
    invf[0:16, 0] = fr
    invf[16:32, 0] = fr
    cst = np.zeros((128, 4), np.float32)
    cst[:, 0] = EPS
    cst[:, 1] = EPS * 128.0
    cst[:, 2] = -MSHIFT
    return {
        "identb": np.eye(128, dtype=np.float32).astype(bf), "identf": np.eye(128, dtype=np.float32),
        "onesb": np.ones((128, 128), np.float32).astype(bf), "rotT": rotT.astype(bf), "maskA": maskA, "maskH": maskH,
        "resetm": resetm, "invf": invf, "cst": cst,
    }


def pk(v, nkc):
    return np.ascontiguousarray(np.asarray(v, np.float32).reshape(nkc, 128).T)


def make_in_maps(inputs, D, NH, DFF, TM, TH, n_seg):
    NKC = D // 128
    x = np.asarray(inputs["x"], np.float32)
    B, S, _ = x.shape
    posi = np.asarray(inputs["positions"], np.int32)
    consts = make_consts()
    shared = {
        "n1w": pk(inputs["norm1_w"][0], NKC), "n2w": pk(inputs["norm2_w"][0], NKC),
        "bada": np.ascontiguousarray(np.asarray(inputs["b_ada"][0], np.float32).reshape(6 * NKC, 128).T),
        "w_ada": np.ascontiguousarray(np.asarray(inputs["w_ada"][0], np.float32)),
        "w_in": np.ascontiguousarray(np.asarray(inputs["w_in"][0], np.float32)),
        "qnw": pk(inputs["q_norm_w"][0], 1), "knw": pk(inputs["k_norm_w"][0], 1),
        "aonw": pk(inputs["attn_out_norm_w"][0], NH),
        "lbl": np.ascontiguousarray(np.asarray(inputs["hg_lb_logits"], np.float32).reshape(2 * NH, 128).T),
        "hgnw": pk(inputs["hg_norm_w"][0], 1),
        "w_out": np.ascontiguousarray(np.asarray(inputs["w_out"][0], np.float32)),
        "w_ff1": np.ascontiguousarray(np.asarray(inputs["w_ff1"][0], np.float32)),
        "w_ff2": np.ascontiguousarray(np.asarray(inputs["w_ff2"][0], np.float32)),
    }
    shared.update(consts)
    maps = []
    for b in range(B):
        for j in range(n_seg):
            s0 = j * TM
            xin = np.zeros((TH + TM, D), np.float32)
            pin = np.zeros((1, TH + TM), np.int32)
            if j > 0:
                xin[:] = x[b, s0 - TH:s0 + TM]
                pin[0, :] = posi[b, s0 - TH:s0 + TM]
            else:
                xin[TH:] = x[b, 0:TM]
                pin[0, TH:] = posi[b, 0:TM]
            m = dict(shared)
            m["x"] = xin
            m["pos"] = pin
            m["cT"] = pk(inputs["c"][b], NKC)
            m["valid"] = np.full((128, 1), 1.0 if j > 0 else 0.0, np.float32)
            maps.append(m)
    return maps


_NC_CACHE = {}


def kernel(**inputs):
    D, NH, DFF, TM, TH = 2048, 8, 8192, 4096, 2048
    x = np.asarray(inputs["x"])
    B, S, _ = x.shape
    n_seg = S // TM
    key = (D, NH, DFF, TM, TH)
    if key not in _NC_CACHE:
        _NC_CACHE[key] = build_program(*key)
    nc = _NC_CACHE[key]
    in_maps = make_in_maps(inputs, D, NH, DFF, TM, TH, n_seg)
    res = run_bass_kernel_spmd(nc, in_maps, core_ids=list(range(len(in_maps))))
    outp = np.empty((B, S, D), np.float32)
    k = 0
    for b in range(B):
        for j in range(n_seg):
            outp[b, j * TM:(j + 1) * TM] = np.asarray(res.results[k]["out"], np.float32)
            k += 1
    return outp
```
